# Optimizing a Trainium2 kernel written in Bass

```python
import math
import jax
import jax.numpy as jnp
from jax import lax
import numpy as np

D_MODEL = 1024
BATCH = 16
SEQ = 4096
DEPTH = 2
DEC_BATCH = 8
DEC_SEQ = 64
PAST_LEN = 1024

CHUNK = 64
N_EVEN = (DEPTH + 1) // 2
N_ODD = DEPTH // 2
S5_WIDTH = D_MODEL // 2
S5_GROUP = 16
S5_GROUPS = S5_WIDTH // S5_GROUP
S5_STATE = 64
GLA_WIDTH = D_MODEL // 2
GLA_HEADS = 4
GLA_DK = GLA_WIDTH // 2 // GLA_HEADS
GLA_DV = GLA_WIDTH // GLA_HEADS
GLA_GATE_RANK = 16
GLA_GATE_NORM = 16.0
HEAD_DIM = 64
SWA_WIDTH = D_MODEL // 2
SWA_HEADS = SWA_WIDTH // HEAD_DIM
SWA_KV_HEADS = 2
SWA_GROUP = SWA_HEADS // SWA_KV_HEADS
WINDOW = 128
SSD_INNER = D_MODEL // 2
SSD_HEAD_DIM = 64
SSD_HEADS = SSD_INNER // SSD_HEAD_DIM
SSD_STATE = 128
SSD_GROUPS = 2
SSD_CONV = 4
SSD_CONV_DIM = SSD_INNER + 2 * SSD_GROUPS * SSD_STATE
D_FF = 2816
FFN_CONV = 3
T5_BUCKETS = 32
T5_MAX_DIST = 128
RMS_EPS = 1e-6
NEG_INF = -1e30
EVEN_IN = S5_WIDTH + 2 * GLA_HEADS * GLA_DK + 2 * GLA_WIDTH + GLA_GATE_RANK
ODD_IN = SWA_WIDTH + 2 * SWA_KV_HEADS * HEAD_DIM + SSD_INNER + SSD_CONV_DIM + SSD_HEADS

kernel_name = 'hybrid_streaming_encoder_step'

STATE_NAMES = ('s5_re', 's5_im', 'gla', 'swa_k', 'swa_v', 'ssd', 'ssd_conv', 'ffn_conv')


def _rms(x, g):
    xf = x.astype(jnp.float32)
    y = xf * lax.rsqrt(jnp.mean(xf * xf, axis=-1, keepdims=True) + RMS_EPS)
    return (y * g.astype(jnp.float32)).astype(x.dtype)


def _chunk_len(t):
    return CHUNK if t % CHUNK == 0 else t


def _causal_dwconv(x, prefix, w, b):
    width = w.shape[0]
    t = x.shape[1]
    xp = jnp.concatenate([prefix.astype(x.dtype), x], axis=1)
    out = b
    for j in range(width):
        out = out + xp[:, j:j + t] * w[j]
    return out, xp[:, t:]


def _t5_bucket(rel):
    nb = T5_BUCKETS // 2
    max_exact = nb // 2
    ret = (rel > 0).astype(jnp.int32) * nb
    n = jnp.abs(rel)
    nf = jnp.maximum(n, 1).astype(jnp.float32)
    large = max_exact + (jnp.log(nf / max_exact) / math.log(T5_MAX_DIST / max_exact)
                         * (nb - max_exact)).astype(jnp.int32)
    large = jnp.minimum(large, nb - 1)
    return ret + jnp.where(n < max_exact, n, large)


def _rel_bias(table, n_q, n_k, k_offset):
    rel = (jnp.arange(n_k)[None, :] - k_offset) - jnp.arange(n_q)[:, None]
    bias = jnp.transpose(table.astype(jnp.float32)[_t5_bucket(rel)], (2, 0, 1))
    return bias.reshape(SWA_KV_HEADS, SWA_GROUP, n_q, n_k)


def _swa_attend(q, k, v, bias, valid, sink):
    s = jnp.einsum('bnqkgd,bnskd->bnkgqs', q, k).astype(jnp.float32) * HEAD_DIM ** -0.5 + bias
    s = jnp.where(valid[None, :, None, None, None, :], s, NEG_INF)
    sink_col = jnp.broadcast_to(sink.astype(jnp.float32)[None, None, :, :, None, None], s.shape[:-1] + (1,))
    p = jax.nn.softmax(jnp.concatenate([s, sink_col], axis=-1), axis=-1)[..., :-1]
    return jnp.einsum('bnkgqs,bnskd->bnqkgd', p.astype(v.dtype), v)


def _swa_prompt(q, k, v, table, sink):
    b, t = q.shape[:2]
    nc = t // CHUNK
    nb = WINDOW // CHUNK

    def band(a):
        ap = jnp.concatenate([jnp.zeros((b, WINDOW) + a.shape[2:], a.dtype), a], axis=1)
        ap = ap.reshape((b, nc + nb, CHUNK) + a.shape[2:])
        return jnp.concatenate([ap[:, j:j + nc] for j in range(nb + 1)], axis=2)

    key_pos = jnp.arange(nc)[:, None] * CHUNK - WINDOW + jnp.arange(WINDOW + CHUNK)[None, :]
    bias = _rel_bias(table, CHUNK, WINDOW + CHUNK, WINDOW)
    qc = q.reshape((b, nc, CHUNK) + q.shape[2:])
    o = _swa_attend(qc, band(k), band(v), bias, key_pos >= 0, sink)
    return o.reshape(b, t, SWA_WIDTH), k[:, -WINDOW:], v[:, -WINDOW:]


def _swa_sample(q, k, v, cache_k, cache_v, table, sink):
    b, t = q.shape[:2]
    kk = jnp.concatenate([cache_k.astype(k.dtype), k], axis=1)[:, None]
    vv = jnp.concatenate([cache_v.astype(v.dtype), v], axis=1)[:, None]
    bias = _rel_bias(table, t, WINDOW + t, WINDOW)
    valid = jnp.ones((1, WINDOW + t), bool)
    o = _swa_attend(q[:, None], kk, vv, bias, valid, sink)
    return o.reshape(b, t, SWA_WIDTH)


def _cplx_combine(e1, e2):
    a1r, a1i, b1r, b1i = e1
    a2r, a2i, b2r, b2i = e2
    return (a2r * a1r - a2i * a1i, a2r * a1i + a2i * a1r,
            a2r * b1r - a2i * b1i + b2r, a2r * b1i + a2i * b1r + b2i)


def _s5_scan(u, a_re, a_im, log_dt, b_re, b_im, c_re, c_im, d, x0_re, x0_im):
    f32 = jnp.float32
    a_re, a_im = a_re.astype(f32), a_im.astype(f32)
    b_re, b_im, c_re, c_im = b_re.astype(f32), b_im.astype(f32), c_re.astype(f32), c_im.astype(f32)
    dt = jnp.exp(log_dt.astype(f32))[:, None]
    mag = jnp.exp(a_re * dt)
    ab_re, ab_im = mag * jnp.cos(a_im * dt), mag * jnp.sin(a_im * dt)
    den = a_re * a_re + a_im * a_im
    num_re, num_im = ab_re - 1.0, ab_im
    g_re = (num_re * a_re + num_im * a_im) / den
    g_im = (num_im * a_re - num_re * a_im) / den
    bb_re = g_re[..., None] * b_re - g_im[..., None] * b_im
    bb_im = g_re[..., None] * b_im + g_im[..., None] * b_re
    bsz, t, g, h = u.shape
    L = _chunk_len(t)
    nc = t // L
    uc = jnp.moveaxis(u.reshape(bsz, nc, L, g, h), 1, 0)

    def step(carry, u_blk):
        xr, xi = carry
        bu_re = jnp.einsum('gph,blgh->blgp', bb_re, u_blk)
        bu_im = jnp.einsum('gph,blgh->blgp', bb_im, u_blk)
        bu_re = bu_re.at[:, 0].add(ab_re * xr - ab_im * xi)
        bu_im = bu_im.at[:, 0].add(ab_re * xi + ab_im * xr)
        ar = jnp.broadcast_to(ab_re, bu_re.shape)
        ai = jnp.broadcast_to(ab_im, bu_re.shape)
        _, _, sr, si = lax.associative_scan(_cplx_combine, (ar, ai, bu_re, bu_im), axis=1)
        y = jnp.einsum('ghp,blgp->blgh', c_re, sr) - jnp.einsum('ghp,blgp->blgh', c_im, si)
        return (sr[:, -1], si[:, -1]), y

    (xr, xi), y = lax.scan(step, (x0_re, x0_im), uc)
    y = jnp.moveaxis(y, 0, 1).reshape(bsz, t, g, h) + d.astype(f32).reshape(g, h) * u
    return y, xr, xi


def _gla(q, k, v, g, s0):
    bsz, t, h, dk = q.shape
    dv = v.shape[-1]
    L = _chunk_len(t)
    nc = t // L
    q, k, v, g = [a.reshape(bsz, nc, L, h, a.shape[-1]) for a in (q, k, v, g)]
    cum = jnp.cumsum(g, axis=2)
    cum_last = cum[:, :, -1]
    qe = q * jnp.exp(cum)
    ke = k * jnp.exp(-cum)
    causal = jnp.tril(jnp.ones((L, L), bool))
    att = jnp.where(causal, jnp.einsum('bclhk,bcshk->bchls', qe, ke), 0.0)
    o = jnp.einsum('bchls,bcshv->bclhv', att, v)
    upd = jnp.einsum('bclhk,bclhv->bchkv', k * jnp.exp(cum_last[:, :, None] - cum), v)

    def step(s, inp):
        dec, u = inp
        return dec[..., None] * s + u, s

    s_fin, s_prev = lax.scan(step, s0, (jnp.moveaxis(jnp.exp(cum_last), 1, 0), jnp.moveaxis(upd, 1, 0)))
    o = o + jnp.einsum('bclhk,cbhkv->bclhv', qe, s_prev)
    return o.reshape(bsz, t, h, dv), s_fin


def _ssd(x, dt, a, bm, cm, s0):
    bsz, t, h, p = x.shape
    L = _chunk_len(t)
    nc = t // L
    x = x.reshape(bsz, nc, L, h, p)
    dt = dt.reshape(bsz, nc, L, h)
    bm = bm.reshape(bsz, nc, L, h, -1)
    cm = cm.reshape(bsz, nc, L, h, -1)
    cum = jnp.cumsum(dt * a, axis=2)
    seg = cum[:, :, :, None, :] - cum[:, :, None, :, :]
    causal = jnp.tril(jnp.ones((L, L), bool))[..., None]
    decay = jnp.exp(jnp.where(causal, seg, -jnp.inf))
    cb = jnp.einsum('bclhn,bcshn->bclsh', cm, bm)
    y = jnp.einsum('bclsh,bcsh,bcshp->bclhp', cb * decay, dt, x)
    upd = jnp.einsum('bclh,bclhn,bclhp->bchpn', jnp.exp(cum[:, :, -1:] - cum) * dt, bm, x)

    def step(s, inp):
        dec, u = inp
        return dec[..., None, None] * s + u, s

    s_fin, s_prev = lax.scan(step, s0, (jnp.moveaxis(jnp.exp(cum[:, :, -1]), 1, 0), jnp.moveaxis(upd, 1, 0)))
    y = y + jnp.einsum('bclhn,cbhpn,bclh->bclhp', cm, s_prev, jnp.exp(cum))
    return y.reshape(bsz, t, h, p), s_fin


def _even_mixers(h, P, i, s5_re0, s5_im0, gla_s0):
    f32 = jnp.float32
    bsz, t, _ = h.shape
    z = h @ P['ev_w_in'][i]
    o1 = S5_WIDTH
    o2 = o1 + GLA_HEADS * GLA_DK
    o3 = o2 + GLA_HEADS * GLA_DK
    o4 = o3 + GLA_WIDTH
    o5 = o4 + GLA_GATE_RANK
    u, q, k, v, gl, r = jnp.split(z, [o1, o2, o3, o4, o5], axis=-1)
    ua = u.astype(f32).reshape(bsz, t, S5_GROUPS, S5_GROUP)
    ya, s5_re, s5_im = _s5_scan(ua, P['s5_a_re'][i], P['s5_a_im'][i], P['s5_log_dt'][i],
                                P['s5_b_re'][i], P['s5_b_im'][i], P['s5_c_re'][i], P['s5_c_im'][i],
                                P['s5_d'][i], s5_re0.astype(f32), s5_im0.astype(f32))
    ya = jax.nn.gelu(ya.reshape(bsz, t, S5_WIDTH))
    ya = ya * jax.nn.sigmoid(ya @ P['s5_w_glu'][i].astype(f32) + P['s5_b_glu'][i].astype(f32))
    qh = q.astype(f32).reshape(bsz, t, GLA_HEADS, GLA_DK) * GLA_DK ** -0.5
    kh = k.astype(f32).reshape(bsz, t, GLA_HEADS, GLA_DK)
    vh = v.astype(f32).reshape(bsz, t, GLA_HEADS, GLA_DV)
    gate = jax.nn.log_sigmoid((gl @ P['gla_w_gate2'][i] + P['gla_b_gate'][i]).astype(f32)) / GLA_GATE_NORM
    ob, gla_s = _gla(qh, kh, vh, gate.reshape(bsz, t, GLA_HEADS, GLA_DK), gla_s0.astype(f32))
    ob = _rms(ob, P['gla_norm_g'][i]) * jax.nn.silu(r.astype(f32).reshape(bsz, t, GLA_HEADS, GLA_DV))
    mixed = jnp.concatenate([ya, ob.reshape(bsz, t, GLA_WIDTH)], axis=-1).astype(h.dtype)
    return mixed @ P['ev_w_out'][i], s5_re, s5_im, gla_s


def _odd_mixers(h, P, i, ssd_s0, conv_prefix, cache_k, cache_v):
    f32 = jnp.float32
    bsz, t, _ = h.shape
    z = h @ P['od_w_in'][i]
    o1 = SWA_WIDTH
    o2 = o1 + SWA_KV_HEADS * HEAD_DIM
    o3 = o2 + SWA_KV_HEADS * HEAD_DIM
    o4 = o3 + SSD_INNER
    o5 = o4 + SSD_CONV_DIM
    q, k, v, zg, xbc, dtr = jnp.split(z, [o1, o2, o3, o4, o5], axis=-1)
    q = _rms(q.reshape(bsz, t, SWA_KV_HEADS, SWA_GROUP, HEAD_DIM), P['swa_q_norm'][i])
    k = _rms(k.reshape(bsz, t, SWA_KV_HEADS, HEAD_DIM), P['swa_k_norm'][i])
    v = v.reshape(bsz, t, SWA_KV_HEADS, HEAD_DIM)
    sink = P['swa_sink'][i].reshape(SWA_KV_HEADS, SWA_GROUP)
    if cache_k is None:
        oc, k_new, v_new = _swa_prompt(q, k, v, P['t5_bias'], sink)
    else:
        oc = _swa_sample(q, k, v, cache_k, cache_v, P['t5_bias'], sink)
        k_new, v_new = k, v
    xbc, conv_state = _causal_dwconv(xbc, conv_prefix, P['ssd_conv_w'][i], P['ssd_conv_b'][i])
    xbc = jax.nn.silu(xbc.astype(f32))
    xs, bm, cm = jnp.split(xbc, [SSD_INNER, SSD_INNER + SSD_GROUPS * SSD_STATE], axis=-1)
    xs = xs.reshape(bsz, t, SSD_HEADS, SSD_HEAD_DIM)
    rep = SSD_HEADS // SSD_GROUPS
    bm = jnp.repeat(bm.reshape(bsz, t, SSD_GROUPS, SSD_STATE), rep, axis=2)
    cm = jnp.repeat(cm.reshape(bsz, t, SSD_GROUPS, SSD_STATE), rep, axis=2)
    dt = jax.nn.softplus(dtr.astype(f32) + P['ssd_dt_bias'][i].astype(f32))
    a = -jnp.exp(P['ssd_a_log'][i].astype(f32))
    yd, ssd_s = _ssd(xs, dt, a, bm, cm, ssd_s0.astype(f32))
    yd = yd + P['ssd_d'][i].astype(f32)[:, None] * xs
    yd = _rms(yd.reshape(bsz, t, SSD_INNER) * jax.nn.silu(zg.astype(f32)), P['ssd_norm_g'][i])
    mixed = jnp.concatenate([oc.astype(f32), yd], axis=-1).astype(h.dtype)
    return mixed @ P['od_w_out'][i], k_new, v_new, ssd_s, conv_state


def _conv_ffn(h, P, layer, prefix):
    u = h @ P['ffn_w_up'][layer]
    uc, new_prefix = _causal_dwconv(u, prefix, P['ffn_conv_w'][layer], P['ffn_conv_b'][layer])
    a, gate = jnp.split(uc, 2, axis=-1)
    return (jax.nn.silu(a) * gate) @ P['ffn_w_down'][layer], new_prefix


def _trunk(x, c, P, st, sample):
    new = {name: [] for name in STATE_NAMES}
    cm = jax.nn.silu(c)
    for layer in range(DEPTH):
        i = layer // 2
        mod = cm @ P['w_mod'][layer] + P['b_mod'][layer]
        sh1, sc1, g1, sh2, sc2, g2 = jnp.split(mod[:, None, :], 6, axis=-1)
        hn = _rms(x, P['norm1_g'][layer]) * (1 + sc1) + sh1
        if layer % 2 == 0:
            out, sr, si, sg = _even_mixers(hn, P, i, st['s5_re'][i], st['s5_im'][i], st['gla'][i])
            new['s5_re'].append(sr)
            new['s5_im'].append(si)
            new['gla'].append(sg)
        else:
            ck = st['swa_k'][i] if sample else None
            cv = st['swa_v'][i] if sample else None
            out, kn, vn, ss, sc = _odd_mixers(hn, P, i, st['ssd'][i], st['ssd_conv'][i], ck, cv)
            new['swa_k'].append(kn)
            new['swa_v'].append(vn)
            new['ssd'].append(ss)
            new['ssd_conv'].append(sc)
        x = x + g1 * out.astype(x.dtype)
        hn = _rms(x, P['norm2_g'][layer]) * (1 + sc2) + sh2
        f, fp = _conv_ffn(hn, P, layer, st['ffn_conv'][layer])
        new['ffn_conv'].append(fp)
        x = x + g2 * f.astype(x.dtype)
    return x, {name: jnp.stack(vals) for name, vals in new.items()}


def setup_inputs(seed: int = 0) -> dict:
    key = jax.random.key(seed)
    ks = iter(jax.random.split(key, 64))
    f32 = jnp.float32

    def nrm(shape, scale=1.0):
        return scale * jax.random.normal(next(ks), shape, f32)

    def unif(shape, lo, hi):
        return jax.random.uniform(next(ks), shape, f32, lo, hi)

    E, O = N_EVEN, N_ODD
    dt_ssd = jnp.exp(unif((O, SSD_HEADS), math.log(1e-3), math.log(1e-1)))
    a_im0 = jnp.broadcast_to(math.pi * jnp.arange(S5_STATE, dtype=f32), (E, S5_GROUPS, S5_STATE))
    return {
        'x_prompt': nrm((BATCH, SEQ, D_MODEL)),
        'x_sample': nrm((DEC_BATCH, DEC_SEQ, D_MODEL)),
        'state_s5_re': nrm((E, DEC_BATCH, S5_GROUPS, S5_STATE), 0.3),
        'state_s5_im': nrm((E, DEC_BATCH, S5_GROUPS, S5_STATE), 0.3),
        'state_gla': nrm((E, DEC_BATCH, GLA_HEADS, GLA_DK, GLA_DV), 1.0),
        'cache_swa_k': nrm((O, DEC_BATCH, WINDOW, SWA_KV_HEADS, HEAD_DIM)),
        'cache_swa_v': nrm((O, DEC_BATCH, WINDOW, SWA_KV_HEADS, HEAD_DIM)),
        'state_ssd': nrm((O, DEC_BATCH, SSD_HEADS, SSD_HEAD_DIM, SSD_STATE), 0.3),
        'state_ssd_conv': nrm((O, DEC_BATCH, SSD_CONV - 1, SSD_CONV_DIM)),
        'state_ffn_conv': nrm((DEPTH, DEC_BATCH, FFN_CONV - 1, 2 * D_FF)),
        'c_prompt': nrm((BATCH, D_MODEL)),
        'c_sample': nrm((DEC_BATCH, D_MODEL)),
        't5_bias': nrm((T5_BUCKETS, SWA_HEADS), 0.5),
        'norm1_g': 1.0 + nrm((DEPTH, D_MODEL), 0.02),
        'norm2_g': 1.0 + nrm((DEPTH, D_MODEL), 0.02),
        'w_mod': nrm((DEPTH, D_MODEL, 6 * D_MODEL), 0.5 * D_MODEL ** -0.5),
        'b_mod': nrm((DEPTH, 6 * D_MODEL), 0.02),
        'ffn_w_up': nrm((DEPTH, D_MODEL, 2 * D_FF), D_MODEL ** -0.5),
        'ffn_conv_w': nrm((DEPTH, FFN_CONV, 2 * D_FF), 0.5),
        'ffn_conv_b': nrm((DEPTH, 2 * D_FF), 0.02),
        'ffn_w_down': nrm((DEPTH, D_FF, D_MODEL), D_FF ** -0.5),
        'ev_w_in': nrm((E, D_MODEL, EVEN_IN), D_MODEL ** -0.5),
        'ev_w_out': nrm((E, S5_WIDTH + GLA_WIDTH, D_MODEL), (S5_WIDTH + GLA_WIDTH) ** -0.5),
        's5_a_re': -0.5 + nrm((E, S5_GROUPS, S5_STATE), 0.01),
        's5_a_im': a_im0 + nrm((E, S5_GROUPS, S5_STATE), 0.01),
        's5_log_dt': unif((E, S5_GROUPS), math.log(1e-3), math.log(1e-1)),
        's5_b_re': nrm((E, S5_GROUPS, S5_STATE, S5_GROUP), (2 * S5_GROUP) ** -0.5),
        's5_b_im': nrm((E, S5_GROUPS, S5_STATE, S5_GROUP), (2 * S5_GROUP) ** -0.5),
        's5_c_re': nrm((E, S5_GROUPS, S5_GROUP, S5_STATE), 0.5),
        's5_c_im': nrm((E, S5_GROUPS, S5_GROUP, S5_STATE), 0.5),
        's5_d': nrm((E, S5_WIDTH), 1.0),
        's5_w_glu': nrm((E, S5_WIDTH, S5_WIDTH), S5_WIDTH ** -0.5),
        's5_b_glu': nrm((E, S5_WIDTH), 0.02),
        'gla_w_gate2': nrm((E, GLA_GATE_RANK, GLA_HEADS * GLA_DK), GLA_GATE_RANK ** -0.5),
        'gla_b_gate': nrm((E, GLA_HEADS * GLA_DK), 0.02),
        'gla_norm_g': 1.0 + nrm((E, GLA_DV), 0.02),
        'od_w_in': nrm((O, D_MODEL, ODD_IN), D_MODEL ** -0.5),
        'od_w_out': nrm((O, SWA_WIDTH + SSD_INNER, D_MODEL), (SWA_WIDTH + SSD_INNER) ** -0.5),
        'swa_q_norm': 1.0 + nrm((O, HEAD_DIM), 0.02),
        'swa_k_norm': 1.0 + nrm((O, HEAD_DIM), 0.02),
        'swa_sink': nrm((O, SWA_HEADS), 0.5),
        'ssd_conv_w': nrm((O, SSD_CONV, SSD_CONV_DIM), 0.5),
        'ssd_conv_b': nrm((O, SSD_CONV_DIM), 0.02),
        'ssd_dt_bias': dt_ssd + jnp.log(-jnp.expm1(-dt_ssd)),
        'ssd_a_log': jnp.log(unif((O, SSD_HEADS), 1.0, 16.0)),
        'ssd_d': 1.0 + nrm((O, SSD_HEADS), 0.1),
        'ssd_norm_g': 1.0 + nrm((O, SSD_INNER), 0.02),
    }


def reference(x_prompt, x_sample, state_s5_re, state_s5_im, state_gla, cache_swa_k, cache_swa_v,
              state_ssd, state_ssd_conv, state_ffn_conv, c_prompt, c_sample, t5_bias, norm1_g, norm2_g,
              w_mod, b_mod, ffn_w_up, ffn_conv_w, ffn_conv_b, ffn_w_down, ev_w_in, ev_w_out,
              s5_a_re, s5_a_im, s5_log_dt, s5_b_re, s5_b_im, s5_c_re, s5_c_im, s5_d, s5_w_glu, s5_b_glu,
              gla_w_gate2, gla_b_gate, gla_norm_g, od_w_in, od_w_out, swa_q_norm, swa_k_norm, swa_sink,
              ssd_conv_w, ssd_conv_b, ssd_dt_bias, ssd_a_log, ssd_d, ssd_norm_g):
    f32 = jnp.float32
    P = dict(t5_bias=t5_bias, norm1_g=norm1_g, norm2_g=norm2_g, w_mod=w_mod, b_mod=b_mod,
             ffn_w_up=ffn_w_up, ffn_conv_w=ffn_conv_w, ffn_conv_b=ffn_conv_b, ffn_w_down=ffn_w_down,
             ev_w_in=ev_w_in, ev_w_out=ev_w_out, s5_a_re=s5_a_re, s5_a_im=s5_a_im, s5_log_dt=s5_log_dt,
             s5_b_re=s5_b_re, s5_b_im=s5_b_im, s5_c_re=s5_c_re, s5_c_im=s5_c_im, s5_d=s5_d,
             s5_w_glu=s5_w_glu, s5_b_glu=s5_b_glu, gla_w_gate2=gla_w_gate2, gla_b_gate=gla_b_gate,
             gla_norm_g=gla_norm_g, od_w_in=od_w_in, od_w_out=od_w_out, swa_q_norm=swa_q_norm,
             swa_k_norm=swa_k_norm, swa_sink=swa_sink, ssd_conv_w=ssd_conv_w, ssd_conv_b=ssd_conv_b,
             ssd_dt_bias=ssd_dt_bias, ssd_a_log=ssd_a_log, ssd_d=ssd_d, ssd_norm_g=ssd_norm_g)
    bp = x_prompt.shape[0]
    zero_st = dict(
        s5_re=jnp.zeros((N_EVEN, bp, S5_GROUPS, S5_STATE), f32),
        s5_im=jnp.zeros((N_EVEN, bp, S5_GROUPS, S5_STATE), f32),
        gla=jnp.zeros((N_EVEN, bp, GLA_HEADS, GLA_DK, GLA_DV), f32),
        ssd=jnp.zeros((N_ODD, bp, SSD_HEADS, SSD_HEAD_DIM, SSD_STATE), f32),
        ssd_conv=jnp.zeros((N_ODD, bp, SSD_CONV - 1, SSD_CONV_DIM), x_prompt.dtype),
        ffn_conv=jnp.zeros((DEPTH, bp, FFN_CONV - 1, 2 * D_FF), x_prompt.dtype))
    sample_st = dict(s5_re=state_s5_re, s5_im=state_s5_im, gla=state_gla, swa_k=cache_swa_k,
                     swa_v=cache_swa_v, ssd=state_ssd, ssd_conv=state_ssd_conv, ffn_conv=state_ffn_conv)
    y_prompt, stp = _trunk(x_prompt, c_prompt, P, zero_st, False)
    y_sample, sts = _trunk(x_sample, c_sample, P, sample_st, True)
    p_s5_re, p_s5_im, p_gla = stp['s5_re'], stp['s5_im'], stp['gla']
    p_swa_k, p_swa_v, p_ssd = stp['swa_k'], stp['swa_v'], stp['ssd']
    p_ssd_conv, p_ffn_conv = stp['ssd_conv'], stp['ffn_conv']
    s_s5_re, s_s5_im, s_gla = sts['s5_re'], sts['s5_im'], sts['gla']
    s_swa_k, s_swa_v, s_ssd = sts['swa_k'], sts['swa_v'], sts['ssd']
    s_ssd_conv, s_ffn_conv = sts['ssd_conv'], sts['ffn_conv']
    return (y_prompt, y_sample,
            p_s5_re, p_s5_im, p_gla, p_swa_k, p_swa_v, p_ssd, p_ssd_conv, p_ffn_conv,
            s_s5_re, s_s5_im, s_gla, s_swa_k, s_swa_v, s_ssd, s_ssd_conv, s_ffn_conv)
```

```python
import math
import os
from contextlib import ExitStack
import numpy as np
import concourse.bass as bass
import concourse.mybir as mybir
from concourse.bass_utils import run_bass_kernel_spmd

F32 = mybir.dt.float32
BF16 = mybir.dt.bfloat16
I32 = mybir.dt.int32
AF = mybir.ActivationFunctionType
ALU = mybir.AluOpType

NDMA = 48
D = 1024
DFF = 2816
EVEN_IN = 2064
ODD_IN = 2312
EPS = 1e-6


class FW:
    ENGS = ('pe', 'dve', 'act', 'pool', 'sp')

    def __init__(self, nc, stack):
        self.nc = nc
        self.stack = stack
        self.sem = {}
        self.prog = {e: [] for e in self.ENGS}
        self.cnt = {}
        self.seen = {e: {} for e in self.ENGS}
        self.bufs = {}
        for e in self.ENGS:
            self._mksem('E_' + e)
        self.dma_pool = {'sp': [f'DS{i}' for i in range(32)], 'pool': [f'DP{i}' for i in range(24)],
                         'act': [f'DA{i}' for i in range(8)]}
        for q, names in self.dma_pool.items():
            for n in names:
                self._mksem(n)
        self.dma_rr = {'sp': 0, 'pool': 0, 'act': 0}
        self.dma_last_clock = {}
        self.nops = 0
        self.noself = ('pe', 'sp') + tuple(os.environ.get('KNOSELF', '').split(','))

    def _mksem(self, name):
        self.sem[name] = self.stack.enter_context(self.nc.semaphore(name))
        self.cnt[name] = 0

    def _deps(self, reads, writes):
        deps = []
        for b in reads:
            st = self.bufs.get(b)
            if st and st['w'] is not None:
                deps.append(('raw', st['w']))
        for b in writes:
            st = self.bufs.get(b)
            if st:
                if st['w'] is not None:
                    deps.append(('waw', st['w']))
                for r in st['r']:
                    deps.append(('war', r))
        return deps

    def _waits(self, eng, deps):
        seen = self.seen[eng]
        own = 'E_' + eng
        need = {}
        used = []
        for kind, (s, v, clock) in deps:
            if s == own and (eng in self.noself or kind != 'raw'):
                continue
            used.append((s, v, clock))
            if seen.get(s, 0) >= v:
                continue
            if need.get(s, 0) < v:
                need[s] = v
        for s, v, clock in used:
            for cs, cv in clock.items():
                if cs == own:
                    continue
                if seen.get(cs, 0) < cv:
                    seen[cs] = cv
            if seen.get(s, 0) < v:
                seen[s] = v
        return sorted(need.items())

    def _record(self, ev, reads, writes):
        for b in reads:
            st = self.bufs.setdefault(b, {'w': None, 'r': []})
            st['r'].append(ev)
            if len(st['r']) > 96:
                st['r'] = st['r'][-96:]
        for b in writes:
            self.bufs[b] = {'w': ev, 'r': []}

    def op(self, eng, fn, reads=(), writes=()):
        deps = self._deps(reads, writes)
        waits = self._waits(eng, deps)
        own = 'E_' + eng
        self.cnt[own] += 1
        v = self.cnt[own]
        clock = dict(self.seen[eng])
        clock[own] = v
        self.prog[eng].append((waits, fn, (own, 1)))
        self._record((own, v, clock), reads, writes)
        self.nops += 1

    def dma(self, q, fn, reads=(), writes=()):
        deps = self._deps(reads, writes)
        names = self.dma_pool[q]
        s = names[self.dma_rr[q]]
        self.dma_rr[q] = (self.dma_rr[q] + 1) % len(names)
        if self.cnt[s] > 0:
            deps.append(('raw', (s, self.cnt[s], self.dma_last_clock.get(s, {}))))
        waits = self._waits(q, deps)
        self.cnt[s] += 16
        v = self.cnt[s]
        clock = dict(self.seen[q])
        clock.pop('E_' + q, None)
        self.dma_last_clock[s] = clock
        self.prog[q].append((waits, fn, (s, 16)))
        self._record((s, v, clock), reads, writes)
        self.nops += 1

    def finish_waits(self, eng='sp'):
        waits = [(s, v) for s, v in self.cnt.items() if v > 0 and s != 'E_' + eng]
        self.prog[eng].append((waits, None, None))

    def replay(self):
        nc = self.nc
        with nc.Block() as block:
            def mk(engname):
                def body(e):
                    for waits, fn, inc in self.prog[engname]:
                        for s, v in waits:
                            e.wait_ge(self.sem[s], v)
                        if fn is not None:
                            ins = fn(e)
                            ins.then_inc(self.sem[inc[0]], inc[1])
                return body
            block.tensor(mk('pe'))
            block.vector(mk('dve'))
            block.scalar(mk('act'))
            block.gpsimd(mk('pool'))
            block.sync(mk('sp'))


HC_IDENT, HC_MC, HC_MG, HC_MB, HC_S5M, HC_S5E = 0, 128, 256, 384, 512, 640
HC_CS = 768
HC_RST = 1024
HC_OHW = 1536
NHC = 1920


def _t5_bucket_np(rel):
    import jax
    import jax.numpy as jnp
    with jax.default_device(jax.devices('cpu')[0]):
        rel = jnp.asarray(rel, jnp.int32)
        nb = 16
        max_exact = 8
        ret = (rel > 0).astype(jnp.int32) * nb
        n = jnp.abs(rel)
        nf = jnp.maximum(n, 1).astype(jnp.float32)
        large = max_exact + (jnp.log(nf / max_exact) / math.log(128 / max_exact) * (nb - max_exact)).astype(jnp.int32)
        large = jnp.minimum(large, nb - 1)
        out = ret + jnp.where(n < max_exact, n, large)
        return np.asarray(out)


def host_consts():
    hc = np.zeros((128, NHC), np.float32)
    j = np.arange(128)[:, None]
    l = np.arange(128)[None, :]
    same = (j // 64) == (l // 64)
    hc[:, HC_IDENT:HC_IDENT + 128] = (j == l)
    hc[:, HC_MC:HC_MC + 128] = same & (j <= l)
    hc[:, HC_MG:HC_MG + 128] = same & (j > l)
    hc[:, HC_MB:HC_MB + 128] = same
    s_sub = j // 16
    hi = j % 16
    ho = l // 8
    l_sub = l % 8
    hc[:, HC_S5M:HC_S5M + 128] = (l_sub >= s_sub)
    hc[:, HC_S5E:HC_S5E + 128] = (l_sub == s_sub) & (ho == hi)
    for c in range(2):
        hc[:, HC_CS + c * 128:HC_CS + (c + 1) * 128] = ((j // 64) == c)
    rst = np.ones(512, np.float32)
    rst[::64] = 0.0
    hc[:, HC_RST:HC_RST + 512] = rst[None, :]
    m = np.arange(128)
    d = np.where(m < 64, m, m - 128)
    for kb in range(3):
        rel = kb * 64 - 128 - d
        bk = _t5_bucket_np(rel)
        for mm in range(128):
            hc[bk[mm], HC_OHW + kb * 128 + mm] = 1.0
    return hc


class Reg:
    def __init__(self, ap, keys):
        self.ap = ap
        self.keys = list(keys)


class Builder:
    def __init__(self, SEQ, do_l1=True):
        self.SEQ = SEQ
        self.do_l1 = do_l1
        self.nc = bass.Bass("TRN2", target_bir_lowering=False)
        self.bank_rr = 0
        self.stg_rr = 0
        self.stage = int(os.environ.get('KSTAGE', '99'))

    def din(self, name, shape, dt=F32):
        return self.nc.dram_tensor(name, list(shape), dt, kind="ExternalInput").ap()

    def dout(self, name, shape, dt=F32):
        return self.nc.dram_tensor(name, list(shape), dt, kind="ExternalOutput").ap()

    def dscr(self, name, shape, dt):
        return self.nc.dram_tensor(name, list(shape), dt, kind="Internal").ap()

    def sb(self, name, shape, dt):
        return self.st.enter_context(self.nc.sbuf_tensor(name, list(shape), dt))

    def MM(self, out, lhsT, rhs, st, sp, r, w):
        self.fw.op('pe', lambda e: e.matmul(out, lhsT=lhsT, rhs=rhs, start=st, stop=sp), r, w)

    def TR(self, out, in_, idn, r, w):
        self.fw.op('pe', lambda e: e.transpose(out=out, in_=in_, identity=idn), r, w)

    def ACT(self, out, in_, func, r, w, bias=None, scale=None, eng='act'):
        kw = {}
        if bias is not None:
            kw['bias'] = bias
        if scale is not None:
            kw['scale'] = scale
        self.fw.op(eng, lambda e: e.activation(out=out, in_=in_, func=func, **kw), r, w)

    def TT(self, eng, out, a, b, op, r, w):
        self.fw.op(eng, lambda e: e.tensor_tensor(out=out, in0=a, in1=b, op=op), r, w)

    def TS(self, eng, out, a, s1, s2, op0, op1, r, w):
        if op1 is None:
            self.fw.op(eng, lambda e: e.tensor_scalar(out=out, in0=a, scalar1=s1, scalar2=None, op0=op0), r, w)
        else:
            self.fw.op(eng, lambda e: e.tensor_scalar(out=out, in0=a, scalar1=s1, scalar2=s2, op0=op0, op1=op1), r, w)

    def STT(self, out, a, s, b, op0, op1, r, w):
        self.fw.op('dve', lambda e: e.scalar_tensor_tensor(out=out, in0=a, scalar=s, in1=b, op0=op0, op1=op1), r, w)

    def CP(self, eng, out, in_, r, w):
        if eng == 'act':
            self.fw.op('act', lambda e: e.copy(out=out, in_=in_), r, w)
        else:
            self.fw.op(eng, lambda e: e.tensor_copy(out=out, in_=in_), r, w)

    def MEMSET(self, eng, ap, val, w):
        self.fw.op(eng, lambda e: e.memset(ap, val), (), w)

    def DMA(self, q, out, in_, r, w, nonc=False):
        if nonc:
            self.fw.dma(q, lambda e: e.dma_start(out=out, in_=in_, allow_slow_non_contiguous=True), r, w)
        else:
            self.fw.dma(q, lambda e: e.dma_start(out=out, in_=in_), r, w)

    def bank(self):
        i = self.bank_rr
        self.bank_rr = (self.bank_rr + 1) % 8
        return self.PB[i], f'pb{i}'

    def scr(self, off_bytes, nbytes, dt, shape_tail=None, parts=128):
        assert off_bytes % 4 == 0 and off_bytes + nbytes <= self.SCRN * 4, (off_bytes, nbytes)
        a = self.SCR[0:parts, off_bytes // 4:(off_bytes + nbytes + 3) // 4]
        if dt == BF16:
            a = a.bitcast(BF16)
        keys = [('scr', pg) for pg in range(off_bytes // 2048, (off_bytes + nbytes - 1) // 2048 + 1)]
        return Reg(a, keys)

    def fm_load(self, dst_ap, dst_keys, src_ap, R, W, src_keys=()):
        s = self.stg_rr
        self.stg_rr = (self.stg_rr + 1) % 2
        stg = self.STG[s]
        self.DMA('sp', stg[0:R, 0:W], src_ap, list(src_keys), [f'stg{s}'])
        pb, pk = self.bank()
        self.TR(pb[0:W, 0:R], stg[0:R, 0:W], self.IDf[0:R, 0:R], [f'stg{s}', 'hc'], [pk])
        self.CP('dve', dst_ap, pb[0:W, 0:R], [pk], dst_keys)

    def tm_store(self, dst_ap, src_ap, src_keys, R, W, q='pool'):
        s = self.stg_rr
        self.stg_rr = (self.stg_rr + 1) % 2
        stg = self.STG[s]
        pb, pk = self.bank()
        self.TR(pb[0:R, 0:W], src_ap, self.IDf[0:W, 0:W], list(src_keys) + ['hc'], [pk])
        self.CP('dve', stg[0:R, 0:W], pb[0:R, 0:W], [pk], [f'stg{s}'])
        self.DMA(q, dst_ap, stg[0:R, 0:W], [f'stg{s}'], ['out'])

    def build(self):
        nc = self.nc
        SEQ = self.SEQ
        I = {}
        I['xp'] = self.din('xp', [2, SEQ, D])
        I['xs'] = self.din('xs', [64, D])
        I['cvec'] = self.din('cvec', [24, 128])
        I['st_s5re'] = self.din('st_s5re', [32, 64])
        I['st_s5im'] = self.din('st_s5im', [32, 64])
        I['st_gla'] = self.din('st_gla', [256, 128])
        I['ck'] = self.din('ck', [128, 128])
        I['cv'] = self.din('cv', [128, 128])
        I['st_ssd'] = self.din('st_ssd', [512, 128])
        I['st_ssdconv'] = self.din('st_ssdconv', [24, 128])
        I['st_ffnconv'] = self.din('st_ffnconv', [2, 88, 128])
        for nm, shp in [('t5_bias', [32, 8]), ('norm1_g', [16, 128]), ('norm2_g', [16, 128]), ('w_mod', [2, D, 6144]),
                        ('b_mod', [2, 6144]), ('ffn_w_up', [2, D, 2 * DFF]), ('ffn_conv_w', [2, 132, 128]),
                        ('ffn_conv_b', [2, 44, 128]), ('ffn_w_down', [2, DFF, D]), ('ev_w_in', [D, EVEN_IN]),
                        ('ev_w_out', [D, D]), ('s5_a_re', [32, 64]), ('s5_a_im', [32, 64]), ('s5_log_dt', [32]),
                        ('s5_b_re', [32, 64, 16]), ('s5_b_im', [32, 64, 16]), ('s5_c_re', [512, 64]),
                        ('s5_c_im', [512, 64]), ('s5_d', [512]), ('s5_w_glu', [512, 512]), ('s5_b_glu', [4, 128]),
                        ('gla_w_gate2', [16, 256]), ('gla_b_gate', [2, 128]), ('gla_norm_g', [128]),
                        ('od_w_in', [D, ODD_IN]), ('od_w_out', [D, D]), ('swa_q_norm', [64]), ('swa_k_norm', [64]),
                        ('swa_sink', [8]), ('ssd_conv_w', [32, 128]), ('ssd_conv_b', [8, 128]), ('ssd_dt_bias', [8]),
                        ('ssd_a_log', [8]), ('ssd_d', [8]), ('ssd_norm_g', [512]), ('hc', [128, NHC])]:
            I[nm] = self.din(nm, shp)
        self.I = I
        O = {}
        O['yp'] = self.dout('yp', [2, SEQ, D])
        O['ys'] = self.dout('ys', [64, D])
        O['o_s5re'] = self.dout('o_s5re', [3, 32, 64])
        O['o_s5im'] = self.dout('o_s5im', [3, 32, 64])
        O['o_gla'] = self.dout('o_gla', [3, 256, 128])
        O['o_pk'] = self.dout('o_pk', [2, 128, 128])
        O['o_pv'] = self.dout('o_pv', [2, 128, 128])
        O['o_sk'] = self.dout('o_sk', [64, 128])
        O['o_sv'] = self.dout('o_sv', [64, 128])
        O['o_ssd'] = self.dout('o_ssd', [3, 512, 128])
        O['o_ssdconv'] = self.dout('o_ssdconv', [3, 3, 1024])
        O['o_ffnconv'] = self.dout('o_ffnconv', [3, 2, 2, 2 * DFF])
        self.O = O
        W = {}
        W['up'] = self.dscr('w_up_s', [2, D, 2 * DFF], BF16)
        W['dn'] = self.dscr('w_dn_s', [2, DFF, D], BF16)
        W['evin'] = self.dscr('w_evin_s', [D, EVEN_IN], BF16)
        W['evout'] = self.dscr('w_evout_s', [D, D], BF16)
        W['odin'] = self.dscr('w_odin_s', [D, ODD_IN], BF16)
        W['odout'] = self.dscr('w_odout_s', [D, D], BF16)
        W['s5t0'] = self.dscr('w_s5t0', [128, 32 * 128], BF16)
        W['s5win'] = self.dscr('w_s5win', [128, 32 * 128], BF16)
        W['s5wout'] = self.dscr('w_s5wout', [64, 2 * 32 * 128], BF16)
        W['biasd'] = self.dscr('w_biasd', [8, 3, 64 * 128], F32)
        W['evin_lo'] = self.dscr('w_evin_lo', [D, 512], BF16)
        self.W = W

        with ExitStack() as st:
            self.st = st
            self.fw = FW(nc, st)
            self.alloc()
            self.prologue()
            for job in [int(x) for x in os.environ.get('KJOBS', '0,1,2').split(',')]:
                if self.stage >= 6:
                    self.run_job(job)
            self.fw.finish_waits('sp')
            self.fw.replay()
        return nc

    def alloc(self):
        nc = self.nc
        self.PB = [self.st.enter_context(nc.psum_tensor(f'pb{i}', [128, 512], F32)) for i in range(8)]
        self.HC = self.sb('HC', [128, NHC], F32)
        self.IDf = self.HC[:, HC_IDENT:HC_IDENT + 128]
        self.IDb = self.sb('IDb', [128, 128], BF16)
        self.ONESb = self.sb('ONESb', [128, 128], BF16)
        self.MBb = self.sb('MBb', [128, 128], BF16)
        self.STG = [self.sb(f'STG{i}', [128, 128], F32) for i in range(2)]
        self.PVN = 1100
        self.PV = self.sb('PV', [128, self.PVN], F32)
        self.BC = self.sb('BC', [128, 640], F32)
        self.XRES = self.sb('XRES', [128, 8, 512], F32)
        self.ACT8 = self.sb('ACT8', [128, 8, 512], BF16)
        self.WS = [self.sb(f'WS{i}', [128, 4096], BF16) for i in range(4)]
        self.ws_rr = 0
        self.WG2 = self.sb('WG2', [16, 256], BF16)
        self.WGL = self.sb('WGL', [128, 8, 16], BF16)
        self.WGLU = self.sb('WGLU', [128, 4, 512], BF16)
        self.A1 = self.sb('A1', [64, 2, 32], F32)
        self.A2 = self.sb('A2', [64, 2, 32], F32)
        self.S3 = self.sb('S3', [64, 3, 32], F32)
        self.GS = self.sb('GS', [128, 2, 128], F32)
        self.GSH = self.sb('GSH', [128, 9, 2, 128], BF16)
        self.FT = self.sb('FT', [128, 2, 44, 2], F32)
        self.FCORR = self.sb('FCORR', [128, 44, 2], F32)
        self.FTMP = self.sb('FTMP', [128, 44, 2], F32)
        self.HNLO = self.sb('HNLO', [128, 8, 512], BF16)
        self.ST = self.sb('ST', [128, 512], F32)
        self.STBH = self.sb('STBH', [128, 9, 512], BF16)
        self.CT = self.sb('CT', [128, 8, 4], F32)
        self.CCORR = self.sb('CCORR', [128, 8, 3], F32)
        self.CTMP = self.sb('CTMP', [128, 8, 3], F32)
        self.KWIN = self.sb('KWIN', [128, 640], BF16)
        self.VWP = self.sb('VWP', [128, 5, 128], BF16)
        self.BIE = self.sb('BIE', [128, 2, 512], F32)
        self.BIO = self.sb('BIO', [128, 2, 512], F32)
        self.WDT = self.sb('WDT', [128, 8, 8], BF16)
        self.T5T = self.sb('T5T', [32, 8], F32)
        self.WROW = self.sb('WROW', [8, 384], F32)
        self.SCRN = 18432
        self.SCR = self.sb('SCR', [128, self.SCRN], F32)
        c = 0
        self.pv = {}
        for nm, n in [('n1g', 16), ('n2g', 16), ('cfm', 24), ('csilu', 24), ('mod', 288), ('gs1', 48), ('gs2', 48),
                      ('fcw', 264), ('fcb', 88), ('bglu', 4), ('negbg', 2), ('glang', 1), ('eps', 1), ('one', 1),
                      ('scw', 32), ('scb', 8), ('qng', 1), ('kng', 1), ('sinkexp', 4), ('dvec', 32), ('tmp', 64)]:
            self.pv[nm] = c
            c += n
        assert c <= self.PVN, c

    def pvc(self, nm, i=0, n=1):
        c = self.pv[nm] + i
        return self.PV[:, c:c + n]

    def prologue(self):
        I, W = self.I, self.W
        fw = self.fw
        self.DMA('sp', self.HC[:], I['hc'], ['in_hc'], ['hc'])
        self.CP('dve', self.IDb[:], self.IDf, ['hc'], ['idb'])
        self.MEMSET('pool', self.ONESb[:], 1.0, ['onesb'])
        self.CP('dve', self.MBb[:], self.HC[:, HC_MB:HC_MB + 128], ['hc'], ['mbb'])
        self.MEMSET('pool', self.pvc('eps'), EPS, ['pv_eps'])
        self.MEMSET('pool', self.pvc('one'), 1.0, ['pv_one'])
        if self.stage < 1:
            return
        for l in range(2):
            for r0 in range(0, D, 128):
                self.DMA('pool', W['up'][l, r0:r0 + 128, :], I['ffn_w_up'][l, r0:r0 + 128, :], ['in_w'], ['w_up'])
            for r0 in range(0, DFF, 256):
                self.DMA('pool', W['dn'][l, r0:r0 + 256, :], I['ffn_w_down'][l, r0:r0 + 256, :], ['in_w'], ['w_dn'])
        for nm, src in [('evin', 'ev_w_in'), ('evout', 'ev_w_out'), ('odin', 'od_w_in'), ('odout', 'od_w_out')]:
            for r0 in range(0, D, 256):
                self.DMA('pool', W[nm][r0:r0 + 256, :], I[src][r0:r0 + 256, :], ['in_w'], ['w_' + nm])
        if self.stage < 2:
            return
        for r in range(8):
            wf = self.scr(0, 2048, F32)
            hi = self.scr(2048, 1024, BF16)
            d32 = self.scr(4096, 2048, F32)
            lo = self.scr(6144, 1024, BF16)
            self.DMA('sp', wf.ap[:, 0:512], I['ev_w_in'][r * 128:(r + 1) * 128, 512:1024], ['in_w'], wf.keys)
            self.CP('dve', hi.ap[:, 0:512], wf.ap[:, 0:512], wf.keys, hi.keys)
            self.TT('dve', d32.ap[:, 0:512], wf.ap[:, 0:512], hi.ap[:, 0:512], ALU.subtract, wf.keys + hi.keys, d32.keys)
            self.CP('act', lo.ap[:, 0:512], d32.ap[:, 0:512], d32.keys, lo.keys)
            self.DMA('sp', W['evin_lo'][r * 128:(r + 1) * 128, :], lo.ap[:, 0:512], lo.keys, ['w_evin_lo'])
        self.DMA('pool', self.WG2[:], I['gla_w_gate2'], ['in_w'], ['wg2'])
        self.DMA('pool', self.WGL[:], I['ev_w_in'][:, 1536:1552].rearrange("(k p) n -> p k n", p=128), ['in_w'], ['wgl'])
        self.DMA('pool', self.WGLU[:], I['s5_w_glu'].rearrange("(k p) n -> p k n", p=128), ['in_w'], ['wglu'])
        if self.stage < 3:
            return
        self.fm_load(self.pvc('n1g', 0, 16), ['pv_n1g'], I['norm1_g'], 16, 128)
        self.fm_load(self.pvc('n2g', 0, 16), ['pv_n2g'], I['norm2_g'], 16, 128)
        self.fm_load(self.pvc('cfm', 0, 24), ['pv_cfm'], I['cvec'], 24, 128)
        for l in range(2):
            self.fm_load(self.pvc('fcw', l * 132, 128), ['pv_fcw'], I['ffn_conv_w'][l, 0:128, :], 128, 128)
            self.fm_load(self.pvc('fcw', l * 132 + 128, 4), ['pv_fcw'], I['ffn_conv_w'][l, 128:132, :], 4, 128)
            self.fm_load(self.pvc('fcb', l * 44, 44), ['pv_fcb'], I['ffn_conv_b'][l], 44, 128)
        self.fm_load(self.pvc('bglu', 0, 4), ['pv_bglu'], I['s5_b_glu'], 4, 128)
        self.fm_load(self.pvc('negbg', 0, 2), ['pv_negbg'], I['gla_b_gate'], 2, 128)
        self.TS('dve', self.pvc('negbg', 0, 2), self.pvc('negbg', 0, 2), -1.0, None, ALU.mult, None, ['pv_negbg'], ['pv_negbg'])
        self.DMA('sp', self.pvc('glang'), I['gla_norm_g'].rearrange("(p o) -> p o", o=1), ['in_w'], ['pv_glang'], nonc=True)
        if self.stage < 4:
            return
        self.ACT(self.pvc('csilu', 0, 24), self.pvc('cfm', 0, 24), AF.Silu, ['pv_cfm'], ['pv_csilu'])
        csl = self.PV[:, self.pv['csilu']:self.pv['csilu'] + 24].rearrange("p (j k) -> p k j", k=8)
        wm = self.scr(0, 8 * 512 * 4, F32)
        wmv = wm.ap.rearrange("p (k n) -> p k n", k=8)
        brow = self.scr(16384, 512 * 4, F32, parts=1)
        onesrow = self.scr(16384 + 2048, 16, F32, parts=1)
        self.MEMSET('dve', onesrow.ap[0:1, 0:3], 1.0, onesrow.keys)
        for l in range(2):
            for cb in range(12):
                self.DMA('sp', wmv, I['w_mod'][l, :, cb * 512:(cb + 1) * 512].rearrange("(k p) n -> p k n", p=128),
                         ['in_w'], wm.keys)
                self.DMA('sp', brow.ap[0:1, 0:512], I['b_mod'][l:l + 1, cb * 512:(cb + 1) * 512], ['in_w'], brow.keys)
                pb, pk = self.bank()
                for mc in range(4):
                    for k in range(8):
                        self.MM(pb[:, mc * 4:mc * 4 + 3], wmv[:, k, mc * 128:(mc + 1) * 128], csl[:, k, :],
                                k == 0, False, wm.keys + ['pv_csilu'], [pk])
                    self.MM(pb[:, mc * 4:mc * 4 + 3], brow.ap[0:1, mc * 128:(mc + 1) * 128], onesrow.ap[0:1, 0:3],
                            False, True, brow.keys + onesrow.keys, [pk])
                for job in range(3):
                    dst = self.pvc('mod', job * 96 + l * 48 + cb * 4, 4)
                    self.CP('dve', dst, pb[:, job:job + 13:4], [pk], ['pv_mod'])
        for job in range(3):
            for l in range(2):
                for which, gname, scoff in [(0, 'n1g', 8), (1, 'n2g', 32)]:
                    dst = self.pvc('gs1' if which == 0 else 'gs2', (job * 2 + l) * 8, 8)
                    sc = self.pvc('mod', job * 96 + l * 48 + scoff, 8)
                    g = self.pvc(gname, l * 8, 8)
                    self.STT(dst, sc, 1.0, g, ALU.add, ALU.mult, ['pv_mod', 'pv_' + gname], ['pv_gs'])
        if self.stage < 5:
            return
        self.s5_prologue()
        if self.do_l1:
            self.l1_prologue()

    def modv(self, job, l, which, k0=0, n=8):
        return self.pvc('mod', job * 96 + l * 48 + which * 8 + k0, n)

    def s5_prologue(self):
        I, W = self.I, self.W
        off = [0]

        def A(n, parts=64):
            r = self.scr(off[0], n * 4, F32, parts=parts)
            off[0] += n * 4
            return r
        are, aim, ldt = A(32), A(32), A(32)
        zr, zi, wr, wi, t1, t2, t3 = A(32), A(32), A(32), A(32), A(32), A(32), A(32)
        LR, LI = A(9 * 32), A(9 * 32)
        gr, gi = A(32), A(32)
        Bre, Bim = A(512), A(512)
        BBr, BBi = A(512), A(512)
        Cre, Cim = A(512), A(512)
        GC = 8
        WTr, WTi = A(GC * 128), A(GC * 128)
        Qr, Qi = A(GC * 128), A(GC * 128)
        Qsr, Qsi = A(GC * 128), A(GC * 128)
        tA, tB = A(GC * 128), A(GC * 128)
        stgb = self.scr(off[0], 2 * 128 * 2, BF16)
        off[0] += 512
        tmpf = self.scr(off[0], 512, F32)
        off[0] += 512
        assert off[0] <= self.SCRN * 4, off[0]
        dve = 'dve'

        def tt(o, a, b, op):
            self.TT(dve, o.ap if isinstance(o, Reg) else o[0], a.ap if isinstance(a, Reg) else a[0],
                    b.ap if isinstance(b, Reg) else b[0], op,
                    (a.keys if isinstance(a, Reg) else a[1]) + (b.keys if isinstance(b, Reg) else b[1]),
                    o.keys if isinstance(o, Reg) else o[1])

        def V(reg, ap):
            return (ap, reg.keys)
        self.fm_load(are.ap, are.keys, I['s5_a_re'], 32, 64)
        self.fm_load(aim.ap, aim.keys, I['s5_a_im'], 32, 64)
        self.DMA('sp', ldt.ap, I['s5_log_dt'].partition_broadcast(64), ['in_w'], ldt.keys)
        self.DMA('sp', Bre.ap.rearrange("p (g h) -> p g h", h=16), I['s5_b_re'].rearrange("g p h -> p g h"), ['in_w'], Bre.keys)
        self.DMA('sp', Bim.ap.rearrange("p (g h) -> p g h", h=16), I['s5_b_im'].rearrange("g p h -> p g h"), ['in_w'], Bim.keys)
        for q4 in range(4):
            self.fm_load(Cre.ap[:, q4 * 128:(q4 + 1) * 128], Cre.keys, I['s5_c_re'][q4 * 128:(q4 + 1) * 128, :], 128, 64)
            self.fm_load(Cim.ap[:, q4 * 128:(q4 + 1) * 128], Cim.keys, I['s5_c_im'][q4 * 128:(q4 + 1) * 128, :], 128, 64)
        self.ACT(ldt.ap, ldt.ap, AF.Exp, ldt.keys, ldt.keys)
        self.STT(zr.ap, are.ap, 1.0 / 256.0, ldt.ap, ALU.mult, ALU.mult, are.keys + ldt.keys, zr.keys)
        self.STT(zi.ap, aim.ap, 1.0 / 256.0, ldt.ap, ALU.mult, ALU.mult, aim.keys + ldt.keys, zi.keys)

        def cmul(or_, oi_, ar, ai, br, bi):
            tt(t1, ar, br, ALU.mult)
            tt(t2, ai, bi, ALU.mult)
            tt(or_, t1, t2, ALU.subtract)
            tt(t1, ar, bi, ALU.mult)
            tt(t2, ai, br, ALU.mult)
            tt(oi_, t1, t2, ALU.add)
        hr, hi_ = wr, wi
        self.TS(dve, hr.ap, zr.ap, 0.2, 1.0, ALU.mult, ALU.add, zr.keys, hr.keys)
        self.TS(dve, hi_.ap, zi.ap, 0.2, None, ALU.mult, None, zi.keys, hi_.keys)
        nr = t3
        for dv in (4.0, 3.0, 2.0):
            cmul(nr, gi, zr, zi, hr, hi_)
            self.TS(dve, hr.ap, nr.ap, 1.0 / dv, 1.0, ALU.mult, ALU.add, nr.keys, hr.keys)
            self.TS(dve, hi_.ap, gi.ap, 1.0 / dv, None, ALU.mult, None, gi.keys, hi_.keys)
        cmul(nr, gi, zr, zi, hr, hi_)
        self.CP(dve, wr.ap, nr.ap, nr.keys, wr.keys)
        self.CP(dve, wi.ap, gi.ap, gi.keys, wi.keys)
        for _ in range(8):
            self.TS(dve, zr.ap, wr.ap, 2.0, None, ALU.add, None, wr.keys, zr.keys)
            cmul(nr, gi, wr, wi, zr, wi)
            self.CP(dve, wr.ap, nr.ap, nr.keys, wr.keys)
            self.CP(dve, wi.ap, gi.ap, gi.keys, wi.keys)
        LRv = LR.ap.rearrange("p (n g) -> p n g", g=32)
        LIv = LI.ap.rearrange("p (n g) -> p n g", g=32)
        self.MEMSET(dve, LRv[:, 0, :], 1.0, LR.keys)
        self.MEMSET(dve, LIv[:, 0, :], 0.0, LI.keys)
        self.TS(dve, LRv[:, 1, :], wr.ap, 1.0, None, ALU.add, None, wr.keys, LR.keys)
        self.CP(dve, LIv[:, 1, :], wi.ap, wi.keys, LI.keys)
        for n in range(1, 8):
            cmul(V(LR, LRv[:, n + 1, :]), V(LI, LIv[:, n + 1, :]), V(LR, LRv[:, n, :]), V(LI, LIv[:, n, :]),
                 V(LR, LRv[:, 1, :]), V(LI, LIv[:, 1, :]))
        tt(t3, are, are, ALU.mult)
        tt(zr, aim, aim, ALU.mult)
        tt(t3, t3, zr, ALU.add)
        self.fw.op(dve, lambda e: e.reciprocal(out=t3.ap, in_=t3.ap), t3.keys, t3.keys)
        self.TS(dve, zi.ap, aim.ap, -1.0, None, ALU.mult, None, aim.keys, zi.keys)
        cmul(gr, gi, wr, wi, are, zi)
        tt(gr, gr, t3, ALU.mult)
        tt(gi, gi, t3, ALU.mult)
        grb = (gr.ap.unsqueeze(2).broadcast_to([64, 32, 16]), gr.keys)
        gib = (gi.ap.unsqueeze(2).broadcast_to([64, 32, 16]), gi.keys)
        v3 = lambda r: (r.ap.rearrange("p (g h) -> p g h", h=16), r.keys)
        tA3 = (tA.ap[:, 0:512].rearrange("p (g h) -> p g h", h=16), tA.keys)
        tB3 = (tB.ap[:, 0:512].rearrange("p (g h) -> p g h", h=16), tB.keys)
        tt(tA3, grb, v3(Bre), ALU.mult)
        tt(tB3, gib, v3(Bim), ALU.mult)
        tt(v3(BBr), tA3, tB3, ALU.subtract)
        tt(tA3, grb, v3(Bim), ALU.mult)
        tt(tB3, gib, v3(Bre), ALU.mult)
        tt(v3(BBi), tA3, tB3, ALU.add)
        l8r = (LRv[:, 8, :], LR.keys)
        l8i = (LIv[:, 8, :], LI.keys)
        tt(t1, l8r, l8r, ALU.mult)
        tt(t2, l8i, l8i, ALU.mult)
        tt(t1, t1, t2, ALU.add)
        self.fw.op(dve, lambda e: e.reciprocal(out=t1.ap, in_=t1.ap), t1.keys, t1.keys)
        tt(zr, l8r, t1, ALU.mult)
        tt(zi, l8i, t1, ALU.mult)
        self.CP(dve, self.A1[:, 0, :], LRv[:, 8, :], LR.keys, ['a12'])
        self.CP(dve, self.A1[:, 1, :], LRv[:, 8, :], LR.keys, ['a12'])
        self.TS(dve, self.A2[:, 0, :], LIv[:, 8, :], -1.0, None, ALU.mult, None, LI.keys, ['a12'])
        self.CP(dve, self.A2[:, 1, :], LIv[:, 8, :], LI.keys, ['a12'])
        dsrc = I['s5_d'].rearrange("(g h) -> h g", h=16)
        for s in range(8):
            self.DMA('sp', self.PV[s * 16:(s + 1) * 16, self.pv['dvec']:self.pv['dvec'] + 32], dsrc, ['in_w'], ['pv_dvec'], nonc=True)
        wout_v = W['s5wout'].rearrange("p (c g m) -> p c g m", c=2, g=32)
        win_v = W['s5win'].rearrange("p (g m) -> p g m", g=32)
        t0_v = W['s5t0'].rearrange("p (g m) -> p g m", g=32)
        S5M = self.HC[:, HC_S5M:HC_S5M + 128]
        S5E = self.HC[:, HC_S5E:HC_S5E + 128]
        sb64 = stgb.ap[0:64, :]
        for gc in range(32 // GC):
            g0 = gc * GC
            gs_ = slice(g0, g0 + GC)
            c3 = lambda r: (r.ap.rearrange("p (g h) -> p g h", h=16)[:, gs_, :], r.keys)
            tAc = (tA.ap[:, 0:GC * 16].rearrange("p (g h) -> p g h", h=16), tA.keys)
            tBc = (tB.ap[:, 0:GC * 16].rearrange("p (g h) -> p g h", h=16), tB.keys)
            WTr4 = WTr.ap.rearrange("p (g s h) -> p g s h", s=8, h=16)
            WTi4 = WTi.ap.rearrange("p (g s h) -> p g s h", s=8, h=16)
            for s in range(8):
                lr = (LRv[:, 7 - s, gs_].unsqueeze(2).broadcast_to([64, GC, 16]), LR.keys)
                li = (LIv[:, 7 - s, gs_].unsqueeze(2).broadcast_to([64, GC, 16]), LI.keys)
                tt(tAc, lr, c3(BBr), ALU.mult)
                tt(tBc, li, c3(BBi), ALU.mult)
                tt((WTr4[:, :, s, :], WTr.keys), tAc, tBc, ALU.subtract)
                tt(tAc, lr, c3(BBi), ALU.mult)
                tt(tBc, li, c3(BBr), ALU.mult)
                tt((WTi4[:, :, s, :], WTi.keys), tAc, tBc, ALU.add)
            C4 = lambda r: (r.ap.rearrange("p (g h) -> p g h", h=16)[:, gs_, :].unsqueeze(3).broadcast_to([64, GC, 16, 8]), r.keys)
            L4 = lambda r, rv: (rv[:, 1:9, gs_].rearrange("p n g -> p g n").unsqueeze(2).broadcast_to([64, GC, 16, 8]), r.keys)
            q4 = lambda r: (r.ap.rearrange("p (g h l) -> p g h l", h=16, l=8), r.keys)
            tt(q4(tA), C4(Cre), L4(LR, LRv), ALU.mult)
            tt(q4(tB), C4(Cim), L4(LI, LIv), ALU.mult)
            tt(q4(Qr), q4(tA), q4(tB), ALU.subtract)
            tt(q4(tA), C4(Cre), L4(LI, LIv), ALU.mult)
            tt(q4(tB), C4(Cim), L4(LR, LRv), ALU.mult)
            tt(q4(Qi), q4(tA), q4(tB), ALU.add)
            zrb = (zr.ap[:, gs_].unsqueeze(2).broadcast_to([64, GC, 128]), zr.keys)
            zib = (zi.ap[:, gs_].unsqueeze(2).broadcast_to([64, GC, 128]), zi.keys)
            g3 = lambda r: (r.ap.rearrange("p (g m) -> p g m", m=128), r.keys)
            tt(g3(tA), g3(Qr), zrb, ALU.mult)
            tt(g3(tB), g3(Qi), zib, ALU.mult)
            tt(g3(Qsr), g3(tA), g3(tB), ALU.add)
            tt(g3(tA), g3(Qr), zib, ALU.mult)
            tt(g3(tB), g3(Qi), zrb, ALU.mult)
            tt(g3(Qsi), g3(tA), g3(tB), ALU.subtract)
            for gg in range(GC):
                g = g0 + gg
                sl = slice(gg * 128, (gg + 1) * 128)
                self.CP('act', sb64[:, 0:128], Qr.ap[:, sl], Qr.keys, stgb.keys)
                self.ACT(sb64[:, 128:256], Qi.ap[:, sl], AF.Copy, Qi.keys, stgb.keys, scale=-1.0)
                self.DMA('sp', wout_v[:, 0, g, :], sb64[:, 0:128], stgb.keys, ['w_s5wout'])
                self.DMA('sp', wout_v[:, 1, g, :], sb64[:, 128:256], stgb.keys, ['w_s5wout'])
                pb, pk = self.bank()
                self.TR(pb[:, 0:64], WTr.ap[:, sl], self.IDf[0:64, 0:64], WTr.keys + ['hc'], [pk])
                self.TR(pb[:, 64:128], WTi.ap[:, sl], self.IDf[0:64, 0:64], WTi.keys + ['hc'], [pk])
                self.CP('act', stgb.ap[:, 0:128], pb[:, 0:128], [pk], stgb.keys)
                self.DMA('sp', win_v[:, g, :], stgb.ap[:, 0:128], stgb.keys, ['w_s5win'])
                pb2, pk2 = self.bank()
                self.MM(pb2[:, 0:128], WTr.ap[:, sl], Qsr.ap[:, sl], True, False, WTr.keys + Qsr.keys, [pk2])
                self.MM(pb2[:, 0:128], WTi.ap[:, sl], Qsi.ap[:, sl], False, True, WTi.keys + Qsi.keys, [pk2])
                self.TT('dve', tmpf.ap[:, 0:128], pb2[:, 0:128], S5M, ALU.mult, [pk2, 'hc'], tmpf.keys)
                self.STT(stgb.ap[:, 128:256], S5E, self.pvc('dvec', g, 1), tmpf.ap[:, 0:128], ALU.mult, ALU.add,
                         ['hc', 'pv_dvec'] + tmpf.keys, stgb.keys)
                self.DMA('sp', t0_v[:, g, :], stgb.ap[:, 128:256], stgb.keys, ['w_s5t0'])

    def l1_prologue(self):
        I, W = self.I, self.W
        BC = self.BC
        self.DMA('sp', BC[:, 0:8], I['ssd_dt_bias'].partition_broadcast(128), ['in_w'], ['bc'])
        self.DMA('sp', BC[:, 8:16], I['ssd_a_log'].partition_broadcast(128), ['in_w'], ['bc'])
        self.DMA('sp', BC[:, 16:24], I['ssd_d'].partition_broadcast(128), ['in_w'], ['bc'])
        self.DMA('sp', BC[:, 32:544], I['ssd_norm_g'].partition_broadcast(128), ['in_w'], ['bc'])
        self.ACT(BC[:, 8:16], BC[:, 8:16], AF.Exp, ['bc'], ['bc'])
        self.TS('dve', BC[:, 8:16], BC[:, 8:16], -1.0, None, ALU.mult, None, ['bc'], ['bc'])
        self.fm_load(self.pvc('scw', 0, 32), ['pv_scw'], I['ssd_conv_w'], 32, 128)
        self.fm_load(self.pvc('scb', 0, 8), ['pv_scb'], I['ssd_conv_b'], 8, 128)
        for half in range(2):
            self.DMA('sp', self.PV[half * 64:(half + 1) * 64, self.pv['qng']:self.pv['qng'] + 1],
                     I['swa_q_norm'].rearrange("(p o) -> p o", o=1), ['in_w'], ['pv_qng'], nonc=True)
            self.DMA('sp', self.PV[half * 64:(half + 1) * 64, self.pv['kng']:self.pv['kng'] + 1],
                     I['swa_k_norm'].rearrange("(p o) -> p o", o=1), ['in_w'], ['pv_kng'], nonc=True)
            self.DMA('sp', self.PV[half * 64:(half + 1) * 64, self.pv['sinkexp']:self.pv['sinkexp'] + 4],
                     I['swa_sink'][half * 4:(half + 1) * 4].partition_broadcast(64), ['in_w'], ['pv_sink'])
        self.TS('dve', self.pvc('qng'), self.pvc('qng'), 0.125, None, ALU.mult, None, ['pv_qng'], ['pv_qng'])
        self.ACT(self.pvc('sinkexp', 0, 4), self.pvc('sinkexp', 0, 4), AF.Exp, ['pv_sink'], ['pv_sink'])
        self.DMA('pool', self.WDT[:], I['od_w_in'][:, 2304:2312].rearrange("(k p) n -> p k n", p=128), ['in_w'], ['wdt'])
        self.DMA('sp', self.T5T[:], I['t5_bias'], ['in_w'], ['t5t'])
        pb, pk = self.bank()
        self.MM(pb[0:8, 0:384], self.T5T[:], self.HC[0:32, HC_OHW:HC_OHW + 384], True, True, ['t5t', 'hc'], [pk])
        self.CP('dve', self.WROW[:], pb[0:8, 0:384], [pk], ['wrow'])
        ps_ = self.WROW[:].ap[0][0]
        for h in range(8):
            for kb in range(3):
                srcap = bass.AP(self.WROW[:].tensor, h * ps_ + kb * 128, [[ps_, 1], [0, 64], [1, 128]])
                self.DMA('sp', W['biasd'][h:h + 1, kb, :].rearrange("o (r m) -> o r m", m=128), srcap, ['wrow'], ['w_biasd'])
        self.MEMSET('pool', self.BIE[:], 0.0, ['bie'])
        self.MEMSET('pool', self.BIO[:], 0.0, ['bio'])
        bt = W['biasd'].tensor

        def skew(h, kb):
            return bass.AP(bt, (h * 3 + kb) * 64 * 128, [[127, 64], [1, 64]])
        for kvh in range(2):
            for g in range(4):
                h = kvh * 4 + g
                for (T_, nm, p0, c0, kb) in [(self.BIE, 'bie', 0, 0, 0), (self.BIE, 'bie', 64, 0, 1), (self.BIE, 'bie', 0, 256, 2),
                                             (self.BIO, 'bio', 0, 0, 1), (self.BIO, 'bio', 64, 0, 2), (self.BIO, 'bio', 64, 256, 0)]:
                    self.DMA('sp', T_[p0:p0 + 64, kvh, c0 + g * 64:c0 + (g + 1) * 64], skew(h, kb), ['w_biasd'], [nm])

    def ws_load(self, views_and_srcs, rkeys):
        i = self.ws_rr
        self.ws_rr = (self.ws_rr + 1) % 4
        t = self.WS[i]
        for mk, src in views_and_srcs:
            self.DMA('sp', mk(t), src, rkeys, [f'ws{i}'])
        return t, f'ws{i}'

    def run_job(self, job):
        SEQ = self.SEQ
        if job < 2:
            T = 512
            ntile = SEQ // T
        else:
            T = 64
            ntile = 1
        self.job = job
        self.sub = int(os.environ.get('KSUB', '99'))
        self.fine = int(os.environ.get('KFINE', '99'))
        self.job_init(job)
        for ti in range(ntile):
            last = (ti == ntile - 1)
            self.tile_load(job, ti, T)
            if self.stage >= 7:
                self.layer0(job, ti, T, last)
            if self.stage >= 8:
                self.ffn(job, 0, T, last)
            if self.do_l1 and self.stage >= 9:
                self.layer1(job, ti, T, last)
            if self.do_l1 and self.stage >= 10:
                self.ffn(job, 1, T, last)
            self.tile_store(job, ti, T)
        self.job_finish(job)

    def job_init(self, job):
        I = self.I
        if job < 2:
            self.MEMSET('pool', self.S3[:], 0.0, ['s3'])
            self.MEMSET('pool', self.GS[:], 0.0, ['gs'])
            self.MEMSET('pool', self.FT[:], 0.0, ['ft'])
        else:
            self.fm_load(self.S3[:, 0, :], ['s3'], I['st_s5re'], 32, 64)
            self.fm_load(self.S3[:, 1, :], ['s3'], I['st_s5im'], 32, 64)
            self.CP('dve', self.S3[:, 2, :], self.S3[:, 0, :], ['s3'], ['s3'])
            self.DMA('sp', self.GS[:], I['st_gla'].rearrange("(hp q) v -> q hp v", q=128), ['in_w'], ['gs'])
            for l in range(2):
                tmp = self.scr(0, 88 * 4, F32)
                self.fm_load(tmp.ap[:, 0:88], tmp.keys, I['st_ffnconv'][l], 88, 128)
                self.CP('dve', self.FT[:, l, :, :], tmp.ap[:, 0:88].rearrange("p (r c) -> p c r", r=2), tmp.keys, ['ft'])
        if self.do_l1:
            self.l1_job_init(job)

    def l1_job_init(self, job):
        I = self.I
        self.MEMSET('pool', self.KWIN[:], 0.0, ['kwin'])
        self.MEMSET('pool', self.VWP[:], 0.0, ['vwp'])
        if job < 2:
            self.MEMSET('pool', self.ST[:], 0.0, ['st'])
            self.MEMSET('pool', self.CT[:], 0.0, ['ct'])
        else:
            for q4 in range(4):
                self.fm_load(self.ST[:, q4 * 128:(q4 + 1) * 128], ['st'], I['st_ssd'][q4 * 128:(q4 + 1) * 128, :], 128, 128)
            tmp = self.scr(0, 24 * 4, F32)
            self.fm_load(tmp.ap[:, 0:24], tmp.keys, I['st_ssdconv'], 24, 128)
            self.CP('dve', self.CT[:, :, 1:4], tmp.ap[:, 0:24].rearrange("p (r c) -> p c r", r=3), tmp.keys, ['ct'])
            tk = self.scr(2048, 128 * 4, F32)
            self.fm_load(tk.ap[:, 0:128], tk.keys, I['ck'], 128, 128)
            self.CP('dve', self.KWIN[:, 0:128], tk.ap[:, 0:128], tk.keys, ['kwin'])
            self.DMA('pool', self.VWP[:, 0, :], I['cv'], ['in_w', 'vwp'], ['vwp'])

    def job_finish(self, job):
        O = self.O
        self.tm_store(O['o_s5re'][job], self.S3[:, 0, :], ['s3'], 32, 64)
        self.tm_store(O['o_s5im'][job], self.S3[:, 1, :], ['s3'], 32, 64)
        self.DMA('pool', O['o_gla'][job].rearrange("(hp q) v -> q hp v", q=128), self.GS[:], ['gs'], ['out'])
        if self.do_l1:
            self.l1_job_finish(job)

    def l1_job_finish(self, job):
        O = self.O
        for q4 in range(4):
            self.tm_store(O['o_ssd'][job, q4 * 128:(q4 + 1) * 128, :], self.ST[:, q4 * 128:(q4 + 1) * 128], ['st'], 128, 128)

    def tile_load(self, job, ti, T):
        I = self.I
        xt = self.scr(0, 4 * 1024 * 4, F32)
        xtv = xt.ap.rearrange("p (b f) -> p b f", b=4)
        nb = max(1, T // 128)
        rows = min(T, 128)
        for b in range(nb):
            if job < 2:
                src = I['xp'][job, ti * T + b * 128: ti * T + b * 128 + rows, :]
            else:
                src = I['xs'][0:64, :]
            self.DMA('sp', xtv[0:rows, b, :], src, ['in_x'], xt.keys)
        for k in range(8):
            pb, pk = self.bank()
            for b in range(nb):
                self.TR(pb[:, b * 128:b * 128 + rows], xtv[0:rows, b, k * 128:(k + 1) * 128], self.IDf[0:rows, 0:rows],
                        xt.keys + ['hc'], [pk])
            self.CP('dve' if k % 2 == 0 else 'act', self.XRES[:, k, 0:T], pb[:, 0:T], [pk], [('xres', k)])

    def tile_store(self, job, ti, T):
        O = self.O
        xt = self.scr(0, 4 * 1024 * 4, F32)
        xtv = xt.ap.rearrange("p (b f) -> p b f", b=4)
        nb = max(1, T // 128)
        rows = min(T, 128)
        for b in range(nb):
            for k2 in range(2):
                pb, pk = self.bank()
                for kk in range(4):
                    k = k2 * 4 + kk
                    self.TR(pb[0:rows, kk * 128:(kk + 1) * 128], self.XRES[:, k, b * 128:b * 128 + rows], self.IDf,
                            [('xres', k), 'hc'], [pk])
                self.CP('dve' if k2 == 0 else 'act', xtv[0:rows, b, k2 * 512:(k2 + 1) * 512], pb[0:rows, :], [pk], xt.keys)
            if job < 2:
                dst = O['yp'][job, ti * T + b * 128: ti * T + b * 128 + rows, :]
            else:
                dst = O['ys'][0:64, :]
            self.DMA('pool', dst, xtv[0:rows, b, :], xt.keys, ['out'])

    def norm(self, job, l, which, T, want_lo=False):
        xres = self.XRES
        sq = self.scr(0, 8 * 512 * 2, BF16)
        sqv = sq.ap.rearrange("p (k t) -> p k t", k=8)
        rs = self.scr(8192, 2048, F32)
        tmpn = [self.scr(10240 + i * 2048, 2048, F32) for i in range(2)]
        for k in range(8):
            self.ACT(sqv[:, k, 0:T], xres[:, k, 0:T], AF.Square, [('xres', k)], sq.keys)
        pb, pk = self.bank()
        for k in range(8):
            self.MM(pb[:, 0:T], self.ONESb[:], sqv[:, k, 0:T], k == 0, k == 7, sq.keys + ['onesb'], [pk])
        self.ACT(rs.ap[:, 0:T], pb[:, 0:T], AF.Sqrt, [pk, 'pv_eps'], rs.keys, bias=self.pvc('eps'), scale=1.0 / D)
        self.fw.op('dve', lambda e: e.reciprocal(out=rs.ap[:, 0:T], in_=rs.ap[:, 0:T]), rs.keys, rs.keys)
        gsn = 'gs1' if which == 0 else 'gs2'
        for k in range(8):
            tm = tmpn[k % 2]
            gs = self.pvc(gsn, (job * 2 + l) * 8 + k, 1)
            sh = self.modv(job, l, 0 if which == 0 else 3, k, 1)
            self.STT(tm.ap[:, 0:T], xres[:, k, 0:T], gs, rs.ap[:, 0:T], ALU.mult, ALU.mult,
                     [('xres', k), 'pv_gs'] + rs.keys, tm.keys)
            self.ACT(self.ACT8[:, k, 0:T], tm.ap[:, 0:T], AF.Identity, tm.keys + ['pv_mod'], [('act8', k)], bias=sh, scale=1.0)
            if want_lo:
                h32 = self.scr(14336 + (k % 2) * 2048, 2048, F32)
                self.TS('pool', h32.ap[:, 0:T], tm.ap[:, 0:T], sh, None, ALU.add, None, tm.keys + ['pv_mod'], h32.keys)
                self.TT('pool', self.HNLO[:, k, 0:T], h32.ap[:, 0:T], self.ACT8[:, k, 0:T], ALU.subtract,
                        h32.keys + [('act8', k)], [('hnlo', k)])

    def resid(self, job, l, which, m, pb, pk, T):
        g = self.modv(job, l, 2 if which == 0 else 5, m, 1)
        self.STT(self.XRES[:, m, 0:T], pb[:, 0:T], g, self.XRES[:, m, 0:T], ALU.mult, ALU.add,
                 [pk, 'pv_mod', ('xres', m)], [('xres', m)])

    def layer0(self, job, ti, T, last):
        W = self.W
        NC = T // 64
        U = min(T, 128)
        NU = T // U
        J = T // 8
        hn = self.ACT8
        hnk = [('act8', k) for k in range(8)]
        self.norm(job, 0, 0, T, want_lo=True)
        o = [0]

        def A(nbytes, dt, parts=128):
            r = self.scr(o[0], nbytes, dt, parts=parts)
            o[0] += (nbytes + 63) // 64 * 64
            return r
        RSIL = A(4 * 512 * 2, BF16)
        VTOK = A(4 * 512 * 2, BF16)
        ZT2 = A(32 * 128 * 2, BF16)
        U2 = A(32 * 64 * 2, BF16)
        gla_start = o[0]
        GL = A(512 * 2, BF16)
        LL = A(2 * 512 * 4, F32)
        CUM = A(2 * 512 * 4, F32)
        E1 = A(2 * 512 * 4, F32)
        E2 = A(2 * 512 * 4, F32)
        KEF = A(2 * 512 * 4, F32)
        QE = A(2 * 512 * 2, BF16)
        QE32 = A(2 * 512 * 4, F32)
        KE = A(2 * 512 * 2, BF16)
        KD = A(2 * 512 * 2, BF16)
        KDT = A(4 * 2 * 128 * 2, BF16)
        ATT = [A(4 * 128 * 2, BF16) for _ in range(2)]
        OSB = [A(4 * 128 * 4, F32) for _ in range(2)]
        OSQ = [A(4 * 128 * 2, BF16) for _ in range(2)]
        ORS = [A(4 * 128 * 4, F32) for _ in range(2)]
        OT = [A(4 * 128 * 4, F32) for _ in range(2)]
        assert o[0] <= self.SCRN * 4, o[0]
        gla_end = o[0]
        rsv = RSIL.ap.rearrange("p (c t) -> p c t", c=4)
        vtv = VTOK.ap.rearrange("p (u f) -> p u f", u=4)
        zt4 = ZT2.ap.rearrange("p (g s h) -> p g s h", g=32, s=8)
        v2 = lambda r: r.ap.rearrange("p (c t) -> p c t", c=2)
        if self.sub < 1:
            return
        evin = W['evin']
        wt, wk = self.ws_load([(lambda t: t[:].rearrange("p (k n) -> p k n", k=8),
                                evin[:, 512:1024].rearrange("(k p) n -> p k n", p=128))], ['w_evin'])
        wv = wt[:].rearrange("p (k n) -> p k n", k=8)
        wtl, wkl = self.ws_load([(lambda t: t[:].rearrange("p (k n) -> p k n", k=8),
                                  W['evin_lo'].rearrange("(k p) n -> p k n", p=128))], ['w_evin_lo'])
        wvl = wtl[:].rearrange("p (k n) -> p k n", k=8)
        qkb = []
        for mc in range(4):
            pb, pk = self.bank()
            for k in range(8):
                self.MM(pb[:, 0:T], wv[:, k, mc * 128:(mc + 1) * 128], hn[:, k, 0:T], k == 0, False, [wk, hnk[k]], [pk])
            for k in range(8):
                self.MM(pb[:, 0:T], wvl[:, k, mc * 128:(mc + 1) * 128], hn[:, k, 0:T], False, False, [wkl, hnk[k]], [pk])
            for k in range(8):
                self.MM(pb[:, 0:T], wv[:, k, mc * 128:(mc + 1) * 128], self.HNLO[:, k, 0:T], False, k == 7, [wk, ('hnlo', k)], [pk])
            qkb.append((pb, pk))
        if self.fine < 1:
            return
        pbg, pkg = self.bank()
        for k in range(8):
            self.MM(pbg[0:16, 0:T], self.WGL[:, k, :], hn[:, k, 0:T], k == 0, k == 7, ['wgl', hnk[k]], [pkg])
        self.CP('act', GL.ap[0:16, 0:T], pbg[0:16, 0:T], [pkg], GL.keys)
        if self.fine < 2:
            return
        pbG, pkG = self.bank(), None
        pbG, pkG = pbG
        gate_banks = []
        for c2 in range(2):
            if T == 512:
                pbx, pkx = (pbG, pkG) if c2 == 0 else self.bank()
            else:
                pbx, pkx = (pbG, pkG)
            col0 = 0 if T == 512 else c2 * 64
            self.MM(pbx[:, col0:col0 + T], self.WG2[:, c2 * 128:(c2 + 1) * 128], GL.ap[0:16, 0:T], True, True,
                    ['wg2'] + GL.keys, [pkx])
            gate_banks.append((pbx, pkx, col0))
        if self.fine < 3:
            return
        llv, cumv, e1v, e2v, kefv = v2(LL), v2(CUM), v2(E1), v2(E2), v2(KEF)
        qev, kev, kdv = v2(QE), v2(KE), v2(KD)
        qe32v = v2(QE32)
        for c2 in range(2):
            pbx, pkx, col0 = gate_banks[c2]
            self.ACT(llv[:, c2, 0:T], pbx[:, col0:col0 + T], AF.Exp, [pkx, 'pv_negbg'], LL.keys,
                     bias=self.pvc('negbg', c2, 1), scale=-1.0)
        for c2 in range(2):
            self.ACT(llv[:, c2, 0:T], llv[:, c2, 0:T], AF.Ln, LL.keys + ['pv_one'], LL.keys, bias=self.pvc('one'), scale=1.0)
        if self.fine < 4:
            return
        for c2 in range(2):
            self.fw.op('dve', (lambda c2: lambda e: e.tensor_tensor_scan(
                out=cumv[:, c2, 0:T], data0=self.HC[:, HC_RST:HC_RST + T], data1=llv[:, c2, 0:T], initial=0.0,
                op0=ALU.mult, op1=ALU.add))(c2), LL.keys + ['hc'], CUM.keys)
        if self.fine < 5:
            return
        self.ACT(e1v[:, :, 0:T], cumv[:, :, 0:T], AF.Exp, CUM.keys, E1.keys, scale=-1.0 / 16.0)
        self.ACT(e2v[:, :, 0:T], cumv[:, :, 0:T], AF.Exp, CUM.keys, E2.keys, scale=1.0 / 16.0)
        if self.fine < 6:
            return
        for c2 in range(2):
            pbq, pkq = qkb[c2]
            self.STT(qe32v[:, c2, 0:T], pbq[:, 0:T], 0.125, e1v[:, c2, 0:T], ALU.mult, ALU.mult, [pkq] + E1.keys, QE32.keys)
            self.CP('pool', qev[:, c2, 0:T], qe32v[:, c2, 0:T], QE32.keys, QE.keys)
            pbk, pkk = qkb[2 + c2]
            self.TT('dve', kefv[:, c2, 0:T], pbk[:, 0:T], e2v[:, c2, 0:T], ALU.mult, [pkk] + E2.keys, KEF.keys)
        self.CP('act', kev[:, :, 0:T], kefv[:, :, 0:T], KEF.keys, KE.keys)
        if self.fine < 7:
            return
        e1last = E1.ap.rearrange("p (c n l) -> p c n l", c=2, l=64)[:, :, 0:NC, 63:64].broadcast_to([128, 2, NC, 64])
        self.TT('pool', KD.ap.rearrange("p (c n l) -> p c n l", c=2, l=64)[:, :, 0:NC, :],
                KEF.ap.rearrange("p (c n l) -> p c n l", c=2, l=64)[:, :, 0:NC, :], e1last, ALU.mult,
                KEF.keys + E1.keys, KD.keys)
        if self.fine < 8:
            return
        wt, wk = self.ws_load([(lambda t: t[:].rearrange("p (k n) -> p k n", k=8),
                                evin[:, 1552:2064].rearrange("(k p) n -> p k n", p=128))], ['w_evin'])
        wv = wt[:].rearrange("p (k n) -> p k n", k=8)
        for mc in range(4):
            pb, pk = self.bank()
            for k in range(8):
                self.MM(pb[:, 0:T], wv[:, k, mc * 128:(mc + 1) * 128], hn[:, k, 0:T], k == 0, k == 7, [wk, hnk[k]], [pk])
            self.ACT(rsv[:, mc, 0:T], pb[:, 0:T], AF.Silu, [pk], RSIL.keys)
        if self.fine < 9:
            return
        if os.environ.get('KSLOT'):
            self.ws_rr = int(os.environ['KSLOT'])
        kvar = int(os.environ.get('KVAR', '0'))
        if kvar != 5:
            wt, wk = self.ws_load([(lambda t: t[:].rearrange("p (k n) -> p k n", k=8),
                                    evin[:, 1024:1536].rearrange("(k p) n -> p k n", p=128))], ['w_evin'])
            wv = wt[:].rearrange("p (k n) -> p k n", k=8)
        for u in range(NU):
            if kvar == 6:
                continue
            pb, pk = self.bank()
            for k in range(8):
                if kvar not in (2, 3, 5):
                    self.MM(pb[0:U, :], hn[:, k, u * U:(u + 1) * U], wv[:, k, :], k == 0, k == 7, [wk, hnk[k]], [pk])
            if kvar not in (1, 3, 5):
                self.CP('act' if u % 2 else 'dve', vtv[0:U, u, :], pb[0:U, :], [pk], VTOK.keys)
        if self.fine < 10:
            return
        wt, wk = self.ws_load([(lambda t: t[:].rearrange("p (k n) -> p k n", k=8),
                                evin[:, 0:512].rearrange("(k p) n -> p k n", p=128))], ['w_evin'])
        wv = wt[:].rearrange("p (k n) -> p k n", k=8)
        for s in range(8):
            pb, pk = self.bank()
            for k in range(8):
                self.MM(pb[0:J, :], hn[:, k, s:T:8], wv[:, k, :], k == 0, k == 7, [wk, hnk[k]], [pk])
            self.CP('act' if s % 2 else 'dve', zt4[0:J, :, s, :], pb[0:J, :].rearrange("p (g h) -> p g h", h=16), [pk], ZT2.keys)
        if self.sub < 2:
            return
        kdt4 = KDT.ap.rearrange("p (u c f) -> p u c f", u=4, c=2)
        if U < 128:
            self.MEMSET('pool', vtv[64:128, 0, :], 0.0, VTOK.keys)
            for s2_ in range(2):
                self.MEMSET('pool', ATT[s2_].ap[64:128, :], 0.0, ATT[s2_].keys)
        for u in range(NU):
            pb, pk = self.bank()
            pbb = pb[:, 0:128].bitcast(BF16)
            for c2 in range(2):
                self.TR(pbb[0:U, c2 * 128:(c2 + 1) * 128], kdv[:, c2, u * U:(u + 1) * U], self.IDb[:], KD.keys + ['idb'], [pk])
            self.CP('act' if u % 2 else 'dve', kdt4[0:U, u, :, :], pbb[0:U, 0:256].rearrange("p (c f) -> p c f", c=2), [pk], KDT.keys)
        if self.fine < 21:
            return
        self.CP('act', self.GSH[:, 0, :, :], self.GS[:], ['gs'], [('gsh', 0)])
        e1l = E1.ap.rearrange("p (c n l) -> p c n l", c=2, l=64)
        for c in range(NC):
            u, cu = divmod(c, U // 64)
            pb, pk = self.bank()
            p0 = cu * 64
            for h in range(4):
                hp, hb = divmod(h, 2)
                self.MM(pb[hb * 64:(hb + 1) * 64, hp * 128:(hp + 1) * 128],
                        kdt4[p0:p0 + 64, u, hp, hb * 64:(hb + 1) * 64], vtv[p0:p0 + 64, u, h * 128:(h + 1) * 128],
                        True, True, KDT.keys + VTOK.keys, [pk])
            for hp in range(2):
                self.STT(self.GS[:, hp, :], self.GS[:, hp, :], e1l[:, hp, c, 63:64], pb[:, hp * 128:(hp + 1) * 128],
                         ALU.mult, ALU.add, ['gs', pk] + E1.keys, ['gs'])
            if c + 1 < NC:
                self.CP('act', self.GSH[:, c + 1, :, :], self.GS[:], ['gs'], [('gsh', c + 1)])
        if self.fine < 22:
            return
        MCm = self.HC[0:U, HC_MC:HC_MC + U]
        mixed = self.ACT8
        for u in range(NU):
            s2 = u % 2
            cols = slice(u * U, (u + 1) * U)
            attv = ATT[s2].ap.rearrange("p (h l) -> p h l", h=4)
            sbk = [self.bank(), self.bank()]
            for h in range(4):
                hp, hb = divmod(h, 2)
                pbs, pks = sbk[hb]
                self.MM(pbs[0:U, hp * U:(hp + 1) * U], kefv[hb * 64:(hb + 1) * 64, hp, cols], qe32v[hb * 64:(hb + 1) * 64, hp, cols],
                        True, True, KEF.keys + QE32.keys, [pks])
            for hb in range(2):
                pbs, pks = sbk[hb]
                self.TT('dve', attv[0:U, hb:4:2, 0:U], pbs[0:U, 0:2 * U].rearrange("p (h l) -> p h l", h=2),
                        MCm.unsqueeze(1).broadcast_to([U, 2, U]), ALU.mult, [pks, 'hc'], ATT[s2].keys)
            obk = [self.bank(), self.bank()]
            for h in range(4):
                hp, hb = divmod(h, 2)
                pbo, pko = obk[hb]
                self.MM(pbo[:, hp * U:(hp + 1) * U], vtv[0:128, u, h * 128:(h + 1) * 128], attv[0:128, h, 0:U], True, False,
                        VTOK.keys + ATT[s2].keys, [pko])
                for cu in range(U // 64):
                    c = u * (U // 64) + cu
                    self.MM(pbo[:, hp * U + cu * 64:hp * U + (cu + 1) * 64], self.GSH[hb * 64:(hb + 1) * 64, c, hp, :],
                            qev[hb * 64:(hb + 1) * 64, hp, c * 64:(c + 1) * 64], False, cu == U // 64 - 1,
                            [('gsh', c)] + QE.keys, [pko])
            if self.fine < 23:
                continue
            osb, osq, ors, ot = OSB[s2], OSQ[s2], ORS[s2], OT[s2]
            n4 = 4 * U
            osb3 = osb.ap[:, 0:n4].rearrange("p (h l) -> p h l", h=4)
            osq3 = osq.ap[:, 0:n4].rearrange("p (h l) -> p h l", h=4)
            for hb in range(2):
                pbo, pko = obk[hb]
                src3 = pbo[:, 0:2 * U].rearrange("p (h l) -> p h l", h=2)
                self.CP('act', osb3[:, hb:4:2, :], src3, [pko], osb.keys)
                self.ACT(osq3[:, hb:4:2, :], src3, AF.Square, [pko], osq.keys)
            pbn, pkn = self.bank()
            self.MM(pbn[:, 0:n4], self.ONESb[:], osq.ap[:, 0:n4], True, True, osq.keys + ['onesb'], [pkn])
            self.ACT(ors.ap[:, 0:n4], pbn[:, 0:n4], AF.Sqrt, [pkn, 'pv_eps'], ors.keys, bias=self.pvc('eps'), scale=1.0 / 128)
            self.fw.op('dve', (lambda ors=ors, n4=n4: lambda e: e.reciprocal(out=ors.ap[:, 0:n4], in_=ors.ap[:, 0:n4]))(),
                       ors.keys, ors.keys)
            self.STT(ot.ap[:, 0:n4], osb.ap[:, 0:n4], self.pvc('glang'), ors.ap[:, 0:n4], ALU.mult, ALU.mult,
                     osb.keys + ors.keys + ['pv_glang'], ot.keys)
            self.TT('pool', mixed[:, 4:8, cols], ot.ap[:, 0:n4].rearrange("p (h l) -> p h l", h=4), rsv[:, :, cols], ALU.mult,
                    ot.keys + RSIL.keys, [('act8', 4 + h) for h in range(4)])
        if self.sub < 3:
            return
        o[0] = gla_start
        VSB = A(2 * 32 * 64 * 4, F32, parts=64)
        XBF = A(2 * 32 * 64 * 2, BF16, parts=64)
        Y2 = A(32 * 64 * 2, BF16)
        YTOK = A(512 * 8 * 2, BF16, parts=64)
        YFM = A(4 * 512 * 2, BF16)
        SGT = [A(512 * 4, F32) for _ in range(2)]
        TM1 = A(64 * 4, F32, parts=64)
        TM2 = A(64 * 4, F32, parts=64)
        assert o[0] <= self.SCRN * 4, o[0]
        u2v = U2.ap.rearrange("p (g j) -> p g j", g=32)
        for half in range(2):
            pb, pk = self.bank()
            pbb = pb[:].bitcast(BF16)
            for gg in range(16):
                g = half * 16 + gg
                self.TR(pbb[:, gg * J:(gg + 1) * J], ZT2.ap[0:J, g * 128:(g + 1) * 128], self.IDb[0:J, 0:J],
                        ZT2.keys + ['idb'], [pk])
            self.CP('act' if half else 'dve', u2v[:, half * 16:(half + 1) * 16, 0:J],
                    pbb[:, 0:16 * J].rearrange("p (g j) -> p g j", g=16), [pk], U2.keys)
        wt_in, wk_in = self.ws_load([(lambda t: t[:], W['s5win'])], ['w_s5win'])
        winv = wt_in[:].rearrange("p (g m) -> p g m", g=32)
        vsb4 = VSB.ap.rearrange("p (c g j) -> p c g j", c=2, g=32)
        xbf4 = XBF.ap.rearrange("p (c g j) -> p c g j", c=2, g=32)
        gpbv = min(32, 512 // J)
        for c in range(2):
            for bi_, g0 in enumerate(range(0, 32, gpbv)):
                pb, pk = self.bank()
                for gg in range(gpbv):
                    g = g0 + gg
                    self.MM(pb[0:64, gg * J:(gg + 1) * J], winv[:, g, c * 64:(c + 1) * 64], u2v[:, g, 0:J], True, True,
                            [wk_in] + U2.keys, [pk])
                self.CP('act' if bi_ % 2 else 'dve', vsb4[:, c, g0:g0 + gpbv, 0:J],
                        pb[0:64, 0:gpbv * J].rearrange("p (g j) -> p g j", g=gpbv), [pk], VSB.keys)
        S3 = self.S3
        tm1 = TM1.ap.rearrange("p (c g) -> p c g", c=2)
        tm2 = TM2.ap.rearrange("p (c g) -> p c g", c=2)
        eng = 'pool'
        for j in range(J):
            self.CP(eng, xbf4[:, :, :, j], S3[:, 0:2, :], ['s3'], XBF.keys)
            self.TT(eng, tm1, self.A1[:], S3[:, 0:2, :], ALU.mult, ['a12', 's3'], TM1.keys)
            self.TT(eng, tm2, self.A2[:], S3[:, 1:3, :], ALU.mult, ['a12', 's3'], TM2.keys)
            self.TT(eng, tm1, tm1, tm2, ALU.add, TM1.keys + TM2.keys, TM1.keys)
            self.TT(eng, S3[:, 0:2, :], tm1, vsb4[:, :, :, j], ALU.add, TM1.keys + VSB.keys, ['s3'])
            self.CP(eng, S3[:, 2, :], S3[:, 0, :], ['s3'], ['s3'])
        wt_t0, wk_t0 = self.ws_load([(lambda t: t[:], W['s5t0'])], ['w_s5t0'])
        t0v = wt_t0[:].rearrange("p (g m) -> p g m", g=32)
        wo = []
        for c in range(2):
            wt_o, wk_o = self.ws_load([(lambda t: t[0:64, :], W['s5wout'][:, c * 4096:(c + 1) * 4096])], ['w_s5wout'])
            wo.append((wt_o[0:64, :].rearrange("p (g m) -> p g m", g=32), wk_o))
        y2v = Y2.ap.rearrange("p (g j) -> p g j", g=32)
        gpb = 512 // J if J >= 16 else 32
        for g0 in range(0, 32, gpb):
            pb, pk = self.bank()
            ng = min(gpb, 32 - g0)
            for gg in range(ng):
                g = g0 + gg
                self.MM(pb[:, gg * J:(gg + 1) * J], t0v[:, g, :], u2v[:, g, 0:J], True, False, [wk_t0] + U2.keys, [pk])
                self.MM(pb[:, gg * J:(gg + 1) * J], wo[0][0][:, g, :], xbf4[:, 0, g, 0:J], False, False, [wo[0][1]] + XBF.keys, [pk])
                self.MM(pb[:, gg * J:(gg + 1) * J], wo[1][0][:, g, :], xbf4[:, 1, g, 0:J], False, True, [wo[1][1]] + XBF.keys, [pk])
            self.ACT(y2v[:, g0:g0 + ng, 0:J], pb[:, 0:ng * J].rearrange("p (g j) -> p g j", g=ng), AF.Gelu, [pk], Y2.keys)
        ytv = YTOK.ap.rearrange("p (g m) -> p g m", g=32)
        for q4 in range(4):
            pb, pk = self.bank()
            pbb = pb[:].bitcast(BF16)
            for gg in range(8):
                g = q4 * 8 + gg
                self.TR(pbb[0:J, gg * 128:(gg + 1) * 128], y2v[:, g, 0:J], self.IDb[:], Y2.keys + ['idb'], [pk])
            self.CP('act' if q4 % 2 else 'dve', ytv[0:J, q4 * 8:(q4 + 1) * 8, :],
                    pbb[0:J, 0:1024].rearrange("p (g m) -> p g m", g=8), [pk], YTOK.keys)
        yt3 = YTOK.ap.rearrange("p (c l) -> p c l", l=8)
        yfv = YFM.ap.rearrange("p (b t) -> p b t", b=4)
        for cb in range(4):
            pb, pk = self.bank()
            pbb = pb[:].bitcast(BF16)
            for l in range(8):
                self.TR(pbb[:, l * J:(l + 1) * J], yt3[0:J, cb * 128:(cb + 1) * 128, l], self.IDb[0:J, 0:J],
                        YTOK.keys + ['idb'], [pk])
            self.CP('act' if cb % 2 else 'dve', yfv[:, cb, 0:T].rearrange("p (j l) -> p l j", l=8),
                    pbb[:, 0:8 * J].rearrange("p (l j) -> p l j", l=8), [pk], YFM.keys)
        for m in range(4):
            pb, pk = self.bank()
            for k in range(4):
                self.MM(pb[:, 0:T], self.WGLU[:, k, m * 128:(m + 1) * 128], yfv[:, k, 0:T], k == 0, k == 3,
                        ['wglu'] + YFM.keys, [pk])
            sg = SGT[m % 2]
            self.ACT(sg.ap[:, 0:T], pb[:, 0:T], AF.Sigmoid, [pk, 'pv_bglu'], sg.keys, bias=self.pvc('bglu', m, 1), scale=1.0)
            self.TT('dve', mixed[:, m, 0:T], yfv[:, m, 0:T], sg.ap[:, 0:T], ALU.mult, YFM.keys + sg.keys, [('act8', m)])
        if self.sub < 4:
            return
        self.out_proj(job, 0, T, W['evout'], None)

    def out_proj(self, job, l, T, wsrc, rowperm):
        mixed = self.ACT8
        for half in range(2):
            cs_ = slice(half * 512, (half + 1) * 512)
            if rowperm:
                pieces = []
                for g in range(4):
                    pieces.append(((lambda g: lambda t: t[0:64, g * 512:(g + 1) * 512])(g), wsrc[g * 64:(g + 1) * 64, cs_]))
                    pieces.append(((lambda g: lambda t: t[64:128, g * 512:(g + 1) * 512])(g), wsrc[256 + g * 64:256 + (g + 1) * 64, cs_]))
                pieces.append((lambda t: t[:, 2048:4096].rearrange("p (k n) -> p k n", k=4),
                               wsrc[512:1024, cs_].rearrange("(k p) n -> p k n", p=128)))
                wt, wk = self.ws_load(pieces, ['w_out'])
            else:
                wt, wk = self.ws_load([(lambda t: t[:].rearrange("p (k n) -> p k n", k=8),
                                        wsrc[:, cs_].rearrange("(k p) n -> p k n", p=128))], ['w_out'])
            wv = wt[:].rearrange("p (k n) -> p k n", k=8)
            for mc in range(4):
                m = half * 4 + mc
                pb, pk = self.bank()
                for k in range(8):
                    self.MM(pb[:, 0:T], wv[:, k, mc * 128:(mc + 1) * 128], mixed[:, k, 0:T], k == 0, k == 7,
                            [wk, ('act8', k)], [pk])
                self.resid(job, l, 0, m, pb, pk, T)

    def ffn(self, job, l, T, last):
        W, O = self.W, self.O
        hn = self.ACT8
        hnk = [('act8', k) for k in range(8)]
        self.norm(job, l, 1, T)
        o = [14336]

        def A(nbytes, dt, parts=128):
            r = self.scr(o[0], nbytes, dt, parts=parts)
            o[0] += (nbytes + 63) // 64 * 64
            return r
        HB = A(22 * 512 * 2, BF16)
        ACC = [[A(512 * 4, F32) for _ in range(2)] for _ in range(2)]
        SA = [A(512 * 4, F32) for _ in range(2)]
        UT = [A(512 * 4, F32, parts=2) for _ in range(2)]
        assert o[0] <= self.SCRN * 4
        hbv = HB.ap.rearrange("p (j t) -> p j t", j=22)
        FT = self.FT
        fcw = lambda j: self.PV[:, self.pv['fcw'] + l * 132 + j * 44: self.pv['fcw'] + l * 132 + (j + 1) * 44]
        self.TT('dve', self.FCORR[:, :, 0], FT[:, l, :, 1], fcw(1), ALU.mult, ['ft', 'pv_fcw'], ['fcorr'])
        self.TT('dve', self.FTMP[:, :, 0], FT[:, l, :, 0], fcw(0), ALU.mult, ['ft', 'pv_fcw'], ['ftmp'])
        self.TT('dve', self.FCORR[:, :, 0], self.FCORR[:, :, 0], self.FTMP[:, :, 0], ALU.add, ['fcorr', 'ftmp'], ['fcorr'])
        self.TT('dve', self.FCORR[:, :, 1], FT[:, l, :, 1], fcw(0), ALU.mult, ['ft', 'pv_fcw'], ['fcorr'])
        up = W['up'][l]
        for pa in range(11):
            wt, wk = self.ws_load([
                (lambda t: t[:].rearrange("p (k n) -> p k n", k=8)[:, :, 0:256],
                 up[:, pa * 256:(pa + 1) * 256].rearrange("(k p) n -> p k n", p=128)),
                (lambda t: t[:].rearrange("p (k n) -> p k n", k=8)[:, :, 256:512],
                 up[:, DFF + pa * 256:DFF + (pa + 1) * 256].rearrange("(k p) n -> p k n", p=128))], ['w_up'])
            wv = wt[:].rearrange("p (k n) -> p k n", k=8)
            for bi in range(2):
                j = pa * 2 + bi
                slot = j % 2
                accs = []
                for ag in range(2):
                    blk = ag * 22 + j
                    pb, pk = self.bank()
                    c0 = ag * 256 + bi * 128
                    for k in range(8):
                        self.MM(pb[:, 0:T], wv[:, k, c0:c0 + 128], hn[:, k, 0:T], k == 0, k == 7, [wk, hnk[k]], [pk])
                    acc = ACC[slot][ag]
                    w2 = self.pvc('fcw', l * 132 + 2 * 44 + blk, 1)
                    w1 = self.pvc('fcw', l * 132 + 1 * 44 + blk, 1)
                    w0 = self.pvc('fcw', l * 132 + 0 * 44 + blk, 1)
                    bb = self.pvc('fcb', l * 44 + blk, 1)
                    self.ACT(acc.ap[:, 0:T], pb[:, 0:T], AF.Identity, [pk, 'pv_fcw', 'pv_fcb'], acc.keys, bias=bb, scale=w2)
                    self.STT(acc.ap[:, 1:T], pb[:, 0:T - 1], w1, acc.ap[:, 1:T], ALU.mult, ALU.add, [pk, 'pv_fcw'] + acc.keys, acc.keys)
                    self.STT(acc.ap[:, 2:T], pb[:, 0:T - 2], w0, acc.ap[:, 2:T], ALU.mult, ALU.add, [pk, 'pv_fcw'] + acc.keys, acc.keys)
                    self.TT('dve', acc.ap[:, 0:2], acc.ap[:, 0:2], self.FCORR[:, blk, :], ALU.add, acc.keys + ['fcorr'], acc.keys)
                    self.CP('act', FT[:, l, blk, :], pb[:, T - 2:T], [pk, 'fcorr'], ['ft'])
                    accs.append(acc)
                sa = SA[slot]
                self.ACT(sa.ap[:, 0:T], accs[0].ap[:, 0:T], AF.Silu, accs[0].keys, sa.keys)
                self.TT('pool', hbv[:, j, 0:T], sa.ap[:, 0:T], accs[1].ap[:, 0:T], ALU.mult, sa.keys + accs[1].keys, [('hb', j)])
            if last:
                pb, pk = self.bank()
                for k in range(8):
                    self.MM(pb[0:2, :], hn[:, k, T - 2:T], wv[:, k, :], k == 0, k == 7, [wk, hnk[k]], [pk])
                ut = UT[pa % 2]
                self.CP('dve', ut.ap[0:2, 0:512], pb[0:2, 0:512], [pk], ut.keys)
                self.DMA('pool', O['o_ffnconv'][job, l, :, pa * 256:(pa + 1) * 256], ut.ap[0:2, 0:256], ut.keys, ['out'])
                self.DMA('pool', O['o_ffnconv'][job, l, :, DFF + pa * 256:DFF + (pa + 1) * 256], ut.ap[0:2, 256:512], ut.keys, ['out'])
        dn = W['dn'][l]
        for m in range(8):
            wt, wk = self.ws_load([(lambda t: t[:, 0:22 * 128].rearrange("p (k n) -> p k n", k=22),
                                    dn[:, m * 128:(m + 1) * 128].rearrange("(k p) n -> p k n", p=128))], ['w_dn'])
            wv = wt[:, 0:22 * 128].rearrange("p (k n) -> p k n", k=22)
            pb, pk = self.bank()
            for k in range(22):
                self.MM(pb[:, 0:T], wv[:, k, :], hbv[:, k, 0:T], k == 0, k == 21, [wk, ('hb', k)], [pk])
            self.resid(job, l, 1, m, pb, pk, T)

    def layer1(self, job, ti, T, last):
        W, O = self.W, self.O
        NC = T // 64
        U = min(T, 128)
        NU = T // U
        NCU = U // 64
        hn = self.ACT8
        hnk = [('act8', k) for k in range(8)]
        self.norm(job, 1, 0, T)
        o = [0]

        def A(nbytes, dt, parts=128):
            r = self.scr(o[0], nbytes, dt, parts=parts)
            o[0] += (nbytes + 63) // 64 * 64
            return r
        QN = A(4 * 512 * 2, BF16)
        KNF = A(512 * 4, F32)
        ZGS = A(4 * 512 * 2, BF16)
        XBC = A(8 * 512 * 2, BF16)
        DT = A(128 * 4, F32)
        DTA = A(4 * 8 * 4, F32)
        VLF = A(128 * 4, F32)
        ACCX = [A(512 * 4, F32) for _ in range(2)]
        NSQ = [A(512 * 2, BF16) for _ in range(2)]
        NRS = [A(512 * 4, F32) for _ in range(2)]
        XT = A(512 * 2, BF16)
        BT = A(256 * 2, BF16)
        LH = A(8 * 128 * 4, F32)
        DEC = A(8 * 128 * 4, F32)
        MGT = A(2 * 128 * 4, F32)
        MT = A(8 * 128 * 2, BF16)
        XDT = A(512 * 2, BF16)
        XD = A(512 * 2, BF16)
        XW = A(512 * 2, BF16)
        EX = A(16 * 4, F32)
        DECB = A(2 * 8 * 4, F32)
        T1 = A(512 * 4, F32)
        YY = A(512 * 4, F32)
        YG = A(512 * 4, F32)
        YJ = A(512 * 4, F32)
        YN = A(512 * 2, BF16)
        SS = A(4 * 4, F32)
        TMPS = [A(512 * 4, F32) for _ in range(2)]
        PT = [A(512 * 2, BF16) for _ in range(4)]
        DEN = [A(256 * 4, F32) for _ in range(2)]
        UT3 = [A(512 * 4, F32, parts=3) for _ in range(2)]
        assert o[0] <= self.SCRN * 4, o[0]
        qnv = QN.ap.rearrange("p (g t) -> p g t", g=4)
        zgv = ZGS.ap.rearrange("p (u f) -> p u f", u=4)
        xbv = XBC.ap.rearrange("p (c t) -> p c t", c=8)
        dtv = DT.ap[:, 0:32].rearrange("p (u h) -> p u h", u=4)
        dtav = self.PV[:, self.pv['tmp']:self.pv['tmp'] + 32].rearrange("p (u h) -> p u h", u=4)
        DTA = Reg(dtav, ['pv_tmp'])
        odin = W['odin']
        MBb = self.MBb

        def qknorm(pb, pk, gvec, gkey, out_ap, out_keys, slot, out2=None, out2_keys=None):
            sq, rs = NSQ[slot], NRS[slot]
            self.ACT(sq.ap[:, 0:T], pb[:, 0:T], AF.Square, [pk], sq.keys)
            pb2, pk2 = self.bank()
            self.MM(pb2[:, 0:T], MBb[:], sq.ap[:, 0:T], True, True, sq.keys + ['mbb'], [pk2])
            self.ACT(rs.ap[:, 0:T], pb2[:, 0:T], AF.Sqrt, [pk2, 'pv_eps'], rs.keys, bias=self.pvc('eps'), scale=1.0 / 64)
            self.fw.op('dve', (lambda rs=rs: lambda e: e.reciprocal(out=rs.ap[:, 0:T], in_=rs.ap[:, 0:T]))(), rs.keys, rs.keys)
            self.STT(out_ap, pb[:, 0:T], gvec, rs.ap[:, 0:T], ALU.mult, ALU.mult, [pk, gkey] + rs.keys, out_keys)
            if out2 is not None:
                self.CP('act', out2, out_ap, out_keys, out2_keys)
        wt, wk = self.ws_load([(lambda t: t[:].rearrange("p (k n) -> p k n", k=8),
                                odin[:, 0:512].rearrange("(k p) n -> p k n", p=128))], ['w_odin'])
        wv = wt[:].rearrange("p (k n) -> p k n", k=8)
        for g in range(4):
            pb, pk = self.bank()
            for kvh in range(2):
                for k in range(8):
                    c0 = kvh * 256 + g * 64
                    self.MM(pb[kvh * 64:(kvh + 1) * 64, 0:T], wv[:, k, c0:c0 + 64], hn[:, k, 0:T], k == 0, k == 7, [wk, hnk[k]], [pk])
            qknorm(pb, pk, self.pvc('qng'), 'pv_qng', qnv[:, g, 0:T], QN.keys, g % 2)
        if self.fine < 31:
            return
        wt, wk = self.ws_load([(lambda t: t[:, 0:2048].rearrange("p (k n) -> p k n", k=8),
                                odin[:, 512:768].rearrange("(k p) n -> p k n", p=128))], ['w_odin'])
        wv = wt[:, 0:2048].rearrange("p (k n) -> p k n", k=8)
        pb, pk = self.bank()
        for k in range(8):
            self.MM(pb[:, 0:T], wv[:, k, 0:128], hn[:, k, 0:T], k == 0, k == 7, [wk, hnk[k]], [pk])
        qknorm(pb, pk, self.pvc('kng'), 'pv_kng', KNF.ap[:, 0:T], KNF.keys, 0, self.KWIN[:, 128:128 + T], ['kwin'])
        for u in range(NU):
            pb, pk = self.bank()
            for k in range(8):
                self.MM(pb[0:U, 0:128], hn[:, k, u * U:(u + 1) * U], wv[:, k, 128:256], k == 0, k == 7, [wk, hnk[k]], [pk])
            self.CP('act', self.VWP[0:U, u + 1, :], pb[0:U, 0:128], [pk], ['vwp'])
            if last and u == NU - 1:
                self.CP('dve', VLF.ap[0:U, 0:128], pb[0:U, 0:128], [pk], VLF.keys)
        if self.fine < 32:
            return
        wt, wk = self.ws_load([(lambda t: t[:].rearrange("p (k n) -> p k n", k=8),
                                odin[:, 768:1280].rearrange("(k p) n -> p k n", p=128))], ['w_odin'])
        wv = wt[:].rearrange("p (k n) -> p k n", k=8)
        for u in range(NU):
            pb, pk = self.bank()
            for k in range(8):
                self.MM(pb[0:U, :], hn[:, k, u * U:(u + 1) * U], wv[:, k, :], k == 0, k == 7, [wk, hnk[k]], [pk])
            self.ACT(zgv[0:U, u, :], pb[0:U, :], AF.Silu, [pk], ZGS.keys)
        if self.fine < 33:
            return
        self.MEMSET('dve', DT.ap[:, 0:128], 0.0, DT.keys)
        for u in range(NU):
            pb, pk = self.bank()
            for k in range(8):
                if os.environ.get('KVAR') != '7':
                    self.MM(pb[0:U, 0:8], hn[:, k, u * U:(u + 1) * U], self.WDT[:, k, :], k == 0, k == 7, ['wdt', hnk[k]], [pk])
            self.TT('dve', dtv[0:U, u, :], pb[0:U, 0:8], self.BC[0:U, 0:8], ALU.add, [pk, 'bc'], DT.keys)
        kv_ = int(os.environ.get('KVAR', '0'))
        if kv_ == 8:
            return
        NW = 128 if os.environ.get('KPAD') else NU * 8
        self.ACT(DT.ap[0:U, 0:NW], DT.ap[0:U, 0:NW], AF.Exp, DT.keys, DT.keys)
        if kv_ == 9:
            return
        if kv_ in (15, 19):
            self.TS('dve', dtav[0:U, 0, :], self.BC[0:U, 8:16], -1.0, None, ALU.mult, None, DT.keys + ['bc'], DTA.keys)
            if kv_ == 15:
                return
        if os.environ.get('KLN') == '1':
            self.TS('dve', DT.ap[0:U, 0:NW], DT.ap[0:U, 0:NW], 1.0, None, ALU.add, None, DT.keys, DT.keys)
            self.ACT(DT.ap[0:U, 0:NW], DT.ap[0:U, 0:NW], AF.Ln, DT.keys, DT.keys)
        else:
            self.ACT(DT.ap[0:U, 0:NW], DT.ap[0:U, 0:NW], AF.Ln, DT.keys + ['pv_one'], DT.keys, bias=self.PV[0:U, self.pv['one']:self.pv['one'] + 1], scale=1.0)
        if kv_ in (10, 19):
            return
        if kv_ == 16:
            self.CP('act', DT.ap[0:U, 0:NU * 8], DT.ap[0:U, 0:NU * 8], DT.keys, DT.keys)
        for u in range(NU):
            if kv_ == 11:
                self.TS('dve', dtav[0:U, u, :], dtv[0:U, u, :], -1.0, None, ALU.mult, None, DT.keys + ['bc'], DTA.keys)
            elif kv_ == 13:
                self.MEMSET('dve', dtav[0:U, u, :], 0.5, DTA.keys)
            elif kv_ == 17:
                self.fw.op('dve', (lambda u=u: lambda e: e.memset(dtav[0:U, u, :], 0.5))(), DT.keys, DTA.keys)
            elif kv_ == 18:
                self.fw.op('pe', (lambda u=u: lambda e: e.matmul(self.PB[7][0:8, 0:8], lhsT=self.IDb[:, 0:8], rhs=self.IDb[:, 0:8], start=True, stop=True))(), DT.keys, ['pb7'])
            elif kv_ == 14:
                self.TS('dve', dtav[0:U, u, :], self.BC[0:U, 8:16], -1.0, None, ALU.mult, None, DT.keys + ['bc'], DTA.keys)
            elif kv_ == 12:
                self.TT('dve', dtav[0:U, u, :], dtv[0:U, u, :], self.BC[0:U, 0:8], ALU.mult, DT.keys + ['bc'], DTA.keys)
            else:
                self.TT('dve', dtav[0:U, u, :], dtv[0:U, u, :], self.BC[0:U, 8:16], ALU.mult, DT.keys + ['bc'], DTA.keys)
        if self.fine < 34:
            return
        CT = self.CT
        scw = lambda j: self.PV[:, self.pv['scw'] + j * 8: self.pv['scw'] + (j + 1) * 8]
        CC, CM = self.CCORR, self.CTMP
        self.TT('dve', CC[:, :, 0], CT[:, :, 3], scw(2), ALU.mult, ['ct', 'pv_scw'], ['ccorr'])
        self.TT('dve', CM[:, :, 0], CT[:, :, 2], scw(1), ALU.mult, ['ct', 'pv_scw'], ['ctmp'])
        self.TT('dve', CC[:, :, 0], CC[:, :, 0], CM[:, :, 0], ALU.add, ['ccorr', 'ctmp'], ['ccorr'])
        self.TT('dve', CM[:, :, 0], CT[:, :, 1], scw(0), ALU.mult, ['ct', 'pv_scw'], ['ctmp'])
        self.TT('dve', CC[:, :, 0], CC[:, :, 0], CM[:, :, 0], ALU.add, ['ccorr', 'ctmp'], ['ccorr'])
        self.TT('dve', CC[:, :, 1], CT[:, :, 3], scw(1), ALU.mult, ['ct', 'pv_scw'], ['ccorr'])
        self.TT('dve', CM[:, :, 1], CT[:, :, 2], scw(0), ALU.mult, ['ct', 'pv_scw'], ['ctmp'])
        self.TT('dve', CC[:, :, 1], CC[:, :, 1], CM[:, :, 1], ALU.add, ['ccorr', 'ctmp'], ['ccorr'])
        self.TT('dve', CC[:, :, 2], CT[:, :, 3], scw(0), ALU.mult, ['ct', 'pv_scw'], ['ccorr'])
        if kv_ == 31:
            return
        for pa in range(2):
            wt, wk = self.ws_load([(lambda t: t[:].rearrange("p (k n) -> p k n", k=8),
                                    odin[:, 1280 + pa * 512:1280 + (pa + 1) * 512].rearrange("(k p) n -> p k n", p=128))], ['w_odin'])
            wv = wt[:].rearrange("p (k n) -> p k n", k=8)
            for mc in range(4):
                c = pa * 4 + mc
                pb, pk = self.bank()
                for k in range(8):
                    self.MM(pb[:, 0:T], wv[:, k, mc * 128:(mc + 1) * 128], hn[:, k, 0:T], k == 0, k == 7, [wk, hnk[k]], [pk])
                if kv_ == 32:
                    continue
                acc = ACCX[c % 2]
                w = [self.pvc('scw', j * 8 + c, 1) for j in range(4)]
                self.ACT(acc.ap[:, 0:T], pb[:, 0:T], AF.Identity, [pk, 'pv_scw', 'pv_scb'], acc.keys, bias=self.pvc('scb', c, 1), scale=w[3])
                for sh in (1, 2, 3):
                    if kv_ == 34 and sh == 3:
                        continue
                    self.STT(acc.ap[:, sh:T], pb[:, 0:T - sh], w[3 - sh], acc.ap[:, sh:T], ALU.mult, ALU.add, [pk, 'pv_scw'] + acc.keys, acc.keys)
                if kv_ != 35:
                    self.TT('dve', acc.ap[:, 0:3], acc.ap[:, 0:3], CC[:, c, :], ALU.add, acc.keys + ['ccorr'], acc.keys)
                if kv_ != 36:
                    self.CP('dve', CT[:, c, :], pb[:, T - 4:T], [pk, 'ccorr'], ['ct'])
                self.ACT(xbv[:, c, 0:T], acc.ap[:, 0:T], AF.Silu, acc.keys, XBC.keys)
            if last and kv_ != 33:
                pb, pk = self.bank()
                for k in range(8):
                    self.MM(pb[0:3, :], hn[:, k, T - 3:T], wv[:, k, :], k == 0, k == 7, [wk, hnk[k]], [pk])
                ut = UT3[pa]
                self.CP('dve', ut.ap[0:3, 0:512], pb[0:3, 0:512], [pk], ut.keys)
                self.DMA('pool', O['o_ssdconv'][job, :, pa * 512:(pa + 1) * 512], ut.ap[0:3, 0:512], ut.keys, ['out'])
        if self.sub < 11:
            return
        mixed = self.ACT8
        MC = self.HC[0:U, HC_MC:HC_MC + U]
        MG = self.HC[0:U, HC_MG:HC_MG + U]
        self.CP('act', self.STBH[:, 0, :], self.ST[:], ['st'], [('stbh', 0)])
        lhv = LH.ap.rearrange("p (h l) -> p h l", h=8)
        decv = DEC.ap.rearrange("p (h l) -> p h l", h=8)
        mgtv = MGT.ap.rearrange("p (g l) -> p g l", g=2)
        mtv = MT.ap.rearrange("p (h l) -> p h l", h=8)
        decb = DECB.ap.rearrange("p (c h) -> p c h", c=2)
        for u in range(NU):
            cols = slice(u * U, (u + 1) * U)
            pb, pk = self.bank()
            pbb = pb[:].bitcast(BF16)
            for q in range(4):
                self.TR(pbb[0:U, q * 128:(q + 1) * 128], xbv[:, q, cols], self.IDb[:], XBC.keys + ['idb'], [pk])
            for q in range(2):
                self.TR(pbb[0:U, 512 + q * 128:512 + (q + 1) * 128], xbv[:, 4 + q, cols], self.IDb[:], XBC.keys + ['idb'], [pk])
            self.CP('dve', XT.ap[0:U, 0:512], pbb[0:U, 0:512], [pk], XT.keys)
            self.CP('act', BT.ap[0:U, 0:256], pbb[0:U, 512:768], [pk], BT.keys)
            pb, pk = self.bank()
            self.MM(pb[0:U, 0:8], MG, dtav[0:U, u, :], True, True, ['hc'] + DTA.keys, [pk])
            self.MM(pb[0:U, 8:16], MC, dtav[0:U, u, :], True, True, ['hc'] + DTA.keys, [pk])
            for cu in range(NCU):
                self.MM(pb[:, 16 + cu * 8:24 + cu * 8], self.HC[0:U, HC_CS + cu * 128:HC_CS + (cu + 1) * 128], dtav[0:U, u, :], True, True,
                        ['hc'] + DTA.keys, [pk])
            self.ACT(EX.ap[0:U, 0:16], pb[0:U, 0:16], AF.Exp, [pk], EX.keys)
            self.ACT(DECB.ap[:, 0:NCU * 8], pb[:, 16:16 + NCU * 8], AF.Exp, [pk], DECB.keys)
            for h in range(8):
                self.TS('pool' if h % 2 else 'dve', lhv[0:U, h, 0:U], MG, dtav[0:U, u, h:h + 1], None, ALU.mult, None, ['hc'] + DTA.keys, LH.keys)
            dbk = [self.bank(), self.bank()]
            for h in range(8):
                pbd, pkd = dbk[h // 4]
                self.MM(pbd[0:U, (h % 4) * U:(h % 4 + 1) * U], lhv[0:U, h, 0:U], MC, True, True, LH.keys + ['hc'], [pkd])
            for hh in range(2):
                pbd, pkd = dbk[hh]
                self.ACT(decv[0:U, hh * 4:(hh + 1) * 4, 0:U], pbd[0:U, 0:4 * U].rearrange("p (h l) -> p h l", h=4), AF.Exp, [pkd], DEC.keys)
            pbg, pkg = self.bank()
            for grp in range(2):
                self.MM(pbg[0:U, grp * U:(grp + 1) * U], xbv[:, 4 + grp, cols], xbv[:, 6 + grp, cols], True, True, XBC.keys, [pkg])
            self.TT('dve', mgtv[0:U, :, 0:U], pbg[0:U, 0:2 * U].rearrange("p (g l) -> p g l", g=2), MC.unsqueeze(1).broadcast_to([U, 2, U]),
                    ALU.mult, [pkg, 'hc'], MGT.keys)
            self.TT('dve', mtv[0:U, :, 0:U].rearrange("p (g q) l -> p g q l", g=2), decv[0:U, :, 0:U].rearrange("p (g q) l -> p g q l", g=2),
                    mgtv[0:U, :, 0:U].unsqueeze(2).broadcast_to([U, 2, 4, U]), ALU.mult, DEC.keys + MGT.keys, MT.keys)
            x3 = XT.ap[0:U, 0:512].rearrange("p (h q) -> p h q", h=8)
            self.TT('dve', XDT.ap[0:U, 0:512].rearrange("p (h q) -> p h q", h=8), x3, dtv[0:U, u, :].unsqueeze(2).broadcast_to([U, 8, 64]),
                    ALU.mult, XT.keys + DT.keys, XDT.keys)
            self.TT('pool', XD.ap[0:U, 0:512].rearrange("p (h q) -> p h q", h=8), x3, self.BC[0:U, 16:24].unsqueeze(2).broadcast_to([U, 8, 64]),
                    ALU.mult, XT.keys + ['bc'], XD.keys)
            self.TT('dve', XW.ap[0:U, 0:512].rearrange("p (h q) -> p h q", h=8), XDT.ap[0:U, 0:512].rearrange("p (h q) -> p h q", h=8),
                    EX.ap[0:U, 0:8].unsqueeze(2).broadcast_to([U, 8, 64]), ALU.mult, XDT.keys + EX.keys, XW.keys)
            pby, pky = self.bank()
            self.MM(pby[0:U, :], self.IDb[0:U, 0:U], XD.ap[0:U, 0:512], True, False, ['idb'] + XD.keys, [pky])
            for h in range(8):
                self.MM(pby[0:U, h * 64:(h + 1) * 64], mtv[0:U, h, 0:U], XDT.ap[0:U, h * 64:(h + 1) * 64], False, h == 7, MT.keys + XDT.keys, [pky])
            for cu in range(NCU):
                c = u * NCU + cu
                p0 = cu * 64
                pbu, pku = self.bank()
                for grp in range(2):
                    self.MM(pbu[:, grp * 256:(grp + 1) * 256], BT.ap[p0:p0 + 64, grp * 128:(grp + 1) * 128], XW.ap[p0:p0 + 64, grp * 256:(grp + 1) * 256],
                            True, True, BT.keys + XW.keys, [pku])
                self.TT('dve', self.ST[:].rearrange("p (h q) -> p h q", h=8), self.ST[:].rearrange("p (h q) -> p h q", h=8),
                        decb[:, cu, :].unsqueeze(2).broadcast_to([128, 8, 64]), ALU.mult, ['st'] + DECB.keys, ['st'])
                self.TT('dve', self.ST[:], self.ST[:], pbu[:, :], ALU.add, ['st', pku], ['st'])
                self.CP('act', self.STBH[:, c + 1, :], self.ST[:], ['st'], [('stbh', c + 1)])
            pbi, pki = self.bank()
            for cu in range(NCU):
                c = u * NCU + cu
                for grp in range(2):
                    self.MM(pbi[cu * 64:(cu + 1) * 64, grp * 256:(grp + 1) * 256], xbv[:, 6 + grp, u * U + cu * 64:u * U + (cu + 1) * 64],
                            self.STBH[:, c, grp * 256:(grp + 1) * 256], True, True, XBC.keys + [('stbh', c)], [pki])
            self.TT('dve', T1.ap[0:U, 0:512].rearrange("p (h q) -> p h q", h=8), pbi[0:U, :].rearrange("p (h q) -> p h q", h=8),
                    EX.ap[0:U, 8:16].unsqueeze(2).broadcast_to([U, 8, 64]), ALU.mult, [pki] + EX.keys, T1.keys)
            self.TT('dve', YY.ap[0:U, 0:512], pby[0:U, :], T1.ap[0:U, 0:512], ALU.add, [pky] + T1.keys, YY.keys)
            self.TT('pool', YG.ap[0:U, 0:512], YY.ap[0:U, 0:512], zgv[0:U, u, :], ALU.mult, YY.keys + ZGS.keys, YG.keys)
            self.fw.op('act', (lambda u=u: lambda e: e.activation(out=YJ.ap[0:U, 0:512], in_=YG.ap[0:U, 0:512], func=AF.Square,
                                                                  accum_out=SS.ap[0:U, 0:1]))(), YG.keys, YJ.keys + SS.keys)
            self.ACT(SS.ap[0:U, 0:1], SS.ap[0:U, 0:1], AF.Sqrt, SS.keys + ['pv_eps'], SS.keys, bias=self.PV[0:U, self.pv['eps']:self.pv['eps'] + 1], scale=1.0 / 512)
            self.fw.op('dve', lambda e: e.reciprocal(out=SS.ap[0:U, 0:1], in_=SS.ap[0:U, 0:1]), SS.keys, SS.keys)
            self.STT(YN.ap[0:U, 0:512], YG.ap[0:U, 0:512], SS.ap[0:U, 0:1], self.BC[0:U, 32:544], ALU.mult, ALU.mult, YG.keys + SS.keys + ['bc'], YN.keys)
            pbt, pkt = self.bank()
            pbtb = pbt[:].bitcast(BF16)
            for q in range(4):
                self.TR(pbtb[:, q * U:(q + 1) * U], YN.ap[0:U, q * 128:(q + 1) * 128], self.IDb[0:U, 0:U], YN.keys + ['idb'], [pkt])
            self.CP('act', mixed[:, 4:8, cols], pbtb[:, 0:4 * U].rearrange("p (q l) -> p q l", q=4), [pkt], [('act8', 4 + q) for q in range(4)])
        if self.sub < 12:
            return
        gc0 = ti * NC
        for c in range(NC):
            gc = gc0 + c if job < 2 else 2
            odd = c % 2
            BI = self.BIO if odd else self.BIE
            pts = []
            for kvh in range(2):
                pbs, pks = self.bank()
                rows = slice(kvh * 64, (kvh + 1) * 64)
                rhs = qnv[rows, :, c * 64:(c + 1) * 64]
                if not odd:
                    self.MM(pbs[0:128, 0:256], self.KWIN[rows, c * 64:c * 64 + 128], rhs, True, True, ['kwin'] + QN.keys, [pks])
                    self.MM(pbs[0:64, 256:512], self.KWIN[rows, (c + 2) * 64:(c + 3) * 64], rhs, True, True, ['kwin'] + QN.keys, [pks])
                else:
                    self.MM(pbs[0:128, 0:256], self.KWIN[rows, (c + 1) * 64:(c + 1) * 64 + 128], rhs, True, True, ['kwin'] + QN.keys, [pks])
                    self.MM(pbs[64:128, 256:512], self.KWIN[rows, c * 64:(c + 1) * 64], rhs, True, True, ['kwin'] + QN.keys, [pks])
                tmp = TMPS[kvh]
                pt = PT[(c % 2) * 2 + kvh]
                self.TT('dve', tmp.ap[:, 0:512], pbs[:, 0:512], BI[:, kvh, :], ALU.add, [pks, 'bie', 'bio'], tmp.keys)
                self.ACT(pt.ap[:, 0:512], tmp.ap[:, 0:512], AF.Exp, tmp.keys, pt.keys)
                if gc == 0:
                    self.MEMSET('pool', pt.ap[:, 0:256], 0.0, pt.keys)
                elif gc == 1:
                    self.MEMSET('pool', pt.ap[64:128, 256:512], 0.0, pt.keys)
                pts.append(pt)
            pbo, pko = self.bank()
            pbd, pkd = self.bank()
            for kvh in range(2):
                pt = pts[kvh]
                vc = slice(kvh * 64, (kvh + 1) * 64)
                if not odd:
                    pair_slot, single_slot, sp0 = c // 2, c // 2 + 1, 0
                else:
                    pair_slot, single_slot, sp0 = (c + 1) // 2, c // 2, 64
                for (pbx, pkx, lhs_pair, lhs_single) in [
                        (pbo, pko, self.VWP[:, pair_slot, vc], self.VWP[sp0:sp0 + 64, single_slot, vc]),
                        (pbd, pkd, self.ONESb[:, 0:64], self.ONESb[sp0:sp0 + 64, 0:64])]:
                    self.MM(pbx[kvh * 64:(kvh + 1) * 64, 0:256], lhs_pair, pt.ap[:, 0:256], True, False, ['vwp', 'onesb'] + pt.keys, [pkx])
                    self.MM(pbx[kvh * 64:(kvh + 1) * 64, 0:256], lhs_single, pt.ap[sp0:sp0 + 64, 256:512], False, True, ['vwp', 'onesb'] + pt.keys, [pkx])
            den = DEN[c % 2]
            self.TT('dve', den.ap[:, 0:256].rearrange("p (g l) -> p g l", g=4), pbd[:, 0:256].rearrange("p (g l) -> p g l", g=4),
                    self.pvc('sinkexp', 0, 4).unsqueeze(2).broadcast_to([128, 4, 64]), ALU.add, [pkd, 'pv_sink'], den.keys)
            self.fw.op('dve', (lambda den=den: lambda e: e.reciprocal(out=den.ap[:, 0:256], in_=den.ap[:, 0:256]))(), den.keys, den.keys)
            self.TT('dve', mixed[:, 0:4, c * 64:(c + 1) * 64], pbo[:, 0:256].rearrange("p (g l) -> p g l", g=4),
                    den.ap[:, 0:256].rearrange("p (g l) -> p g l", g=4), ALU.mult, [pko] + den.keys, [('act8', g) for g in range(4)])
        if last:
            R = U
            if job < 2:
                self.tm_store(O['o_pk'][job], KNF.ap[:, T - 128:T], KNF.keys, 128, 128)
                self.DMA('pool', O['o_pv'][job], VLF.ap[0:128, 0:128], VLF.keys, ['out'])
            else:
                self.tm_store(O['o_sk'], KNF.ap[:, 0:64], KNF.keys, 64, 128)
                self.DMA('pool', O['o_sv'], VLF.ap[0:64, 0:128], VLF.keys, ['out'])
        else:
            self.CP('dve', self.KWIN[:, 0:128], self.KWIN[:, T:T + 128], ['kwin'], ['kwin'])
            self.CP('act', self.VWP[:, 0, :], self.VWP[:, NU, :], ['vwp'], ['vwp'])
        if self.sub < 13:
            return
        self.out_proj(job, 1, T, W['odout'], True)


_CACHE = {}


def _get_nc(SEQ, do_l1=True):
    key = (SEQ, do_l1)
    if key not in _CACHE:
        _CACHE[key] = Builder(SEQ, do_l1).build()
    return _CACHE[key]


def make_in_maps(inp, SEQ):
    f = lambda a: np.ascontiguousarray(np.asarray(a, dtype=np.float32))
    hc = host_consts()
    shared = {
        't5_bias': f(inp['t5_bias']), 'norm1_g': f(inp['norm1_g']).reshape(16, 128), 'norm2_g': f(inp['norm2_g']).reshape(16, 128),
        'w_mod': f(inp['w_mod']), 'b_mod': f(inp['b_mod']), 'ffn_w_up': f(inp['ffn_w_up']),
        'ffn_conv_w': f(inp['ffn_conv_w']).reshape(2, 132, 128), 'ffn_conv_b': f(inp['ffn_conv_b']).reshape(2, 44, 128),
        'ffn_w_down': f(inp['ffn_w_down']), 'ev_w_in': f(inp['ev_w_in'])[0], 'ev_w_out': f(inp['ev_w_out'])[0],
        's5_a_re': f(inp['s5_a_re'])[0], 's5_a_im': f(inp['s5_a_im'])[0], 's5_log_dt': f(inp['s5_log_dt'])[0],
        's5_b_re': f(inp['s5_b_re'])[0], 's5_b_im': f(inp['s5_b_im'])[0],
        's5_c_re': f(inp['s5_c_re'])[0].reshape(512, 64), 's5_c_im': f(inp['s5_c_im'])[0].reshape(512, 64),
        's5_d': f(inp['s5_d'])[0], 's5_w_glu': f(inp['s5_w_glu'])[0], 's5_b_glu': f(inp['s5_b_glu'])[0].reshape(4, 128),
        'gla_w_gate2': f(inp['gla_w_gate2'])[0], 'gla_b_gate': f(inp['gla_b_gate'])[0].reshape(2, 128),
        'gla_norm_g': f(inp['gla_norm_g'])[0], 'od_w_in': f(inp['od_w_in'])[0], 'od_w_out': f(inp['od_w_out'])[0],
        'swa_q_norm': f(inp['swa_q_norm'])[0], 'swa_k_norm': f(inp['swa_k_norm'])[0], 'swa_sink': f(inp['swa_sink'])[0],
        'ssd_conv_w': f(inp['ssd_conv_w'])[0].reshape(32, 128), 'ssd_conv_b': f(inp['ssd_conv_b'])[0].reshape(8, 128),
        'ssd_dt_bias': f(inp['ssd_dt_bias'])[0], 'ssd_a_log': f(inp['ssd_a_log'])[0], 'ssd_d': f(inp['ssd_d'])[0],
        'ssd_norm_g': f(inp['ssd_norm_g'])[0], 'hc': hc,
    }
    xp, xs = f(inp['x_prompt']), f(inp['x_sample'])
    cp, cs = f(inp['c_prompt']), f(inp['c_sample'])
    maps = []
    for c in range(8):
        m = dict(shared)
        m['xp'] = xp[2 * c:2 * c + 2]
        m['xs'] = xs[c]
        m['cvec'] = np.concatenate([cp[2 * c], cp[2 * c + 1], cs[c]]).reshape(24, 128)
        m['st_s5re'] = f(inp['state_s5_re'])[0, c]
        m['st_s5im'] = f(inp['state_s5_im'])[0, c]
        m['st_gla'] = f(inp['state_gla'])[0, c].reshape(256, 128)
        m['ck'] = f(inp['cache_swa_k'])[0, c].reshape(128, 128)
        m['cv'] = f(inp['cache_swa_v'])[0, c].reshape(128, 128)
        m['st_ssd'] = f(inp['state_ssd'])[0, c].reshape(512, 128)
        m['st_ssdconv'] = f(inp['state_ssd_conv'])[0, c].reshape(24, 128)
        m['st_ffnconv'] = f(inp['state_ffn_conv'])[:, c].reshape(2, 88, 128)
        maps.append({k: np.ascontiguousarray(v) for k, v in m.items()})
    return maps


def assemble(res, SEQ):
    B, DB = 16, 8
    y_prompt = np.zeros((B, SEQ, D), np.float32)
    y_sample = np.zeros((DB, 64, D), np.float32)
    p_s5_re = np.zeros((1, B, 32, 64), np.float32)
    p_s5_im = np.zeros((1, B, 32, 64), np.float32)
    p_gla = np.zeros((1, B, 4, 64, 128), np.float32)
    p_swa_k = np.zeros((1, B, 128, 2, 64), np.float32)
    p_swa_v = np.zeros((1, B, 128, 2, 64), np.float32)
    p_ssd = np.zeros((1, B, 8, 64, 128), np.float32)
    p_ssd_conv = np.zeros((1, B, 3, 1024), np.float32)
    p_ffn_conv = np.zeros((2, B, 2, 2 * DFF), np.float32)
    s_s5_re = np.zeros((1, DB, 32, 64), np.float32)
    s_s5_im = np.zeros((1, DB, 32, 64), np.float32)
    s_gla = np.zeros((1, DB, 4, 64, 128), np.float32)
    s_swa_k = np.zeros((1, DB, 64, 2, 64), np.float32)
    s_swa_v = np.zeros((1, DB, 64, 2, 64), np.float32)
    s_ssd = np.zeros((1, DB, 8, 64, 128), np.float32)
    s_ssd_conv = np.zeros((1, DB, 3, 1024), np.float32)
    s_ffn_conv = np.zeros((2, DB, 2, 2 * DFF), np.float32)
    for c in range(8):
        r = res[c]
        y_prompt[2 * c:2 * c + 2] = r['yp']
        y_sample[c] = r['ys']
        for jb in range(2):
            b = 2 * c + jb
            p_s5_re[0, b] = r['o_s5re'][jb]
            p_s5_im[0, b] = r['o_s5im'][jb]
            p_gla[0, b] = r['o_gla'][jb].reshape(4, 64, 128)
            p_swa_k[0, b] = r['o_pk'][jb].reshape(128, 2, 64)
            p_swa_v[0, b] = r['o_pv'][jb].reshape(128, 2, 64)
            p_ssd[0, b] = r['o_ssd'][jb].reshape(8, 64, 128)
            p_ssd_conv[0, b] = r['o_ssdconv'][jb]
            p_ffn_conv[:, b] = r['o_ffnconv'][jb]
        s_s5_re[0, c] = r['o_s5re'][2]
        s_s5_im[0, c] = r['o_s5im'][2]
        s_gla[0, c] = r['o_gla'][2].reshape(4, 64, 128)
        s_swa_k[0, c] = r['o_sk'].reshape(64, 2, 64)
        s_swa_v[0, c] = r['o_sv'].reshape(64, 2, 64)
        s_ssd[0, c] = r['o_ssd'][2].reshape(8, 64, 128)
        s_ssd_conv[0, c] = r['o_ssdconv'][2]
        s_ffn_conv[:, c] = r['o_ffnconv'][2]
    return (y_prompt, y_sample, p_s5_re, p_s5_im, p_gla, p_swa_k, p_swa_v, p_ssd, p_ssd_conv, p_ffn_conv,
            s_s5_re, s_s5_im, s_gla, s_swa_k, s_swa_v, s_ssd, s_ssd_conv, s_ffn_conv)


def kernel(**inputs):
    SEQ = int(np.asarray(inputs['x_prompt']).shape[1])
    nc = _get_nc(SEQ)
    in_maps = make_in_maps(inputs, SEQ)
    res = run_bass_kernel_spmd(nc, in_maps, core_ids=list(range(8)))
    return assemble(res.results, SEQ)
```

```python
import math
import os
from contextlib import ExitStack
import numpy as np
import concourse.bass as bass
import concourse.mybir as mybir
from concourse.bass_utils import run_bass_kernel_spmd

F32 = mybir.dt.float32
BF16 = mybir.dt.bfloat16
I32 = mybir.dt.int32
AF = mybir.ActivationFunctionType
ALU = mybir.AluOpType

NDMA = 48
D = 1024
DFF = 2816
EVEN_IN = 2064
ODD_IN = 2312
EPS = 1e-6


class FW:
    ENGS = ('pe', 'dve', 'act', 'pool', 'sp')

    def __init__(self, nc, stack):
        self.nc = nc
        self.stack = stack
        self.sem = {}
        self.prog = {e: [] for e in self.ENGS}
        self.cnt = {}
        self.seen = {e: {} for e in self.ENGS}
        self.bufs = {}
        for e in self.ENGS:
            self._mksem('E_' + e)
        self.dma_pool = {'sp': [f'DS{i}' for i in range(32)], 'pool': [f'DP{i}' for i in range(24)],
                         'act': [f'DA{i}' for i in range(8)]}
        for q, names in self.dma_pool.items():
            for n in names:
                self._mksem(n)
        self.dma_rr = {'sp': 0, 'pool': 0, 'act': 0}
        self.dma_last_clock = {}
        self.nops = 0
        self.noself = ('pe', 'sp') + tuple(os.environ.get('KNOSELF', '').split(','))

    def _mksem(self, name):
        self.sem[name] = self.stack.enter_context(self.nc.semaphore(name))
        self.cnt[name] = 0

    def _deps(self, reads, writes):
        deps = []
        for b in reads:
            st = self.bufs.get(b)
            if st and st['w'] is not None:
                deps.append(('raw', st['w']))
        for b in writes:
            st = self.bufs.get(b)
            if st:
                if st['w'] is not None:
                    deps.append(('waw', st['w']))
                for r in st['r']:
                    deps.append(('war', r))
        return deps

    def _waits(self, eng, deps):
        seen = self.seen[eng]
        own = 'E_' + eng
        need = {}
        used = []
        for kind, (s, v, clock) in deps:
            if s == own and (eng in self.noself or kind != 'raw'):
                continue
            used.append((s, v, clock))
            if seen.get(s, 0) >= v:
                continue
            if need.get(s, 0) < v:
                need[s] = v
        for s, v, clock in used:
            for cs, cv in clock.items():
                if cs == own:
                    continue
                if seen.get(cs, 0) < cv:
                    seen[cs] = cv
            if seen.get(s, 0) < v:
                seen[s] = v
        return sorted(need.items())

    def _record(self, ev, reads, writes):
        for b in reads:
            st = self.bufs.setdefault(b, {'w': None, 'r': []})
            st['r'].append(ev)
            if len(st['r']) > 96:
                st['r'] = st['r'][-96:]
        for b in writes:
            self.bufs[b] = {'w': ev, 'r': []}

    def op(self, eng, fn, reads=(), writes=()):
        deps = self._deps(reads, writes)
        waits = self._waits(eng, deps)
        own = 'E_' + eng
        self.cnt[own] += 1
        v = self.cnt[own]
        clock = dict(self.seen[eng])
        clock[own] = v
        self.prog[eng].append((waits, fn, (own, 1)))
        self._record((own, v, clock), reads, writes)
        self.nops += 1

    def dma(self, q, fn, reads=(), writes=()):
        deps = self._deps(reads, writes)
        names = self.dma_pool[q]
        s = names[self.dma_rr[q]]
        self.dma_rr[q] = (self.dma_rr[q] + 1) % len(names)
        if self.cnt[s] > 0:
            deps.append(('raw', (s, self.cnt[s], self.dma_last_clock.get(s, {}))))
        waits = self._waits(q, deps)
        self.cnt[s] += 16
        v = self.cnt[s]
        clock = dict(self.seen[q])
        clock.pop('E_' + q, None)
        self.dma_last_clock[s] = clock
        self.prog[q].append((waits, fn, (s, 16)))
        self._record((s, v, clock), reads, writes)
        self.nops += 1

    def finish_waits(self, eng='sp'):
        waits = [(s, v) for s, v in self.cnt.items() if v > 0 and s != 'E_' + eng]
        self.prog[eng].append((waits, None, None))

    def replay(self):
        nc = self.nc
        with nc.Block() as block:
            def mk(engname):
                def body(e):
                    for waits, fn, inc in self.prog[engname]:
                        for s, v in waits:
                            e.wait_ge(self.sem[s], v)
                        if fn is not None:
                            ins = fn(e)
                            ins.then_inc(self.sem[inc[0]], inc[1])
                return body
            block.tensor(mk('pe'))
            block.vector(mk('dve'))
            block.scalar(mk('act'))
            block.gpsimd(mk('pool'))
            block.sync(mk('sp'))


HC_IDENT, HC_MC, HC_MG, HC_MB, HC_S5M, HC_S5E = 0, 128, 256, 384, 512, 640
HC_CS = 768
HC_RST = 1024
HC_OHW = 1536
NHC = 1920


def _t5_bucket_np(rel):
    import jax
    import jax.numpy as jnp
    with jax.default_device(jax.devices('cpu')[0]):
        rel = jnp.asarray(rel, jnp.int32)
        nb = 16
        max_exact = 8
        ret = (rel > 0).astype(jnp.int32) * nb
        n = jnp.abs(rel)
        nf = jnp.maximum(n, 1).astype(jnp.float32)
        large = max_exact + (jnp.log(nf / max_exact) / math.log(128 / max_exact) * (nb - max_exact)).astype(jnp.int32)
        large = jnp.minimum(large, nb - 1)
        out = ret + jnp.where(n < max_exact, n, large)
        return np.asarray(out)


def host_consts():
    hc = np.zeros((128, NHC), np.float32)
    j = np.arange(128)[:, None]
    l = np.arange(128)[None, :]
    same = (j // 64) == (l // 64)
    hc[:, HC_IDENT:HC_IDENT + 128] = (j == l)
    hc[:, HC_MC:HC_MC + 128] = same & (j <= l)
    hc[:, HC_MG:HC_MG + 128] = same & (j > l)
    hc[:, HC_MB:HC_MB + 128] = same
    s_sub = j // 16
    hi = j % 16
    ho = l // 8
    l_sub = l % 8
    hc[:, HC_S5M:HC_S5M + 128] = (l_sub >= s_sub)
    hc[:, HC_S5E:HC_S5E + 128] = (l_sub == s_sub) & (ho == hi)
    for c in range(2):
        hc[:, HC_CS + c * 128:HC_CS + (c + 1) * 128] = ((j // 64) == c)
    rst = np.ones(512, np.float32)
    rst[::64] = 0.0
    hc[:, HC_RST:HC_RST + 512] = rst[None, :]
    m = np.arange(128)
    d = np.where(m < 64, m, m - 128)
    for kb in range(3):
        rel = kb * 64 - 128 - d
        bk = _t5_bucket_np(rel)
        for mm in range(128):
            hc[bk[mm], HC_OHW + kb * 128 + mm] = 1.0
    return hc


class Reg:
    def __init__(self, ap, keys):
        self.ap = ap
        self.keys = list(keys)


class Builder:
    def __init__(self, SEQ, do_l1=True):
        self.SEQ = SEQ
        self.do_l1 = do_l1
        self.nc = bass.Bass("TRN2", target_bir_lowering=False)
        self.bank_rr = 0
        self.stg_rr = 0
        self.stage = int(os.environ.get('KSTAGE', '99'))

    def din(self, name, shape, dt=F32):
        return self.nc.dram_tensor(name, list(shape), dt, kind="ExternalInput").ap()

    def dout(self, name, shape, dt=F32):
        return self.nc.dram_tensor(name, list(shape), dt, kind="ExternalOutput").ap()

    def dscr(self, name, shape, dt):
        return self.nc.dram_tensor(name, list(shape), dt, kind="Internal").ap()

    def sb(self, name, shape, dt):
        return self.st.enter_context(self.nc.sbuf_tensor(name, list(shape), dt))

    def MM(self, out, lhsT, rhs, st, sp, r, w):
        self.fw.op('pe', lambda e: e.matmul(out, lhsT=lhsT, rhs=rhs, start=st, stop=sp), r, w)

    def TR(self, out, in_, idn, r, w):
        self.fw.op('pe', lambda e: e.transpose(out=out, in_=in_, identity=idn), r, w)

    def ACT(self, out, in_, func, r, w, bias=None, scale=None, eng='act'):
        kw = {}
        if bias is not None:
            kw['bias'] = bias
        if scale is not None:
            kw['scale'] = scale
        self.fw.op(eng, lambda e: e.activation(out=out, in_=in_, func=func, **kw), r, w)

    def TT(self, eng, out, a, b, op, r, w):
        self.fw.op(eng, lambda e: e.tensor_tensor(out=out, in0=a, in1=b, op=op), r, w)

    def TS(self, eng, out, a, s1, s2, op0, op1, r, w):
        if op1 is None:
            self.fw.op(eng, lambda e: e.tensor_scalar(out=out, in0=a, scalar1=s1, scalar2=None, op0=op0), r, w)
        else:
            self.fw.op(eng, lambda e: e.tensor_scalar(out=out, in0=a, scalar1=s1, scalar2=s2, op0=op0, op1=op1), r, w)

    def STT(self, out, a, s, b, op0, op1, r, w):
        self.fw.op('dve', lambda e: e.scalar_tensor_tensor(out=out, in0=a, scalar=s, in1=b, op0=op0, op1=op1), r, w)

    def CP(self, eng, out, in_, r, w):
        if eng == 'act':
            self.fw.op('act', lambda e: e.copy(out=out, in_=in_), r, w)
        else:
            self.fw.op(eng, lambda e: e.tensor_copy(out=out, in_=in_), r, w)

    def MEMSET(self, eng, ap, val, w):
        self.fw.op(eng, lambda e: e.memset(ap, val), (), w)

    def DMA(self, q, out, in_, r, w, nonc=False):
        if nonc:
            self.fw.dma(q, lambda e: e.dma_start(out=out, in_=in_, allow_slow_non_contiguous=True), r, w)
        else:
            self.fw.dma(q, lambda e: e.dma_start(out=out, in_=in_), r, w)

    def bank(self):
        i = self.bank_rr
        self.bank_rr = (self.bank_rr + 1) % 8
        return self.PB[i], f'pb{i}'

    def scr(self, off_bytes, nbytes, dt, shape_tail=None, parts=128):
        assert off_bytes % 4 == 0 and off_bytes + nbytes <= self.SCRN * 4, (off_bytes, nbytes)
        a = self.SCR[0:parts, off_bytes // 4:(off_bytes + nbytes + 3) // 4]
        if dt == BF16:
            a = a.bitcast(BF16)
        keys = [('scr', pg) for pg in range(off_bytes // 2048, (off_bytes + nbytes - 1) // 2048 + 1)]
        return Reg(a, keys)

    def fm_load(self, dst_ap, dst_keys, src_ap, R, W, src_keys=()):
        s = self.stg_rr
        self.stg_rr = (self.stg_rr + 1) % 2
        stg = self.STG[s]
        self.DMA('sp', stg[0:R, 0:W], src_ap, list(src_keys), [f'stg{s}'])
        pb, pk = self.bank()
        self.TR(pb[0:W, 0:R], stg[0:R, 0:W], self.IDf[0:R, 0:R], [f'stg{s}', 'hc'], [pk])
        self.CP('dve', dst_ap, pb[0:W, 0:R], [pk], dst_keys)

    def tm_store(self, dst_ap, src_ap, src_keys, R, W, q='pool'):
        s = self.stg_rr
        self.stg_rr = (self.stg_rr + 1) % 2
        stg = self.STG[s]
        pb, pk = self.bank()
        self.TR(pb[0:R, 0:W], src_ap, self.IDf[0:W, 0:W], list(src_keys) + ['hc'], [pk])
        self.CP('dve', stg[0:R, 0:W], pb[0:R, 0:W], [pk], [f'stg{s}'])
        self.DMA(q, dst_ap, stg[0:R, 0:W], [f'stg{s}'], ['out'])

    def build(self):
        nc = self.nc
        SEQ = self.SEQ
        I = {}
        I['xp'] = self.din('xp', [2, SEQ, D])
        I['xs'] = self.din('xs', [64, D])
        I['cvec'] = self.din('cvec', [24, 128])
        I['st_s5re'] = self.din('st_s5re', [32, 64])
        I['st_s5im'] = self.din('st_s5im', [32, 64])
        I['st_gla'] = self.din('st_gla', [256, 128])
        I['ck'] = self.din('ck', [128, 128])
        I['cv'] = self.din('cv', [128, 128])
        I['st_ssd'] = self.din('st_ssd', [512, 128])
        I['st_ssdconv'] = self.din('st_ssdconv', [24, 128])
        I['st_ffnconv'] = self.din('st_ffnconv', [2, 88, 128])
        for nm, shp in [('t5_bias', [32, 8]), ('norm1_g', [16, 128]), ('norm2_g', [16, 128]), ('w_mod', [2, D, 6144]),
                        ('b_mod', [2, 6144]), ('ffn_w_up', [2, D, 2 * DFF]), ('ffn_conv_w', [2, 132, 128]),
                        ('ffn_conv_b', [2, 44, 128]), ('ffn_w_down', [2, DFF, D]), ('ev_w_in', [D, EVEN_IN]),
                        ('ev_w_out', [D, D]), ('s5_a_re', [32, 64]), ('s5_a_im', [32, 64]), ('s5_log_dt', [32]),
                        ('s5_b_re', [32, 64, 16]), ('s5_b_im', [32, 64, 16]), ('s5_c_re', [512, 64]),
                        ('s5_c_im', [512, 64]), ('s5_d', [512]), ('s5_w_glu', [512, 512]), ('s5_b_glu', [4, 128]),
                        ('gla_w_gate2', [16, 256]), ('gla_b_gate', [2, 128]), ('gla_norm_g', [128]),
                        ('od_w_in', [D, ODD_IN]), ('od_w_out', [D, D]), ('swa_q_norm', [64]), ('swa_k_norm', [64]),
                        ('swa_sink', [8]), ('ssd_conv_w', [32, 128]), ('ssd_conv_b', [8, 128]), ('ssd_dt_bias', [8]),
                        ('ssd_a_log', [8]), ('ssd_d', [8]), ('ssd_norm_g', [512]), ('hc', [128, NHC])]:
            I[nm] = self.din(nm, shp)
        self.I = I
        O = {}
        O['yp'] = self.dout('yp', [2, SEQ, D])
        O['ys'] = self.dout('ys', [64, D])
        O['o_s5re'] = self.dout('o_s5re', [3, 32, 64])
        O['o_s5im'] = self.dout('o_s5im', [3, 32, 64])
        O['o_gla'] = self.dout('o_gla', [3, 256, 128])
        O['o_pk'] = self.dout('o_pk', [2, 128, 128])
        O['o_pv'] = self.dout('o_pv', [2, 128, 128])
        O['o_sk'] = self.dout('o_sk', [64, 128])
        O['o_sv'] = self.dout('o_sv', [64, 128])
        O['o_ssd'] = self.dout('o_ssd', [3, 512, 128])
        O['o_ssdconv'] = self.dout('o_ssdconv', [3, 3, 1024])
        O['o_ffnconv'] = self.dout('o_ffnconv', [3, 2, 2, 2 * DFF])
        self.O = O
        W = {}
        W['up'] = self.dscr('w_up_s', [2, D, 2 * DFF], BF16)
        W['dn'] = self.dscr('w_dn_s', [2, DFF, D], BF16)
        W['evin'] = self.dscr('w_evin_s', [D, EVEN_IN], BF16)
        W['evout'] = self.dscr('w_evout_s', [D, D], BF16)
        W['odin'] = self.dscr('w_odin_s', [D, ODD_IN], BF16)
        W['odout'] = self.dscr('w_odout_s', [D, D], BF16)
        W['s5t0'] = self.dscr('w_s5t0', [128, 32 * 128], BF16)
        W['s5win'] = self.dscr('w_s5win', [128, 32 * 128], BF16)
        W['s5wout'] = self.dscr('w_s5wout', [64, 2 * 32 * 128], BF16)
        W['biasd'] = self.dscr('w_biasd', [8, 3, 64 * 128], F32)
        W['evin_lo'] = self.dscr('w_evin_lo', [D, 512], BF16)
        self.W = W

        with ExitStack() as st:
            self.st = st
            self.fw = FW(nc, st)
            self.alloc()
            self.prologue()
            for job in [int(x) for x in os.environ.get('KJOBS', '0,1,2').split(',')]:
                if self.stage >= 6:
                    self.run_job(job)
            self.fw.finish_waits('sp')
            self.fw.replay()
        return nc

    def alloc(self):
        nc = self.nc
        self.PB = [self.st.enter_context(nc.psum_tensor(f'pb{i}', [128, 512], F32)) for i in range(8)]
        self.HC = self.sb('HC', [128, NHC], F32)
        self.IDf = self.HC[:, HC_IDENT:HC_IDENT + 128]
        self.IDb = self.sb('IDb', [128, 128], BF16)
        self.ONESb = self.sb('ONESb', [128, 128], BF16)
        self.MBb = self.sb('MBb', [128, 128], BF16)
        self.STG = [self.sb(f'STG{i}', [128, 128], F32) for i in range(2)]
        self.PVN = 1100
        self.PV = self.sb('PV', [128, self.PVN], F32)
        self.BC = self.sb('BC', [128, 640], F32)
        self.XRES = self.sb('XRES', [128, 8, 512], F32)
        self.ACT8 = self.sb('ACT8', [128, 8, 512], BF16)
        self.WS = [self.sb(f'WS{i}', [128, 4096], BF16) for i in range(4)]
        self.ws_rr = 0
        self.WG2 = self.sb('WG2', [16, 256], BF16)
        self.WGL = self.sb('WGL', [128, 8, 16], BF16)
        self.WGLU = self.sb('WGLU', [128, 4, 512], BF16)
        self.A1 = self.sb('A1', [64, 2, 32], F32)
        self.A2 = self.sb('A2', [64, 2, 32], F32)
        self.S3 = self.sb('S3', [64, 3, 32], F32)
        self.GS = self.sb('GS', [128, 2, 128], F32)
        self.GSH = self.sb('GSH', [128, 9, 2, 128], BF16)
        self.FT = self.sb('FT', [128, 2, 44, 2], F32)
        self.FCORR = self.sb('FCORR', [128, 44, 2], F32)
        self.FTMP = self.sb('FTMP', [128, 44, 2], F32)
        self.HNLO = self.sb('HNLO', [128, 8, 512], BF16)
        self.ST = self.sb('ST', [128, 512], F32)
        self.STBH = self.sb('STBH', [128, 9, 512], BF16)
        self.CT = self.sb('CT', [128, 8, 4], F32)
        self.CCORR = self.sb('CCORR', [128, 8, 3], F32)
        self.CTMP = self.sb('CTMP', [128, 8, 3], F32)
        self.KWIN = self.sb('KWIN', [128, 640], BF16)
        self.VWP = self.sb('VWP', [128, 5, 128], BF16)
        self.BIE = self.sb('BIE', [128, 2, 512], F32)
        self.BIO = self.sb('BIO', [128, 2, 512], F32)
        self.WDT = self.sb('WDT', [128, 8, 8], BF16)
        self.T5T = self.sb('T5T', [32, 8], F32)
        self.WROW = self.sb('WROW', [8, 384], F32)
        self.SCRN = 18432
        self.SCR = self.sb('SCR', [128, self.SCRN], F32)
        c = 0
        self.pv = {}
        for nm, n in [('n1g', 16), ('n2g', 16), ('cfm', 24), ('csilu', 24), ('mod', 288), ('gs1', 48), ('gs2', 48),
                      ('fcw', 264), ('fcb', 88), ('bglu', 4), ('negbg', 2), ('glang', 1), ('eps', 1), ('one', 1),
                      ('scw', 32), ('scb', 8), ('qng', 1), ('kng', 1), ('sinkexp', 4), ('dvec', 32), ('tmp', 64)]:
            self.pv[nm] = c
            c += n
        assert c <= self.PVN, c

    def pvc(self, nm, i=0, n=1):
        c = self.pv[nm] + i
        return self.PV[:, c:c + n]

    def prologue(self):
        I, W = self.I, self.W
        fw = self.fw
        self.DMA('sp', self.HC[:], I['hc'], ['in_hc'], ['hc'])
        self.CP('dve', self.IDb[:], self.IDf, ['hc'], ['idb'])
        self.MEMSET('pool', self.ONESb[:], 1.0, ['onesb'])
        self.CP('dve', self.MBb[:], self.HC[:, HC_MB:HC_MB + 128], ['hc'], ['mbb'])
        self.MEMSET('pool', self.pvc('eps'), EPS, ['pv_eps'])
        self.MEMSET('pool', self.pvc('one'), 1.0, ['pv_one'])
        if self.stage < 1:
            return
        def cast_mixer(pairs):
            for nm, srcn in pairs:
                for r0 in range(0, D, 256):
                    self.DMA('pool', W[nm][r0:r0 + 256, :], I[srcn][r0:r0 + 256, :], ['in_w'], ['w_' + nm])

        def cast_ffn(l):
            for r0 in range(0, D, 128):
                self.DMA('pool', W['up'][l, r0:r0 + 128, :], I['ffn_w_up'][l, r0:r0 + 128, :], ['in_w'], ['w_up'])
            for r0 in range(0, DFF, 256):
                self.DMA('pool', W['dn'][l, r0:r0 + 256, :], I['ffn_w_down'][l, r0:r0 + 256, :], ['in_w'], ['w_dn'])
        cast_mixer([('evin', 'ev_w_in'), ('evout', 'ev_w_out')])
        cast_ffn(0)
        cast_mixer([('odin', 'od_w_in'), ('odout', 'od_w_out')])
        cast_ffn(1)
        if self.stage < 2:
            return
        for r in range(8):
            wf = self.scr(0, 2048, F32)
            hi = self.scr(2048, 1024, BF16)
            d32 = self.scr(4096, 2048, F32)
            lo = self.scr(6144, 1024, BF16)
            self.DMA('sp', wf.ap[:, 0:512], I['ev_w_in'][r * 128:(r + 1) * 128, 512:1024], ['in_w'], wf.keys)
            self.CP('dve', hi.ap[:, 0:512], wf.ap[:, 0:512], wf.keys, hi.keys)
            self.TT('dve', d32.ap[:, 0:512], wf.ap[:, 0:512], hi.ap[:, 0:512], ALU.subtract, wf.keys + hi.keys, d32.keys)
            self.CP('act', lo.ap[:, 0:512], d32.ap[:, 0:512], d32.keys, lo.keys)
            self.DMA('sp', W['evin_lo'][r * 128:(r + 1) * 128, :], lo.ap[:, 0:512], lo.keys, ['w_evin_lo'])
        self.DMA('pool', self.WG2[:], I['gla_w_gate2'], ['in_w'], ['wg2'])
        self.DMA('pool', self.WGL[:], I['ev_w_in'][:, 1536:1552].rearrange("(k p) n -> p k n", p=128), ['in_w'], ['wgl'])
        self.DMA('pool', self.WGLU[:], I['s5_w_glu'].rearrange("(k p) n -> p k n", p=128), ['in_w'], ['wglu'])
        if self.stage < 3:
            return
        self.fm_load(self.pvc('n1g', 0, 16), ['pv_n1g'], I['norm1_g'], 16, 128)
        self.fm_load(self.pvc('n2g', 0, 16), ['pv_n2g'], I['norm2_g'], 16, 128)
        self.fm_load(self.pvc('cfm', 0, 24), ['pv_cfm'], I['cvec'], 24, 128)
        for l in range(2):
            self.fm_load(self.pvc('fcw', l * 132, 128), ['pv_fcw'], I['ffn_conv_w'][l, 0:128, :], 128, 128)
            self.fm_load(self.pvc('fcw', l * 132 + 128, 4), ['pv_fcw'], I['ffn_conv_w'][l, 128:132, :], 4, 128)
            self.fm_load(self.pvc('fcb', l * 44, 44), ['pv_fcb'], I['ffn_conv_b'][l], 44, 128)
        self.fm_load(self.pvc('bglu', 0, 4), ['pv_bglu'], I['s5_b_glu'], 4, 128)
        self.fm_load(self.pvc('negbg', 0, 2), ['pv_negbg'], I['gla_b_gate'], 2, 128)
        self.TS('dve', self.pvc('negbg', 0, 2), self.pvc('negbg', 0, 2), -1.0, None, ALU.mult, None, ['pv_negbg'], ['pv_negbg'])
        self.DMA('sp', self.pvc('glang'), I['gla_norm_g'].rearrange("(p o) -> p o", o=1), ['in_w'], ['pv_glang'], nonc=True)
        if self.stage < 4:
            return
        self.ACT(self.pvc('csilu', 0, 24), self.pvc('cfm', 0, 24), AF.Silu, ['pv_cfm'], ['pv_csilu'])
        csl = self.PV[:, self.pv['csilu']:self.pv['csilu'] + 24].rearrange("p (j k) -> p k j", k=8)
        wm = self.scr(0, 8 * 512 * 4, F32)
        wmv = wm.ap.rearrange("p (k n) -> p k n", k=8)
        brow = self.scr(16384, 512 * 4, F32, parts=1)
        onesrow = self.scr(16384 + 2048, 16, F32, parts=1)
        self.MEMSET('dve', onesrow.ap[0:1, 0:3], 1.0, onesrow.keys)
        for l in range(2):
            for cb in range(12):
                self.DMA('sp', wmv, I['w_mod'][l, :, cb * 512:(cb + 1) * 512].rearrange("(k p) n -> p k n", p=128),
                         ['in_w'], wm.keys)
                self.DMA('sp', brow.ap[0:1, 0:512], I['b_mod'][l:l + 1, cb * 512:(cb + 1) * 512], ['in_w'], brow.keys)
                pb, pk = self.bank()
                for mc in range(4):
                    for k in range(8):
                        self.MM(pb[:, mc * 4:mc * 4 + 3], wmv[:, k, mc * 128:(mc + 1) * 128], csl[:, k, :],
                                k == 0, False, wm.keys + ['pv_csilu'], [pk])
                    self.MM(pb[:, mc * 4:mc * 4 + 3], brow.ap[0:1, mc * 128:(mc + 1) * 128], onesrow.ap[0:1, 0:3],
                            False, True, brow.keys + onesrow.keys, [pk])
                for job in range(3):
                    dst = self.pvc('mod', job * 96 + l * 48 + cb * 4, 4)
                    self.CP('dve', dst, pb[:, job:job + 13:4], [pk], ['pv_mod'])
        for job in range(3):
            for l in range(2):
                for which, gname, scoff in [(0, 'n1g', 8), (1, 'n2g', 32)]:
                    dst = self.pvc('gs1' if which == 0 else 'gs2', (job * 2 + l) * 8, 8)
                    sc = self.pvc('mod', job * 96 + l * 48 + scoff, 8)
                    g = self.pvc(gname, l * 8, 8)
                    self.STT(dst, sc, 1.0, g, ALU.add, ALU.mult, ['pv_mod', 'pv_' + gname], ['pv_gs'])
        if self.stage < 5:
            return
        self.s5_prologue()
        if self.do_l1:
            self.l1_prologue()

    def modv(self, job, l, which, k0=0, n=8):
        return self.pvc('mod', job * 96 + l * 48 + which * 8 + k0, n)

    def s5_prologue(self):
        I, W = self.I, self.W
        off = [0]

        def A(n, parts=64):
            r = self.scr(off[0], n * 4, F32, parts=parts)
            off[0] += n * 4
            return r
        are, aim, ldt = A(32), A(32), A(32)
        zr, zi, wr, wi, t1, t2, t3 = A(32), A(32), A(32), A(32), A(32), A(32), A(32)
        LR, LI = A(9 * 32), A(9 * 32)
        gr, gi = A(32), A(32)
        Bre, Bim = A(512), A(512)
        BBr, BBi = A(512), A(512)
        Cre, Cim = A(512), A(512)
        GC = 8
        WTr, WTi = A(GC * 128), A(GC * 128)
        Qr, Qi = A(GC * 128), A(GC * 128)
        Qsr, Qsi = A(GC * 128), A(GC * 128)
        tA, tB = A(GC * 128), A(GC * 128)
        stgb = self.scr(off[0], 2 * 128 * 2, BF16)
        off[0] += 512
        tmpf = self.scr(off[0], 512, F32)
        off[0] += 512
        assert off[0] <= self.SCRN * 4, off[0]
        dve = 'dve'

        def tt(o, a, b, op):
            self.TT(dve, o.ap if isinstance(o, Reg) else o[0], a.ap if isinstance(a, Reg) else a[0],
                    b.ap if isinstance(b, Reg) else b[0], op,
                    (a.keys if isinstance(a, Reg) else a[1]) + (b.keys if isinstance(b, Reg) else b[1]),
                    o.keys if isinstance(o, Reg) else o[1])

        def V(reg, ap):
            return (ap, reg.keys)
        self.fm_load(are.ap, are.keys, I['s5_a_re'], 32, 64)
        self.fm_load(aim.ap, aim.keys, I['s5_a_im'], 32, 64)
        self.DMA('sp', ldt.ap, I['s5_log_dt'].partition_broadcast(64), ['in_w'], ldt.keys)
        self.DMA('sp', Bre.ap.rearrange("p (g h) -> p g h", h=16), I['s5_b_re'].rearrange("g p h -> p g h"), ['in_w'], Bre.keys)
        self.DMA('sp', Bim.ap.rearrange("p (g h) -> p g h", h=16), I['s5_b_im'].rearrange("g p h -> p g h"), ['in_w'], Bim.keys)
        for q4 in range(4):
            self.fm_load(Cre.ap[:, q4 * 128:(q4 + 1) * 128], Cre.keys, I['s5_c_re'][q4 * 128:(q4 + 1) * 128, :], 128, 64)
            self.fm_load(Cim.ap[:, q4 * 128:(q4 + 1) * 128], Cim.keys, I['s5_c_im'][q4 * 128:(q4 + 1) * 128, :], 128, 64)
        self.ACT(ldt.ap, ldt.ap, AF.Exp, ldt.keys, ldt.keys)
        self.STT(zr.ap, are.ap, 1.0 / 256.0, ldt.ap, ALU.mult, ALU.mult, are.keys + ldt.keys, zr.keys)
        self.STT(zi.ap, aim.ap, 1.0 / 256.0, ldt.ap, ALU.mult, ALU.mult, aim.keys + ldt.keys, zi.keys)

        def cmul(or_, oi_, ar, ai, br, bi):
            tt(t1, ar, br, ALU.mult)
            tt(t2, ai, bi, ALU.mult)
            tt(or_, t1, t2, ALU.subtract)
            tt(t1, ar, bi, ALU.mult)
            tt(t2, ai, br, ALU.mult)
            tt(oi_, t1, t2, ALU.add)
        hr, hi_ = wr, wi
        self.TS(dve, hr.ap, zr.ap, 0.2, 1.0, ALU.mult, ALU.add, zr.keys, hr.keys)
        self.TS(dve, hi_.ap, zi.ap, 0.2, None, ALU.mult, None, zi.keys, hi_.keys)
        nr = t3
        for dv in (4.0, 3.0, 2.0):
            cmul(nr, gi, zr, zi, hr, hi_)
            self.TS(dve, hr.ap, nr.ap, 1.0 / dv, 1.0, ALU.mult, ALU.add, nr.keys, hr.keys)
            self.TS(dve, hi_.ap, gi.ap, 1.0 / dv, None, ALU.mult, None, gi.keys, hi_.keys)
        cmul(nr, gi, zr, zi, hr, hi_)
        self.CP(dve, wr.ap, nr.ap, nr.keys, wr.keys)
        self.CP(dve, wi.ap, gi.ap, gi.keys, wi.keys)
        for _ in range(8):
            self.TS(dve, zr.ap, wr.ap, 2.0, None, ALU.add, None, wr.keys, zr.keys)
            cmul(nr, gi, wr, wi, zr, wi)
            self.CP(dve, wr.ap, nr.ap, nr.keys, wr.keys)
            self.CP(dve, wi.ap, gi.ap, gi.keys, wi.keys)
        LRv = LR.ap.rearrange("p (n g) -> p n g", g=32)
        LIv = LI.ap.rearrange("p (n g) -> p n g", g=32)
        self.MEMSET(dve, LRv[:, 0, :], 1.0, LR.keys)
        self.MEMSET(dve, LIv[:, 0, :], 0.0, LI.keys)
        self.TS(dve, LRv[:, 1, :], wr.ap, 1.0, None, ALU.add, None, wr.keys, LR.keys)
        self.CP(dve, LIv[:, 1, :], wi.ap, wi.keys, LI.keys)
        for n in range(1, 8):
            cmul(V(LR, LRv[:, n + 1, :]), V(LI, LIv[:, n + 1, :]), V(LR, LRv[:, n, :]), V(LI, LIv[:, n, :]),
                 V(LR, LRv[:, 1, :]), V(LI, LIv[:, 1, :]))
        tt(t3, are, are, ALU.mult)
        tt(zr, aim, aim, ALU.mult)
        tt(t3, t3, zr, ALU.add)
        self.fw.op(dve, lambda e: e.reciprocal(out=t3.ap, in_=t3.ap), t3.keys, t3.keys)
        self.TS(dve, zi.ap, aim.ap, -1.0, None, ALU.mult, None, aim.keys, zi.keys)
        cmul(gr, gi, wr, wi, are, zi)
        tt(gr, gr, t3, ALU.mult)
        tt(gi, gi, t3, ALU.mult)
        grb = (gr.ap.unsqueeze(2).broadcast_to([64, 32, 16]), gr.keys)
        gib = (gi.ap.unsqueeze(2).broadcast_to([64, 32, 16]), gi.keys)
        v3 = lambda r: (r.ap.rearrange("p (g h) -> p g h", h=16), r.keys)
        tA3 = (tA.ap[:, 0:512].rearrange("p (g h) -> p g h", h=16), tA.keys)
        tB3 = (tB.ap[:, 0:512].rearrange("p (g h) -> p g h", h=16), tB.keys)
        tt(tA3, grb, v3(Bre), ALU.mult)
        tt(tB3, gib, v3(Bim), ALU.mult)
        tt(v3(BBr), tA3, tB3, ALU.subtract)
        tt(tA3, grb, v3(Bim), ALU.mult)
        tt(tB3, gib, v3(Bre), ALU.mult)
        tt(v3(BBi), tA3, tB3, ALU.add)
        l8r = (LRv[:, 8, :], LR.keys)
        l8i = (LIv[:, 8, :], LI.keys)
        tt(t1, l8r, l8r, ALU.mult)
        tt(t2, l8i, l8i, ALU.mult)
        tt(t1, t1, t2, ALU.add)
        self.fw.op(dve, lambda e: e.reciprocal(out=t1.ap, in_=t1.ap), t1.keys, t1.keys)
        tt(zr, l8r, t1, ALU.mult)
        tt(zi, l8i, t1, ALU.mult)
        self.CP(dve, self.A1[:, 0, :], LRv[:, 8, :], LR.keys, ['a12'])
        self.CP(dve, self.A1[:, 1, :], LRv[:, 8, :], LR.keys, ['a12'])
        self.TS(dve, self.A2[:, 0, :], LIv[:, 8, :], -1.0, None, ALU.mult, None, LI.keys, ['a12'])
        self.CP(dve, self.A2[:, 1, :], LIv[:, 8, :], LI.keys, ['a12'])
        dsrc = I['s5_d'].rearrange("(g h) -> h g", h=16)
        for s in range(8):
            self.DMA('sp', self.PV[s * 16:(s + 1) * 16, self.pv['dvec']:self.pv['dvec'] + 32], dsrc, ['in_w'], ['pv_dvec'], nonc=True)
        wout_v = W['s5wout'].rearrange("p (c g m) -> p c g m", c=2, g=32)
        win_v = W['s5win'].rearrange("p (g m) -> p g m", g=32)
        t0_v = W['s5t0'].rearrange("p (g m) -> p g m", g=32)
        S5M = self.HC[:, HC_S5M:HC_S5M + 128]
        S5E = self.HC[:, HC_S5E:HC_S5E + 128]
        sb64 = stgb.ap[0:64, :]
        for gc in range(32 // GC):
            g0 = gc * GC
            gs_ = slice(g0, g0 + GC)
            c3 = lambda r: (r.ap.rearrange("p (g h) -> p g h", h=16)[:, gs_, :], r.keys)
            tAc = (tA.ap[:, 0:GC * 16].rearrange("p (g h) -> p g h", h=16), tA.keys)
            tBc = (tB.ap[:, 0:GC * 16].rearrange("p (g h) -> p g h", h=16), tB.keys)
            WTr4 = WTr.ap.rearrange("p (g s h) -> p g s h", s=8, h=16)
            WTi4 = WTi.ap.rearrange("p (g s h) -> p g s h", s=8, h=16)
            for s in range(8):
                lr = (LRv[:, 7 - s, gs_].unsqueeze(2).broadcast_to([64, GC, 16]), LR.keys)
                li = (LIv[:, 7 - s, gs_].unsqueeze(2).broadcast_to([64, GC, 16]), LI.keys)
                tt(tAc, lr, c3(BBr), ALU.mult)
                tt(tBc, li, c3(BBi), ALU.mult)
                tt((WTr4[:, :, s, :], WTr.keys), tAc, tBc, ALU.subtract)
                tt(tAc, lr, c3(BBi), ALU.mult)
                tt(tBc, li, c3(BBr), ALU.mult)
                tt((WTi4[:, :, s, :], WTi.keys), tAc, tBc, ALU.add)
            C4 = lambda r: (r.ap.rearrange("p (g h) -> p g h", h=16)[:, gs_, :].unsqueeze(3).broadcast_to([64, GC, 16, 8]), r.keys)
            L4 = lambda r, rv: (rv[:, 1:9, gs_].rearrange("p n g -> p g n").unsqueeze(2).broadcast_to([64, GC, 16, 8]), r.keys)
            q4 = lambda r: (r.ap.rearrange("p (g h l) -> p g h l", h=16, l=8), r.keys)
            tt(q4(tA), C4(Cre), L4(LR, LRv), ALU.mult)
            tt(q4(tB), C4(Cim), L4(LI, LIv), ALU.mult)
            tt(q4(Qr), q4(tA), q4(tB), ALU.subtract)
            tt(q4(tA), C4(Cre), L4(LI, LIv), ALU.mult)
            tt(q4(tB), C4(Cim), L4(LR, LRv), ALU.mult)
            tt(q4(Qi), q4(tA), q4(tB), ALU.add)
            zrb = (zr.ap[:, gs_].unsqueeze(2).broadcast_to([64, GC, 128]), zr.keys)
            zib = (zi.ap[:, gs_].unsqueeze(2).broadcast_to([64, GC, 128]), zi.keys)
            g3 = lambda r: (r.ap.rearrange("p (g m) -> p g m", m=128), r.keys)
            tt(g3(tA), g3(Qr), zrb, ALU.mult)
            tt(g3(tB), g3(Qi), zib, ALU.mult)
            tt(g3(Qsr), g3(tA), g3(tB), ALU.add)
            tt(g3(tA), g3(Qr), zib, ALU.mult)
            tt(g3(tB), g3(Qi), zrb, ALU.mult)
            tt(g3(Qsi), g3(tA), g3(tB), ALU.subtract)
            for gg in range(GC):
                g = g0 + gg
                sl = slice(gg * 128, (gg + 1) * 128)
                self.CP('act', sb64[:, 0:128], Qr.ap[:, sl], Qr.keys, stgb.keys)
                self.ACT(sb64[:, 128:256], Qi.ap[:, sl], AF.Copy, Qi.keys, stgb.keys, scale=-1.0)
                self.DMA('sp', wout_v[:, 0, g, :], sb64[:, 0:128], stgb.keys, ['w_s5wout'])
                self.DMA('sp', wout_v[:, 1, g, :], sb64[:, 128:256], stgb.keys, ['w_s5wout'])
                pb, pk = self.bank()
                self.TR(pb[:, 0:64], WTr.ap[:, sl], self.IDf[0:64, 0:64], WTr.keys + ['hc'], [pk])
                self.TR(pb[:, 64:128], WTi.ap[:, sl], self.IDf[0:64, 0:64], WTi.keys + ['hc'], [pk])
                self.CP('act', stgb.ap[:, 0:128], pb[:, 0:128], [pk], stgb.keys)
                self.DMA('sp', win_v[:, g, :], stgb.ap[:, 0:128], stgb.keys, ['w_s5win'])
                pb2, pk2 = self.bank()
                self.MM(pb2[:, 0:128], WTr.ap[:, sl], Qsr.ap[:, sl], True, False, WTr.keys + Qsr.keys, [pk2])
                self.MM(pb2[:, 0:128], WTi.ap[:, sl], Qsi.ap[:, sl], False, True, WTi.keys + Qsi.keys, [pk2])
                self.TT('dve', tmpf.ap[:, 0:128], pb2[:, 0:128], S5M, ALU.mult, [pk2, 'hc'], tmpf.keys)
                self.STT(stgb.ap[:, 128:256], S5E, self.pvc('dvec', g, 1), tmpf.ap[:, 0:128], ALU.mult, ALU.add,
                         ['hc', 'pv_dvec'] + tmpf.keys, stgb.keys)
                self.DMA('sp', t0_v[:, g, :], stgb.ap[:, 128:256], stgb.keys, ['w_s5t0'])

    def l1_prologue(self):
        I, W = self.I, self.W
        BC = self.BC
        self.DMA('sp', BC[:, 0:8], I['ssd_dt_bias'].partition_broadcast(128), ['in_w'], ['bc'])
        self.DMA('sp', BC[:, 8:16], I['ssd_a_log'].partition_broadcast(128), ['in_w'], ['bc'])
        self.DMA('sp', BC[:, 16:24], I['ssd_d'].partition_broadcast(128), ['in_w'], ['bc'])
        self.DMA('sp', BC[:, 32:544], I['ssd_norm_g'].partition_broadcast(128), ['in_w'], ['bc'])
        self.ACT(BC[:, 8:16], BC[:, 8:16], AF.Exp, ['bc'], ['bc'])
        self.TS('dve', BC[:, 8:16], BC[:, 8:16], -1.0, None, ALU.mult, None, ['bc'], ['bc'])
        self.fm_load(self.pvc('scw', 0, 32), ['pv_scw'], I['ssd_conv_w'], 32, 128)
        self.fm_load(self.pvc('scb', 0, 8), ['pv_scb'], I['ssd_conv_b'], 8, 128)
        for half in range(2):
            self.DMA('sp', self.PV[half * 64:(half + 1) * 64, self.pv['qng']:self.pv['qng'] + 1],
                     I['swa_q_norm'].rearrange("(p o) -> p o", o=1), ['in_w'], ['pv_qng'], nonc=True)
            self.DMA('sp', self.PV[half * 64:(half + 1) * 64, self.pv['kng']:self.pv['kng'] + 1],
                     I['swa_k_norm'].rearrange("(p o) -> p o", o=1), ['in_w'], ['pv_kng'], nonc=True)
            self.DMA('sp', self.PV[half * 64:(half + 1) * 64, self.pv['sinkexp']:self.pv['sinkexp'] + 4],
                     I['swa_sink'][half * 4:(half + 1) * 4].partition_broadcast(64), ['in_w'], ['pv_sink'])
        self.TS('dve', self.pvc('qng'), self.pvc('qng'), 0.125, None, ALU.mult, None, ['pv_qng'], ['pv_qng'])
        self.ACT(self.pvc('sinkexp', 0, 4), self.pvc('sinkexp', 0, 4), AF.Exp, ['pv_sink'], ['pv_sink'])
        self.DMA('pool', self.WDT[:], I['od_w_in'][:, 2304:2312].rearrange("(k p) n -> p k n", p=128), ['in_w'], ['wdt'])
        self.DMA('sp', self.T5T[:], I['t5_bias'], ['in_w'], ['t5t'])
        pb, pk = self.bank()
        self.MM(pb[0:8, 0:384], self.T5T[:], self.HC[0:32, HC_OHW:HC_OHW + 384], True, True, ['t5t', 'hc'], [pk])
        self.CP('dve', self.WROW[:], pb[0:8, 0:384], [pk], ['wrow'])
        ps_ = self.WROW[:].ap[0][0]
        for h in range(8):
            for kb in range(3):
                srcap = bass.AP(self.WROW[:].tensor, h * ps_ + kb * 128, [[ps_, 1], [0, 64], [1, 128]])
                self.DMA('sp', W['biasd'][h:h + 1, kb, :].rearrange("o (r m) -> o r m", m=128), srcap, ['wrow'], ['w_biasd'])
        self.MEMSET('pool', self.BIE[:], 0.0, ['bie'])
        self.MEMSET('pool', self.BIO[:], 0.0, ['bio'])
        bt = W['biasd'].tensor

        def skew(h, kb):
            return bass.AP(bt, (h * 3 + kb) * 64 * 128, [[127, 64], [1, 64]])
        for kvh in range(2):
            for g in range(4):
                h = kvh * 4 + g
                for (T_, nm, p0, c0, kb) in [(self.BIE, 'bie', 0, 0, 0), (self.BIE, 'bie', 64, 0, 1), (self.BIE, 'bie', 0, 256, 2),
                                             (self.BIO, 'bio', 0, 0, 1), (self.BIO, 'bio', 64, 0, 2), (self.BIO, 'bio', 64, 256, 0)]:
                    self.DMA('sp', T_[p0:p0 + 64, kvh, c0 + g * 64:c0 + (g + 1) * 64], skew(h, kb), ['w_biasd'], [nm])

    def ws_load(self, views_and_srcs, rkeys):
        i = self.ws_rr
        self.ws_rr = (self.ws_rr + 1) % 4
        t = self.WS[i]
        for mk, src in views_and_srcs:
            self.DMA('sp', mk(t), src, rkeys, [f'ws{i}'])
        return t, f'ws{i}'

    def run_job(self, job):
        SEQ = self.SEQ
        if job < 2:
            T = 512
            ntile = SEQ // T
        else:
            T = 64
            ntile = 1
        self.job = job
        self.sub = int(os.environ.get('KSUB', '99'))
        self.fine = int(os.environ.get('KFINE', '99'))
        self.job_init(job)
        for ti in range(ntile):
            last = (ti == ntile - 1)
            self.tile_load(job, ti, T)
            if self.stage >= 7:
                self.layer0(job, ti, T, last)
            if self.stage >= 8:
                self.ffn(job, 0, T, last)
            if self.do_l1 and self.stage >= 9:
                self.layer1(job, ti, T, last)
            if self.do_l1 and self.stage >= 10:
                self.ffn(job, 1, T, last)
            self.tile_store(job, ti, T)
        self.job_finish(job)

    def job_init(self, job):
        I = self.I
        if job < 2:
            self.MEMSET('pool', self.S3[:], 0.0, ['s3'])
            self.MEMSET('pool', self.GS[:], 0.0, ['gs'])
            self.MEMSET('pool', self.FT[:], 0.0, ['ft'])
        else:
            self.fm_load(self.S3[:, 0, :], ['s3'], I['st_s5re'], 32, 64)
            self.fm_load(self.S3[:, 1, :], ['s3'], I['st_s5im'], 32, 64)
            self.CP('dve', self.S3[:, 2, :], self.S3[:, 0, :], ['s3'], ['s3'])
            self.DMA('sp', self.GS[:], I['st_gla'].rearrange("(hp q) v -> q hp v", q=128), ['in_w'], ['gs'])
            for l in range(2):
                tmp = self.scr(0, 88 * 4, F32)
                self.fm_load(tmp.ap[:, 0:88], tmp.keys, I['st_ffnconv'][l], 88, 128)
                self.CP('dve', self.FT[:, l, :, :], tmp.ap[:, 0:88].rearrange("p (r c) -> p c r", r=2), tmp.keys, ['ft'])
        if self.do_l1:
            self.l1_job_init(job)

    def l1_job_init(self, job):
        I = self.I
        self.MEMSET('pool', self.KWIN[:], 0.0, ['kwin'])
        self.MEMSET('pool', self.VWP[:], 0.0, ['vwp'])
        if job < 2:
            self.MEMSET('pool', self.ST[:], 0.0, ['st'])
            self.MEMSET('pool', self.CT[:], 0.0, ['ct'])
        else:
            for q4 in range(4):
                self.fm_load(self.ST[:, q4 * 128:(q4 + 1) * 128], ['st'], I['st_ssd'][q4 * 128:(q4 + 1) * 128, :], 128, 128)
            tmp = self.scr(0, 24 * 4, F32)
            self.fm_load(tmp.ap[:, 0:24], tmp.keys, I['st_ssdconv'], 24, 128)
            self.CP('dve', self.CT[:, :, 1:4], tmp.ap[:, 0:24].rearrange("p (r c) -> p c r", r=3), tmp.keys, ['ct'])
            tk = self.scr(2048, 128 * 4, F32)
            self.fm_load(tk.ap[:, 0:128], tk.keys, I['ck'], 128, 128)
            self.CP('dve', self.KWIN[:, 0:128], tk.ap[:, 0:128], tk.keys, ['kwin'])
            self.DMA('pool', self.VWP[:, 0, :], I['cv'], ['in_w', 'vwp'], ['vwp'])

    def job_finish(self, job):
        O = self.O
        self.tm_store(O['o_s5re'][job], self.S3[:, 0, :], ['s3'], 32, 64)
        self.tm_store(O['o_s5im'][job], self.S3[:, 1, :], ['s3'], 32, 64)
        self.DMA('pool', O['o_gla'][job].rearrange("(hp q) v -> q hp v", q=128), self.GS[:], ['gs'], ['out'])
        if self.do_l1:
            self.l1_job_finish(job)

    def l1_job_finish(self, job):
        O = self.O
        for q4 in range(4):
            self.tm_store(O['o_ssd'][job, q4 * 128:(q4 + 1) * 128, :], self.ST[:, q4 * 128:(q4 + 1) * 128], ['st'], 128, 128)

    def tile_load(self, job, ti, T):
        I = self.I
        xt = self.scr(0, 4 * 1024 * 4, F32)
        xtv = xt.ap.rearrange("p (b f) -> p b f", b=4)
        nb = max(1, T // 128)
        rows = min(T, 128)
        for b in range(nb):
            if job < 2:
                src = I['xp'][job, ti * T + b * 128: ti * T + b * 128 + rows, :]
            else:
                src = I['xs'][0:64, :]
            self.DMA('sp', xtv[0:rows, b, :], src, ['in_x'], xt.keys)
        for k in range(8):
            pb, pk = self.bank()
            for b in range(nb):
                self.TR(pb[:, b * 128:b * 128 + rows], xtv[0:rows, b, k * 128:(k + 1) * 128], self.IDf[0:rows, 0:rows],
                        xt.keys + ['hc'], [pk])
            self.CP('dve' if k % 2 == 0 else 'act', self.XRES[:, k, 0:T], pb[:, 0:T], [pk], [('xres', k)])

    def tile_store(self, job, ti, T):
        O = self.O
        xt = self.scr(0, 4 * 1024 * 4, F32)
        xtv = xt.ap.rearrange("p (b f) -> p b f", b=4)
        nb = max(1, T // 128)
        rows = min(T, 128)
        for b in range(nb):
            for k2 in range(2):
                pb, pk = self.bank()
                for kk in range(4):
                    k = k2 * 4 + kk
                    self.TR(pb[0:rows, kk * 128:(kk + 1) * 128], self.XRES[:, k, b * 128:b * 128 + rows], self.IDf,
                            [('xres', k), 'hc'], [pk])
                self.CP('dve' if k2 == 0 else 'act', xtv[0:rows, b, k2 * 512:(k2 + 1) * 512], pb[0:rows, :], [pk], xt.keys)
            if job < 2:
                dst = O['yp'][job, ti * T + b * 128: ti * T + b * 128 + rows, :]
            else:
                dst = O['ys'][0:64, :]
            self.DMA('pool', dst, xtv[0:rows, b, :], xt.keys, ['out'])

    def norm(self, job, l, which, T, want_lo=False):
        xres = self.XRES
        sq = self.scr(0, 8 * 512 * 2, BF16)
        sqv = sq.ap.rearrange("p (k t) -> p k t", k=8)
        rs = self.scr(8192, 2048, F32)
        tmpn = [self.scr(10240 + i * 2048, 2048, F32) for i in range(2)]
        for k in range(8):
            self.ACT(sqv[:, k, 0:T], xres[:, k, 0:T], AF.Square, [('xres', k)], sq.keys)
        pb, pk = self.bank()
        for k in range(8):
            self.MM(pb[:, 0:T], self.ONESb[:], sqv[:, k, 0:T], k == 0, k == 7, sq.keys + ['onesb'], [pk])
        self.ACT(rs.ap[:, 0:T], pb[:, 0:T], AF.Sqrt, [pk, 'pv_eps'], rs.keys, bias=self.pvc('eps'), scale=1.0 / D)
        self.fw.op('dve', lambda e: e.reciprocal(out=rs.ap[:, 0:T], in_=rs.ap[:, 0:T]), rs.keys, rs.keys)
        gsn = 'gs1' if which == 0 else 'gs2'
        for k in range(8):
            tm = tmpn[k % 2]
            gs = self.pvc(gsn, (job * 2 + l) * 8 + k, 1)
            sh = self.modv(job, l, 0 if which == 0 else 3, k, 1)
            self.STT(tm.ap[:, 0:T], xres[:, k, 0:T], gs, rs.ap[:, 0:T], ALU.mult, ALU.mult,
                     [('xres', k), 'pv_gs'] + rs.keys, tm.keys)
            self.ACT(self.ACT8[:, k, 0:T], tm.ap[:, 0:T], AF.Identity, tm.keys + ['pv_mod'], [('act8', k)], bias=sh, scale=1.0)
            if want_lo:
                h32 = self.scr(14336 + (k % 2) * 2048, 2048, F32)
                self.TS('pool', h32.ap[:, 0:T], tm.ap[:, 0:T], sh, None, ALU.add, None, tm.keys + ['pv_mod'], h32.keys)
                self.TT('pool', self.HNLO[:, k, 0:T], h32.ap[:, 0:T], self.ACT8[:, k, 0:T], ALU.subtract,
                        h32.keys + [('act8', k)], [('hnlo', k)])

    def resid(self, job, l, which, m, pb, pk, T):
        g = self.modv(job, l, 2 if which == 0 else 5, m, 1)
        self.STT(self.XRES[:, m, 0:T], pb[:, 0:T], g, self.XRES[:, m, 0:T], ALU.mult, ALU.add,
                 [pk, 'pv_mod', ('xres', m)], [('xres', m)])

    def layer0(self, job, ti, T, last):
        W = self.W
        NC = T // 64
        U = min(T, 128)
        NU = T // U
        J = T // 8
        hn = self.ACT8
        hnk = [('act8', k) for k in range(8)]
        self.norm(job, 0, 0, T, want_lo=True)
        o = [0]

        def A(nbytes, dt, parts=128):
            r = self.scr(o[0], nbytes, dt, parts=parts)
            o[0] += (nbytes + 63) // 64 * 64
            return r
        RSIL = A(4 * 512 * 2, BF16)
        VTOK = A(4 * 512 * 2, BF16)
        ZT2 = A(32 * 128 * 2, BF16)
        U2 = A(32 * 64 * 2, BF16)
        gla_start = o[0]
        GL = A(512 * 2, BF16)
        LL = A(2 * 512 * 4, F32)
        CUM = A(2 * 512 * 4, F32)
        E1 = A(2 * 512 * 4, F32)
        E2 = A(2 * 512 * 4, F32)
        KEF = A(2 * 512 * 4, F32)
        QE = A(2 * 512 * 2, BF16)
        QE32 = A(2 * 512 * 4, F32)
        KE = A(2 * 512 * 2, BF16)
        KD = A(2 * 512 * 2, BF16)
        KDT = A(4 * 2 * 128 * 2, BF16)
        ATT = [A(4 * 128 * 2, BF16) for _ in range(2)]
        OSB = [A(4 * 128 * 4, F32) for _ in range(2)]
        OSQ = [A(4 * 128 * 2, BF16) for _ in range(2)]
        ORS = [A(4 * 128 * 4, F32) for _ in range(2)]
        OT = [A(4 * 128 * 4, F32) for _ in range(2)]
        assert o[0] <= self.SCRN * 4, o[0]
        gla_end = o[0]
        rsv = RSIL.ap.rearrange("p (c t) -> p c t", c=4)
        vtv = VTOK.ap.rearrange("p (u f) -> p u f", u=4)
        zt4 = ZT2.ap.rearrange("p (g s h) -> p g s h", g=32, s=8)
        v2 = lambda r: r.ap.rearrange("p (c t) -> p c t", c=2)
        if self.sub < 1:
            return
        evin = W['evin']
        wt, wk = self.ws_load([(lambda t: t[:].rearrange("p (k n) -> p k n", k=8),
                                evin[:, 512:1024].rearrange("(k p) n -> p k n", p=128))], ['w_evin'])
        wv = wt[:].rearrange("p (k n) -> p k n", k=8)
        wtl, wkl = self.ws_load([(lambda t: t[:].rearrange("p (k n) -> p k n", k=8),
                                  W['evin_lo'].rearrange("(k p) n -> p k n", p=128))], ['w_evin_lo'])
        wvl = wtl[:].rearrange("p (k n) -> p k n", k=8)
        qkb = []
        for mc in range(4):
            pb, pk = self.bank()
            for k in range(8):
                self.MM(pb[:, 0:T], wv[:, k, mc * 128:(mc + 1) * 128], hn[:, k, 0:T], k == 0, False, [wk, hnk[k]], [pk])
            for k in range(8):
                self.MM(pb[:, 0:T], wvl[:, k, mc * 128:(mc + 1) * 128], hn[:, k, 0:T], False, False, [wkl, hnk[k]], [pk])
            for k in range(8):
                self.MM(pb[:, 0:T], wv[:, k, mc * 128:(mc + 1) * 128], self.HNLO[:, k, 0:T], False, k == 7, [wk, ('hnlo', k)], [pk])
            qkb.append((pb, pk))
        if self.fine < 1:
            return
        pbg, pkg = self.bank()
        for k in range(8):
            self.MM(pbg[0:16, 0:T], self.WGL[:, k, :], hn[:, k, 0:T], k == 0, k == 7, ['wgl', hnk[k]], [pkg])
        self.CP('act', GL.ap[0:16, 0:T], pbg[0:16, 0:T], [pkg], GL.keys)
        if self.fine < 2:
            return
        pbG, pkG = self.bank(), None
        pbG, pkG = pbG
        gate_banks = []
        for c2 in range(2):
            if T == 512:
                pbx, pkx = (pbG, pkG) if c2 == 0 else self.bank()
            else:
                pbx, pkx = (pbG, pkG)
            col0 = 0 if T == 512 else c2 * 64
            self.MM(pbx[:, col0:col0 + T], self.WG2[:, c2 * 128:(c2 + 1) * 128], GL.ap[0:16, 0:T], True, True,
                    ['wg2'] + GL.keys, [pkx])
            gate_banks.append((pbx, pkx, col0))
        if self.fine < 3:
            return
        llv, cumv, e1v, e2v, kefv = v2(LL), v2(CUM), v2(E1), v2(E2), v2(KEF)
        qev, kev, kdv = v2(QE), v2(KE), v2(KD)
        qe32v = v2(QE32)
        for c2 in range(2):
            pbx, pkx, col0 = gate_banks[c2]
            self.ACT(llv[:, c2, 0:T], pbx[:, col0:col0 + T], AF.Exp, [pkx, 'pv_negbg'], LL.keys,
                     bias=self.pvc('negbg', c2, 1), scale=-1.0)
        for c2 in range(2):
            self.ACT(llv[:, c2, 0:T], llv[:, c2, 0:T], AF.Ln, LL.keys + ['pv_one'], LL.keys, bias=self.pvc('one'), scale=1.0)
        if self.fine < 4:
            return
        for c2 in range(2):
            self.fw.op('dve', (lambda c2: lambda e: e.tensor_tensor_scan(
                out=cumv[:, c2, 0:T], data0=self.HC[:, HC_RST:HC_RST + T], data1=llv[:, c2, 0:T], initial=0.0,
                op0=ALU.mult, op1=ALU.add))(c2), LL.keys + ['hc'], CUM.keys)
        if self.fine < 5:
            return
        self.ACT(e1v[:, :, 0:T], cumv[:, :, 0:T], AF.Exp, CUM.keys, E1.keys, scale=-1.0 / 16.0)
        self.ACT(e2v[:, :, 0:T], cumv[:, :, 0:T], AF.Exp, CUM.keys, E2.keys, scale=1.0 / 16.0)
        if self.fine < 6:
            return
        for c2 in range(2):
            pbq, pkq = qkb[c2]
            self.STT(qe32v[:, c2, 0:T], pbq[:, 0:T], 0.125, e1v[:, c2, 0:T], ALU.mult, ALU.mult, [pkq] + E1.keys, QE32.keys)
            self.CP('pool', qev[:, c2, 0:T], qe32v[:, c2, 0:T], QE32.keys, QE.keys)
            pbk, pkk = qkb[2 + c2]
            self.TT('dve', kefv[:, c2, 0:T], pbk[:, 0:T], e2v[:, c2, 0:T], ALU.mult, [pkk] + E2.keys, KEF.keys)
        self.CP('act', kev[:, :, 0:T], kefv[:, :, 0:T], KEF.keys, KE.keys)
        if self.fine < 7:
            return
        e1last = E1.ap.rearrange("p (c n l) -> p c n l", c=2, l=64)[:, :, 0:NC, 63:64].broadcast_to([128, 2, NC, 64])
        self.TT('pool', KD.ap.rearrange("p (c n l) -> p c n l", c=2, l=64)[:, :, 0:NC, :],
                KEF.ap.rearrange("p (c n l) -> p c n l", c=2, l=64)[:, :, 0:NC, :], e1last, ALU.mult,
                KEF.keys + E1.keys, KD.keys)
        if self.fine < 8:
            return
        wt, wk = self.ws_load([(lambda t: t[:].rearrange("p (k n) -> p k n", k=8),
                                evin[:, 1552:2064].rearrange("(k p) n -> p k n", p=128))], ['w_evin'])
        wv = wt[:].rearrange("p (k n) -> p k n", k=8)
        for mc in range(4):
            pb, pk = self.bank()
            for k in range(8):
                self.MM(pb[:, 0:T], wv[:, k, mc * 128:(mc + 1) * 128], hn[:, k, 0:T], k == 0, k == 7, [wk, hnk[k]], [pk])
            self.ACT(rsv[:, mc, 0:T], pb[:, 0:T], AF.Silu, [pk], RSIL.keys)
        if self.fine < 9:
            return
        if os.environ.get('KSLOT'):
            self.ws_rr = int(os.environ['KSLOT'])
        kvar = int(os.environ.get('KVAR', '0'))
        if kvar != 5:
            wt, wk = self.ws_load([(lambda t: t[:].rearrange("p (k n) -> p k n", k=8),
                                    evin[:, 1024:1536].rearrange("(k p) n -> p k n", p=128))], ['w_evin'])
            wv = wt[:].rearrange("p (k n) -> p k n", k=8)
        for u in range(NU):
            if kvar == 6:
                continue
            pb, pk = self.bank()
            for k in range(8):
                if kvar not in (2, 3, 5):
                    self.MM(pb[0:U, :], hn[:, k, u * U:(u + 1) * U], wv[:, k, :], k == 0, k == 7, [wk, hnk[k]], [pk])
            if kvar not in (1, 3, 5):
                self.CP('act' if u % 2 else 'dve', vtv[0:U, u, :], pb[0:U, :], [pk], VTOK.keys)
        if self.fine < 10:
            return
        wt, wk = self.ws_load([(lambda t: t[:].rearrange("p (k n) -> p k n", k=8),
                                evin[:, 0:512].rearrange("(k p) n -> p k n", p=128))], ['w_evin'])
        wv = wt[:].rearrange("p (k n) -> p k n", k=8)
        for s in range(8):
            pb, pk = self.bank()
            for k in range(8):
                self.MM(pb[0:J, :], hn[:, k, s:T:8], wv[:, k, :], k == 0, k == 7, [wk, hnk[k]], [pk])
            self.CP('act' if s % 2 else 'dve', zt4[0:J, :, s, :], pb[0:J, :].rearrange("p (g h) -> p g h", h=16), [pk], ZT2.keys)
        if self.sub < 2:
            return
        kdt4 = KDT.ap.rearrange("p (u c f) -> p u c f", u=4, c=2)
        if U < 128:
            self.MEMSET('pool', vtv[64:128, 0, :], 0.0, VTOK.keys)
            for s2_ in range(2):
                self.MEMSET('pool', ATT[s2_].ap[64:128, :], 0.0, ATT[s2_].keys)
        for u in range(NU):
            pb, pk = self.bank()
            pbb = pb[:, 0:128].bitcast(BF16)
            for c2 in range(2):
                self.TR(pbb[0:U, c2 * 128:(c2 + 1) * 128], kdv[:, c2, u * U:(u + 1) * U], self.IDb[:], KD.keys + ['idb'], [pk])
            self.CP('act' if u % 2 else 'dve', kdt4[0:U, u, :, :], pbb[0:U, 0:256].rearrange("p (c f) -> p c f", c=2), [pk], KDT.keys)
        if self.fine < 21:
            return
        self.CP('act', self.GSH[:, 0, :, :], self.GS[:], ['gs'], [('gsh', 0)])
        e1l = E1.ap.rearrange("p (c n l) -> p c n l", c=2, l=64)
        for c in range(NC):
            u, cu = divmod(c, U // 64)
            pb, pk = self.bank()
            p0 = cu * 64
            for h in range(4):
                hp, hb = divmod(h, 2)
                self.MM(pb[hb * 64:(hb + 1) * 64, hp * 128:(hp + 1) * 128],
                        kdt4[p0:p0 + 64, u, hp, hb * 64:(hb + 1) * 64], vtv[p0:p0 + 64, u, h * 128:(h + 1) * 128],
                        True, True, KDT.keys + VTOK.keys, [pk])
            for hp in range(2):
                self.STT(self.GS[:, hp, :], self.GS[:, hp, :], e1l[:, hp, c, 63:64], pb[:, hp * 128:(hp + 1) * 128],
                         ALU.mult, ALU.add, ['gs', pk] + E1.keys, ['gs'])
            if c + 1 < NC:
                self.CP('act', self.GSH[:, c + 1, :, :], self.GS[:], ['gs'], [('gsh', c + 1)])
        if self.fine < 22:
            return
        MCm = self.HC[0:U, HC_MC:HC_MC + U]
        mixed = self.ACT8
        for u in range(NU):
            s2 = u % 2
            cols = slice(u * U, (u + 1) * U)
            attv = ATT[s2].ap.rearrange("p (h l) -> p h l", h=4)
            sbk = [self.bank(), self.bank()]
            for h in range(4):
                hp, hb = divmod(h, 2)
                pbs, pks = sbk[hb]
                self.MM(pbs[0:U, hp * U:(hp + 1) * U], kefv[hb * 64:(hb + 1) * 64, hp, cols], qe32v[hb * 64:(hb + 1) * 64, hp, cols],
                        True, True, KEF.keys + QE32.keys, [pks])
            for hb in range(2):
                pbs, pks = sbk[hb]
                self.TT('dve', attv[0:U, hb:4:2, 0:U], pbs[0:U, 0:2 * U].rearrange("p (h l) -> p h l", h=2),
                        MCm.unsqueeze(1).broadcast_to([U, 2, U]), ALU.mult, [pks, 'hc'], ATT[s2].keys)
            obk = [self.bank(), self.bank()]
            for h in range(4):
                hp, hb = divmod(h, 2)
                pbo, pko = obk[hb]
                self.MM(pbo[:, hp * U:(hp + 1) * U], vtv[0:128, u, h * 128:(h + 1) * 128], attv[0:128, h, 0:U], True, False,
                        VTOK.keys + ATT[s2].keys, [pko])
                for cu in range(U // 64):
                    c = u * (U // 64) + cu
                    self.MM(pbo[:, hp * U + cu * 64:hp * U + (cu + 1) * 64], self.GSH[hb * 64:(hb + 1) * 64, c, hp, :],
                            qev[hb * 64:(hb + 1) * 64, hp, c * 64:(c + 1) * 64], False, cu == U // 64 - 1,
                            [('gsh', c)] + QE.keys, [pko])
            if self.fine < 23:
                continue
            osb, osq, ors, ot = OSB[s2], OSQ[s2], ORS[s2], OT[s2]
            n4 = 4 * U
            osb3 = osb.ap[:, 0:n4].rearrange("p (h l) -> p h l", h=4)
            osq3 = osq.ap[:, 0:n4].rearrange("p (h l) -> p h l", h=4)
            for hb in range(2):
                pbo, pko = obk[hb]
                src3 = pbo[:, 0:2 * U].rearrange("p (h l) -> p h l", h=2)
                self.CP('act', osb3[:, hb:4:2, :], src3, [pko], osb.keys)
                self.ACT(osq3[:, hb:4:2, :], src3, AF.Square, [pko], osq.keys)
            pbn, pkn = self.bank()
            self.MM(pbn[:, 0:n4], self.ONESb[:], osq.ap[:, 0:n4], True, True, osq.keys + ['onesb'], [pkn])
            self.ACT(ors.ap[:, 0:n4], pbn[:, 0:n4], AF.Sqrt, [pkn, 'pv_eps'], ors.keys, bias=self.pvc('eps'), scale=1.0 / 128)
            self.fw.op('dve', (lambda ors=ors, n4=n4: lambda e: e.reciprocal(out=ors.ap[:, 0:n4], in_=ors.ap[:, 0:n4]))(),
                       ors.keys, ors.keys)
            self.STT(ot.ap[:, 0:n4], osb.ap[:, 0:n4], self.pvc('glang'), ors.ap[:, 0:n4], ALU.mult, ALU.mult,
                     osb.keys + ors.keys + ['pv_glang'], ot.keys)
            self.TT('pool', mixed[:, 4:8, cols], ot.ap[:, 0:n4].rearrange("p (h l) -> p h l", h=4), rsv[:, :, cols], ALU.mult,
                    ot.keys + RSIL.keys, [('act8', 4 + h) for h in range(4)])
        if self.sub < 3:
            return
        o[0] = gla_start
        VSB = A(2 * 32 * 64 * 4, F32, parts=64)
        XBF = A(2 * 32 * 64 * 2, BF16, parts=64)
        Y2 = A(32 * 64 * 2, BF16)
        YTOK = A(512 * 8 * 2, BF16, parts=64)
        YFM = A(4 * 512 * 2, BF16)
        SGT = [A(512 * 4, F32) for _ in range(2)]
        TM1 = A(64 * 4, F32, parts=64)
        TM2 = A(64 * 4, F32, parts=64)
        assert o[0] <= self.SCRN * 4, o[0]
        u2v = U2.ap.rearrange("p (g j) -> p g j", g=32)
        for half in range(2):
            pb, pk = self.bank()
            pbb = pb[:].bitcast(BF16)
            for gg in range(16):
                g = half * 16 + gg
                self.TR(pbb[:, gg * J:(gg + 1) * J], ZT2.ap[0:J, g * 128:(g + 1) * 128], self.IDb[0:J, 0:J],
                        ZT2.keys + ['idb'], [pk])
            self.CP('act' if half else 'dve', u2v[:, half * 16:(half + 1) * 16, 0:J],
                    pbb[:, 0:16 * J].rearrange("p (g j) -> p g j", g=16), [pk], U2.keys)
        wt_in, wk_in = self.ws_load([(lambda t: t[:], W['s5win'])], ['w_s5win'])
        winv = wt_in[:].rearrange("p (g m) -> p g m", g=32)
        vsb4 = VSB.ap.rearrange("p (c g j) -> p c g j", c=2, g=32)
        xbf4 = XBF.ap.rearrange("p (c g j) -> p c g j", c=2, g=32)
        gpbv = min(32, 512 // J)
        for c in range(2):
            for bi_, g0 in enumerate(range(0, 32, gpbv)):
                pb, pk = self.bank()
                for gg in range(gpbv):
                    g = g0 + gg
                    self.MM(pb[0:64, gg * J:(gg + 1) * J], winv[:, g, c * 64:(c + 1) * 64], u2v[:, g, 0:J], True, True,
                            [wk_in] + U2.keys, [pk])
                self.CP('act' if bi_ % 2 else 'dve', vsb4[:, c, g0:g0 + gpbv, 0:J],
                        pb[0:64, 0:gpbv * J].rearrange("p (g j) -> p g j", g=gpbv), [pk], VSB.keys)
        S3 = self.S3
        tm1 = TM1.ap.rearrange("p (c g) -> p c g", c=2)
        tm2 = TM2.ap.rearrange("p (c g) -> p c g", c=2)
        eng = os.environ.get('KSCAN', 'dve')
        for j in range(J):
            self.CP(eng, xbf4[:, :, :, j], S3[:, 0:2, :], ['s3'], XBF.keys)
            self.TT(eng, tm1, self.A1[:], S3[:, 0:2, :], ALU.mult, ['a12', 's3'], TM1.keys)
            self.TT(eng, tm2, self.A2[:], S3[:, 1:3, :], ALU.mult, ['a12', 's3'], TM2.keys)
            self.TT(eng, tm1, tm1, tm2, ALU.add, TM1.keys + TM2.keys, TM1.keys)
            self.TT(eng, S3[:, 0:2, :], tm1, vsb4[:, :, :, j], ALU.add, TM1.keys + VSB.keys, ['s3'])
            self.CP(eng, S3[:, 2, :], S3[:, 0, :], ['s3'], ['s3'])
        wt_t0, wk_t0 = self.ws_load([(lambda t: t[:], W['s5t0'])], ['w_s5t0'])
        t0v = wt_t0[:].rearrange("p (g m) -> p g m", g=32)
        wo = []
        for c in range(2):
            wt_o, wk_o = self.ws_load([(lambda t: t[0:64, :], W['s5wout'][:, c * 4096:(c + 1) * 4096])], ['w_s5wout'])
            wo.append((wt_o[0:64, :].rearrange("p (g m) -> p g m", g=32), wk_o))
        y2v = Y2.ap.rearrange("p (g j) -> p g j", g=32)
        gpb = 512 // J if J >= 16 else 32
        for g0 in range(0, 32, gpb):
            pb, pk = self.bank()
            ng = min(gpb, 32 - g0)
            for gg in range(ng):
                g = g0 + gg
                self.MM(pb[:, gg * J:(gg + 1) * J], t0v[:, g, :], u2v[:, g, 0:J], True, False, [wk_t0] + U2.keys, [pk])
                self.MM(pb[:, gg * J:(gg + 1) * J], wo[0][0][:, g, :], xbf4[:, 0, g, 0:J], False, False, [wo[0][1]] + XBF.keys, [pk])
                self.MM(pb[:, gg * J:(gg + 1) * J], wo[1][0][:, g, :], xbf4[:, 1, g, 0:J], False, True, [wo[1][1]] + XBF.keys, [pk])
            self.ACT(y2v[:, g0:g0 + ng, 0:J], pb[:, 0:ng * J].rearrange("p (g j) -> p g j", g=ng), AF.Gelu, [pk], Y2.keys)
        ytv = YTOK.ap.rearrange("p (g m) -> p g m", g=32)
        for q4 in range(4):
            pb, pk = self.bank()
            pbb = pb[:].bitcast(BF16)
            for gg in range(8):
                g = q4 * 8 + gg
                self.TR(pbb[0:J, gg * 128:(gg + 1) * 128], y2v[:, g, 0:J], self.IDb[:], Y2.keys + ['idb'], [pk])
            self.CP('act' if q4 % 2 else 'dve', ytv[0:J, q4 * 8:(q4 + 1) * 8, :],
                    pbb[0:J, 0:1024].rearrange("p (g m) -> p g m", g=8), [pk], YTOK.keys)
        yt3 = YTOK.ap.rearrange("p (c l) -> p c l", l=8)
        yfv = YFM.ap.rearrange("p (b t) -> p b t", b=4)
        for cb in range(4):
            pb, pk = self.bank()
            pbb = pb[:].bitcast(BF16)
            for l in range(8):
                self.TR(pbb[:, l * J:(l + 1) * J], yt3[0:J, cb * 128:(cb + 1) * 128, l], self.IDb[0:J, 0:J],
                        YTOK.keys + ['idb'], [pk])
            self.CP('act' if cb % 2 else 'dve', yfv[:, cb, 0:T].rearrange("p (j l) -> p l j", l=8),
                    pbb[:, 0:8 * J].rearrange("p (l j) -> p l j", l=8), [pk], YFM.keys)
        for m in range(4):
            pb, pk = self.bank()
            for k in range(4):
                self.MM(pb[:, 0:T], self.WGLU[:, k, m * 128:(m + 1) * 128], yfv[:, k, 0:T], k == 0, k == 3,
                        ['wglu'] + YFM.keys, [pk])
            sg = SGT[m % 2]
            self.ACT(sg.ap[:, 0:T], pb[:, 0:T], AF.Sigmoid, [pk, 'pv_bglu'], sg.keys, bias=self.pvc('bglu', m, 1), scale=1.0)
            self.TT('dve', mixed[:, m, 0:T], yfv[:, m, 0:T], sg.ap[:, 0:T], ALU.mult, YFM.keys + sg.keys, [('act8', m)])
        if self.sub < 4:
            return
        self.out_proj(job, 0, T, W['evout'], None)

    def out_proj(self, job, l, T, wsrc, rowperm):
        mixed = self.ACT8
        for half in range(2):
            cs_ = slice(half * 512, (half + 1) * 512)
            if rowperm:
                pieces = []
                for g in range(4):
                    pieces.append(((lambda g: lambda t: t[0:64, g * 512:(g + 1) * 512])(g), wsrc[g * 64:(g + 1) * 64, cs_]))
                    pieces.append(((lambda g: lambda t: t[64:128, g * 512:(g + 1) * 512])(g), wsrc[256 + g * 64:256 + (g + 1) * 64, cs_]))
                pieces.append((lambda t: t[:, 2048:4096].rearrange("p (k n) -> p k n", k=4),
                               wsrc[512:1024, cs_].rearrange("(k p) n -> p k n", p=128)))
                wt, wk = self.ws_load(pieces, ['w_out'])
            else:
                wt, wk = self.ws_load([(lambda t: t[:].rearrange("p (k n) -> p k n", k=8),
                                        wsrc[:, cs_].rearrange("(k p) n -> p k n", p=128))], ['w_out'])
            wv = wt[:].rearrange("p (k n) -> p k n", k=8)
            for mc in range(4):
                m = half * 4 + mc
                pb, pk = self.bank()
                for k in range(8):
                    self.MM(pb[:, 0:T], wv[:, k, mc * 128:(mc + 1) * 128], mixed[:, k, 0:T], k == 0, k == 7,
                            [wk, ('act8', k)], [pk])
                self.resid(job, l, 0, m, pb, pk, T)

    def ffn(self, job, l, T, last):
        W, O = self.W, self.O
        hn = self.ACT8
        hnk = [('act8', k) for k in range(8)]
        self.norm(job, l, 1, T)
        o = [14336]

        def A(nbytes, dt, parts=128):
            r = self.scr(o[0], nbytes, dt, parts=parts)
            o[0] += (nbytes + 63) // 64 * 64
            return r
        HB = A(22 * 512 * 2, BF16)
        ACC = [[A(512 * 4, F32) for _ in range(2)] for _ in range(2)]
        SA = [A(512 * 4, F32) for _ in range(2)]
        UT = [A(512 * 4, F32, parts=2) for _ in range(2)]
        assert o[0] <= self.SCRN * 4
        hbv = HB.ap.rearrange("p (j t) -> p j t", j=22)
        FT = self.FT
        fcw = lambda j: self.PV[:, self.pv['fcw'] + l * 132 + j * 44: self.pv['fcw'] + l * 132 + (j + 1) * 44]
        self.TT('dve', self.FCORR[:, :, 0], FT[:, l, :, 1], fcw(1), ALU.mult, ['ft', 'pv_fcw'], ['fcorr'])
        self.TT('dve', self.FTMP[:, :, 0], FT[:, l, :, 0], fcw(0), ALU.mult, ['ft', 'pv_fcw'], ['ftmp'])
        self.TT('dve', self.FCORR[:, :, 0], self.FCORR[:, :, 0], self.FTMP[:, :, 0], ALU.add, ['fcorr', 'ftmp'], ['fcorr'])
        self.TT('dve', self.FCORR[:, :, 1], FT[:, l, :, 1], fcw(0), ALU.mult, ['ft', 'pv_fcw'], ['fcorr'])
        up = W['up'][l]
        for pa in range(11):
            wt, wk = self.ws_load([
                (lambda t: t[:].rearrange("p (k n) -> p k n", k=8)[:, :, 0:256],
                 up[:, pa * 256:(pa + 1) * 256].rearrange("(k p) n -> p k n", p=128)),
                (lambda t: t[:].rearrange("p (k n) -> p k n", k=8)[:, :, 256:512],
                 up[:, DFF + pa * 256:DFF + (pa + 1) * 256].rearrange("(k p) n -> p k n", p=128))], ['w_up'])
            wv = wt[:].rearrange("p (k n) -> p k n", k=8)
            for bi in range(2):
                j = pa * 2 + bi
                slot = j % 2
                accs = []
                for ag in range(2):
                    blk = ag * 22 + j
                    pb, pk = self.bank()
                    c0 = ag * 256 + bi * 128
                    for k in range(8):
                        self.MM(pb[:, 0:T], wv[:, k, c0:c0 + 128], hn[:, k, 0:T], k == 0, k == 7, [wk, hnk[k]], [pk])
                    acc = ACC[slot][ag]
                    w2 = self.pvc('fcw', l * 132 + 2 * 44 + blk, 1)
                    w1 = self.pvc('fcw', l * 132 + 1 * 44 + blk, 1)
                    w0 = self.pvc('fcw', l * 132 + 0 * 44 + blk, 1)
                    bb = self.pvc('fcb', l * 44 + blk, 1)
                    self.ACT(acc.ap[:, 0:T], pb[:, 0:T], AF.Identity, [pk, 'pv_fcw', 'pv_fcb'], acc.keys, bias=bb, scale=w2)
                    self.STT(acc.ap[:, 1:T], pb[:, 0:T - 1], w1, acc.ap[:, 1:T], ALU.mult, ALU.add, [pk, 'pv_fcw'] + acc.keys, acc.keys)
                    self.STT(acc.ap[:, 2:T], pb[:, 0:T - 2], w0, acc.ap[:, 2:T], ALU.mult, ALU.add, [pk, 'pv_fcw'] + acc.keys, acc.keys)
                    self.TT('dve', acc.ap[:, 0:2], acc.ap[:, 0:2], self.FCORR[:, blk, :], ALU.add, acc.keys + ['fcorr'], acc.keys)
                    self.CP('act', FT[:, l, blk, :], pb[:, T - 2:T], [pk, 'fcorr'], ['ft'])
                    accs.append(acc)
                sa = SA[slot]
                self.ACT(sa.ap[:, 0:T], accs[0].ap[:, 0:T], AF.Silu, accs[0].keys, sa.keys)
                self.TT('pool', hbv[:, j, 0:T], sa.ap[:, 0:T], accs[1].ap[:, 0:T], ALU.mult, sa.keys + accs[1].keys, [('hb', j)])
            if last:
                pb, pk = self.bank()
                for k in range(8):
                    self.MM(pb[0:2, :], hn[:, k, T - 2:T], wv[:, k, :], k == 0, k == 7, [wk, hnk[k]], [pk])
                ut = UT[pa % 2]
                self.CP('dve', ut.ap[0:2, 0:512], pb[0:2, 0:512], [pk], ut.keys)
                self.DMA('pool', O['o_ffnconv'][job, l, :, pa * 256:(pa + 1) * 256], ut.ap[0:2, 0:256], ut.keys, ['out'])
                self.DMA('pool', O['o_ffnconv'][job, l, :, DFF + pa * 256:DFF + (pa + 1) * 256], ut.ap[0:2, 256:512], ut.keys, ['out'])
        dn = W['dn'][l]
        for m in range(8):
            wt, wk = self.ws_load([(lambda t: t[:, 0:22 * 128].rearrange("p (k n) -> p k n", k=22),
                                    dn[:, m * 128:(m + 1) * 128].rearrange("(k p) n -> p k n", p=128))], ['w_dn'])
            wv = wt[:, 0:22 * 128].rearrange("p (k n) -> p k n", k=22)
            pb, pk = self.bank()
            for k in range(22):
                self.MM(pb[:, 0:T], wv[:, k, :], hbv[:, k, 0:T], k == 0, k == 21, [wk, ('hb', k)], [pk])
            self.resid(job, l, 1, m, pb, pk, T)

    def layer1(self, job, ti, T, last):
        W, O = self.W, self.O
        NC = T // 64
        U = min(T, 128)
        NU = T // U
        NCU = U // 64
        hn = self.ACT8
        hnk = [('act8', k) for k in range(8)]
        self.norm(job, 1, 0, T)
        o = [0]

        def A(nbytes, dt, parts=128):
            r = self.scr(o[0], nbytes, dt, parts=parts)
            o[0] += (nbytes + 63) // 64 * 64
            return r
        QN = A(4 * 512 * 2, BF16)
        KNF = A(512 * 4, F32)
        ZGS = A(4 * 512 * 2, BF16)
        XBC = A(8 * 512 * 2, BF16)
        DT = A(128 * 4, F32)
        DTA = A(4 * 8 * 4, F32)
        VLF = A(128 * 4, F32)
        ACCX = [A(512 * 4, F32) for _ in range(2)]
        NSQ = [A(512 * 2, BF16) for _ in range(2)]
        NRS = [A(512 * 4, F32) for _ in range(2)]
        XT = A(512 * 2, BF16)
        BT = A(256 * 2, BF16)
        LH = A(8 * 128 * 4, F32)
        DEC = A(8 * 128 * 4, F32)
        MGT = A(2 * 128 * 4, F32)
        MT = A(8 * 128 * 2, BF16)
        XDT = A(512 * 2, BF16)
        XD = A(512 * 2, BF16)
        XW = A(512 * 2, BF16)
        EX = A(16 * 4, F32)
        DECB = A(2 * 8 * 4, F32)
        T1 = A(512 * 4, F32)
        YY = A(512 * 4, F32)
        YG = A(512 * 4, F32)
        YJ = A(512 * 4, F32)
        YN = A(512 * 2, BF16)
        SS = A(4 * 4, F32)
        TMPS = [A(512 * 4, F32) for _ in range(2)]
        PT = [A(512 * 2, BF16) for _ in range(4)]
        DEN = [A(256 * 4, F32) for _ in range(2)]
        UT3 = [A(512 * 4, F32, parts=3) for _ in range(2)]
        assert o[0] <= self.SCRN * 4, o[0]
        qnv = QN.ap.rearrange("p (g t) -> p g t", g=4)
        zgv = ZGS.ap.rearrange("p (u f) -> p u f", u=4)
        xbv = XBC.ap.rearrange("p (c t) -> p c t", c=8)
        dtv = DT.ap[:, 0:32].rearrange("p (u h) -> p u h", u=4)
        dtav = self.PV[:, self.pv['tmp']:self.pv['tmp'] + 32].rearrange("p (u h) -> p u h", u=4)
        DTA = Reg(dtav, ['pv_tmp'])
        odin = W['odin']
        MBb = self.MBb

        def qknorm(pb, pk, gvec, gkey, out_ap, out_keys, slot, out2=None, out2_keys=None):
            sq, rs = NSQ[slot], NRS[slot]
            self.ACT(sq.ap[:, 0:T], pb[:, 0:T], AF.Square, [pk], sq.keys)
            pb2, pk2 = self.bank()
            self.MM(pb2[:, 0:T], MBb[:], sq.ap[:, 0:T], True, True, sq.keys + ['mbb'], [pk2])
            self.ACT(rs.ap[:, 0:T], pb2[:, 0:T], AF.Sqrt, [pk2, 'pv_eps'], rs.keys, bias=self.pvc('eps'), scale=1.0 / 64)
            self.fw.op('dve', (lambda rs=rs: lambda e: e.reciprocal(out=rs.ap[:, 0:T], in_=rs.ap[:, 0:T]))(), rs.keys, rs.keys)
            self.STT(out_ap, pb[:, 0:T], gvec, rs.ap[:, 0:T], ALU.mult, ALU.mult, [pk, gkey] + rs.keys, out_keys)
            if out2 is not None:
                self.CP('act', out2, out_ap, out_keys, out2_keys)
        wt, wk = self.ws_load([(lambda t: t[:].rearrange("p (k n) -> p k n", k=8),
                                odin[:, 0:512].rearrange("(k p) n -> p k n", p=128))], ['w_odin'])
        wv = wt[:].rearrange("p (k n) -> p k n", k=8)
        for g in range(4):
            pb, pk = self.bank()
            for kvh in range(2):
                for k in range(8):
                    c0 = kvh * 256 + g * 64
                    self.MM(pb[kvh * 64:(kvh + 1) * 64, 0:T], wv[:, k, c0:c0 + 64], hn[:, k, 0:T], k == 0, k == 7, [wk, hnk[k]], [pk])
            qknorm(pb, pk, self.pvc('qng'), 'pv_qng', qnv[:, g, 0:T], QN.keys, g % 2)
        if self.fine < 31:
            return
        wt, wk = self.ws_load([(lambda t: t[:, 0:2048].rearrange("p (k n) -> p k n", k=8),
                                odin[:, 512:768].rearrange("(k p) n -> p k n", p=128))], ['w_odin'])
        wv = wt[:, 0:2048].rearrange("p (k n) -> p k n", k=8)
        pb, pk = self.bank()
        for k in range(8):
            self.MM(pb[:, 0:T], wv[:, k, 0:128], hn[:, k, 0:T], k == 0, k == 7, [wk, hnk[k]], [pk])
        qknorm(pb, pk, self.pvc('kng'), 'pv_kng', KNF.ap[:, 0:T], KNF.keys, 0, self.KWIN[:, 128:128 + T], ['kwin'])
        for u in range(NU):
            pb, pk = self.bank()
            for k in range(8):
                self.MM(pb[0:U, 0:128], hn[:, k, u * U:(u + 1) * U], wv[:, k, 128:256], k == 0, k == 7, [wk, hnk[k]], [pk])
            self.CP('act', self.VWP[0:U, u + 1, :], pb[0:U, 0:128], [pk], ['vwp'])
            if last and u == NU - 1:
                self.CP('dve', VLF.ap[0:U, 0:128], pb[0:U, 0:128], [pk], VLF.keys)
        if self.fine < 32:
            return
        wt, wk = self.ws_load([(lambda t: t[:].rearrange("p (k n) -> p k n", k=8),
                                odin[:, 768:1280].rearrange("(k p) n -> p k n", p=128))], ['w_odin'])
        wv = wt[:].rearrange("p (k n) -> p k n", k=8)
        for u in range(NU):
            pb, pk = self.bank()
            for k in range(8):
                self.MM(pb[0:U, :], hn[:, k, u * U:(u + 1) * U], wv[:, k, :], k == 0, k == 7, [wk, hnk[k]], [pk])
            self.ACT(zgv[0:U, u, :], pb[0:U, :], AF.Silu, [pk], ZGS.keys)
        if self.fine < 33:
            return
        self.MEMSET('dve', DT.ap[:, 0:128], 0.0, DT.keys)
        for u in range(NU):
            pb, pk = self.bank()
            for k in range(8):
                if os.environ.get('KVAR') != '7':
                    self.MM(pb[0:U, 0:8], hn[:, k, u * U:(u + 1) * U], self.WDT[:, k, :], k == 0, k == 7, ['wdt', hnk[k]], [pk])
            self.TT('dve', dtv[0:U, u, :], pb[0:U, 0:8], self.BC[0:U, 0:8], ALU.add, [pk, 'bc'], DT.keys)
        kv_ = int(os.environ.get('KVAR', '0'))
        if kv_ == 8:
            return
        NW = 128 if os.environ.get('KPAD') else NU * 8
        self.ACT(DT.ap[0:U, 0:NW], DT.ap[0:U, 0:NW], AF.Exp, DT.keys, DT.keys)
        if kv_ == 9:
            return
        if kv_ in (15, 19):
            self.TS('dve', dtav[0:U, 0, :], self.BC[0:U, 8:16], -1.0, None, ALU.mult, None, DT.keys + ['bc'], DTA.keys)
            if kv_ == 15:
                return
        if os.environ.get('KLN') == '1':
            self.TS('dve', DT.ap[0:U, 0:NW], DT.ap[0:U, 0:NW], 1.0, None, ALU.add, None, DT.keys, DT.keys)
            self.ACT(DT.ap[0:U, 0:NW], DT.ap[0:U, 0:NW], AF.Ln, DT.keys, DT.keys)
        else:
            self.ACT(DT.ap[0:U, 0:NW], DT.ap[0:U, 0:NW], AF.Ln, DT.keys + ['pv_one'], DT.keys, bias=self.PV[0:U, self.pv['one']:self.pv['one'] + 1], scale=1.0)
        if kv_ in (10, 19):
            return
        if kv_ == 16:
            self.CP('act', DT.ap[0:U, 0:NU * 8], DT.ap[0:U, 0:NU * 8], DT.keys, DT.keys)
        for u in range(NU):
            if kv_ == 11:
                self.TS('dve', dtav[0:U, u, :], dtv[0:U, u, :], -1.0, None, ALU.mult, None, DT.keys + ['bc'], DTA.keys)
            elif kv_ == 13:
                self.MEMSET('dve', dtav[0:U, u, :], 0.5, DTA.keys)
            elif kv_ == 17:
                self.fw.op('dve', (lambda u=u: lambda e: e.memset(dtav[0:U, u, :], 0.5))(), DT.keys, DTA.keys)
            elif kv_ == 18:
                self.fw.op('pe', (lambda u=u: lambda e: e.matmul(self.PB[7][0:8, 0:8], lhsT=self.IDb[:, 0:8], rhs=self.IDb[:, 0:8], start=True, stop=True))(), DT.keys, ['pb7'])
            elif kv_ == 14:
                self.TS('dve', dtav[0:U, u, :], self.BC[0:U, 8:16], -1.0, None, ALU.mult, None, DT.keys + ['bc'], DTA.keys)
            elif kv_ == 12:
                self.TT('dve', dtav[0:U, u, :], dtv[0:U, u, :], self.BC[0:U, 0:8], ALU.mult, DT.keys + ['bc'], DTA.keys)
            else:
                self.TT('dve', dtav[0:U, u, :], dtv[0:U, u, :], self.BC[0:U, 8:16], ALU.mult, DT.keys + ['bc'], DTA.keys)
        if self.fine < 34:
            return
        CT = self.CT
        scw = lambda j: self.PV[:, self.pv['scw'] + j * 8: self.pv['scw'] + (j + 1) * 8]
        CC, CM = self.CCORR, self.CTMP
        self.TT('dve', CC[:, :, 0], CT[:, :, 3], scw(2), ALU.mult, ['ct', 'pv_scw'], ['ccorr'])
        self.TT('dve', CM[:, :, 0], CT[:, :, 2], scw(1), ALU.mult, ['ct', 'pv_scw'], ['ctmp'])
        self.TT('dve', CC[:, :, 0], CC[:, :, 0], CM[:, :, 0], ALU.add, ['ccorr', 'ctmp'], ['ccorr'])
        self.TT('dve', CM[:, :, 0], CT[:, :, 1], scw(0), ALU.mult, ['ct', 'pv_scw'], ['ctmp'])
        self.TT('dve', CC[:, :, 0], CC[:, :, 0], CM[:, :, 0], ALU.add, ['ccorr', 'ctmp'], ['ccorr'])
        self.TT('dve', CC[:, :, 1], CT[:, :, 3], scw(1), ALU.mult, ['ct', 'pv_scw'], ['ccorr'])
        self.TT('dve', CM[:, :, 1], CT[:, :, 2], scw(0), ALU.mult, ['ct', 'pv_scw'], ['ctmp'])
        self.TT('dve', CC[:, :, 1], CC[:, :, 1], CM[:, :, 1], ALU.add, ['ccorr', 'ctmp'], ['ccorr'])
        self.TT('dve', CC[:, :, 2], CT[:, :, 3], scw(0), ALU.mult, ['ct', 'pv_scw'], ['ccorr'])
        if kv_ == 31:
            return
        for pa in range(2):
            wt, wk = self.ws_load([(lambda t: t[:].rearrange("p (k n) -> p k n", k=8),
                                    odin[:, 1280 + pa * 512:1280 + (pa + 1) * 512].rearrange("(k p) n -> p k n", p=128))], ['w_odin'])
            wv = wt[:].rearrange("p (k n) -> p k n", k=8)
            for mc in range(4):
                c = pa * 4 + mc
                pb, pk = self.bank()
                for k in range(8):
                    self.MM(pb[:, 0:T], wv[:, k, mc * 128:(mc + 1) * 128], hn[:, k, 0:T], k == 0, k == 7, [wk, hnk[k]], [pk])
                if kv_ == 32:
                    continue
                acc = ACCX[c % 2]
                w = [self.pvc('scw', j * 8 + c, 1) for j in range(4)]
                self.ACT(acc.ap[:, 0:T], pb[:, 0:T], AF.Identity, [pk, 'pv_scw', 'pv_scb'], acc.keys, bias=self.pvc('scb', c, 1), scale=w[3])
                for sh in (1, 2, 3):
                    if kv_ == 34 and sh == 3:
                        continue
                    self.STT(acc.ap[:, sh:T], pb[:, 0:T - sh], w[3 - sh], acc.ap[:, sh:T], ALU.mult, ALU.add, [pk, 'pv_scw'] + acc.keys, acc.keys)
                if kv_ != 35:
                    self.TT('dve', acc.ap[:, 0:3], acc.ap[:, 0:3], CC[:, c, :], ALU.add, acc.keys + ['ccorr'], acc.keys)
                if kv_ != 36:
                    self.CP('dve', CT[:, c, :], pb[:, T - 4:T], [pk, 'ccorr'], ['ct'])
                self.ACT(xbv[:, c, 0:T], acc.ap[:, 0:T], AF.Silu, acc.keys, XBC.keys)
            if last and kv_ != 33:
                pb, pk = self.bank()
                for k in range(8):
                    self.MM(pb[0:3, :], hn[:, k, T - 3:T], wv[:, k, :], k == 0, k == 7, [wk, hnk[k]], [pk])
                ut = UT3[pa]
                self.CP('dve', ut.ap[0:3, 0:512], pb[0:3, 0:512], [pk], ut.keys)
                self.DMA('pool', O['o_ssdconv'][job, :, pa * 512:(pa + 1) * 512], ut.ap[0:3, 0:512], ut.keys, ['out'])
        if self.sub < 11:
            return
        mixed = self.ACT8
        MC = self.HC[0:U, HC_MC:HC_MC + U]
        MG = self.HC[0:U, HC_MG:HC_MG + U]
        self.CP('act', self.STBH[:, 0, :], self.ST[:], ['st'], [('stbh', 0)])
        lhv = LH.ap.rearrange("p (h l) -> p h l", h=8)
        decv = DEC.ap.rearrange("p (h l) -> p h l", h=8)
        mgtv = MGT.ap.rearrange("p (g l) -> p g l", g=2)
        mtv = MT.ap.rearrange("p (h l) -> p h l", h=8)
        decb = DECB.ap.rearrange("p (c h) -> p c h", c=2)
        for u in range(NU):
            cols = slice(u * U, (u + 1) * U)
            pb, pk = self.bank()
            pbb = pb[:].bitcast(BF16)
            for q in range(4):
                self.TR(pbb[0:U, q * 128:(q + 1) * 128], xbv[:, q, cols], self.IDb[:], XBC.keys + ['idb'], [pk])
            for q in range(2):
                self.TR(pbb[0:U, 512 + q * 128:512 + (q + 1) * 128], xbv[:, 4 + q, cols], self.IDb[:], XBC.keys + ['idb'], [pk])
            self.CP('dve', XT.ap[0:U, 0:512], pbb[0:U, 0:512], [pk], XT.keys)
            self.CP('act', BT.ap[0:U, 0:256], pbb[0:U, 512:768], [pk], BT.keys)
            pb, pk = self.bank()
            self.MM(pb[0:U, 0:8], MG, dtav[0:U, u, :], True, True, ['hc'] + DTA.keys, [pk])
            self.MM(pb[0:U, 8:16], MC, dtav[0:U, u, :], True, True, ['hc'] + DTA.keys, [pk])
            for cu in range(NCU):
                self.MM(pb[:, 16 + cu * 8:24 + cu * 8], self.HC[0:U, HC_CS + cu * 128:HC_CS + (cu + 1) * 128], dtav[0:U, u, :], True, True,
                        ['hc'] + DTA.keys, [pk])
            self.ACT(EX.ap[0:U, 0:16], pb[0:U, 0:16], AF.Exp, [pk], EX.keys)
            self.ACT(DECB.ap[:, 0:NCU * 8], pb[:, 16:16 + NCU * 8], AF.Exp, [pk], DECB.keys)
            for h in range(8):
                self.TS('pool' if h % 2 else 'dve', lhv[0:U, h, 0:U], MG, dtav[0:U, u, h:h + 1], None, ALU.mult, None, ['hc'] + DTA.keys, LH.keys)
            dbk = [self.bank(), self.bank()]
            for h in range(8):
                pbd, pkd = dbk[h // 4]
                self.MM(pbd[0:U, (h % 4) * U:(h % 4 + 1) * U], lhv[0:U, h, 0:U], MC, True, True, LH.keys + ['hc'], [pkd])
            for hh in range(2):
                pbd, pkd = dbk[hh]
                self.ACT(decv[0:U, hh * 4:(hh + 1) * 4, 0:U], pbd[0:U, 0:4 * U].rearrange("p (h l) -> p h l", h=4), AF.Exp, [pkd], DEC.keys)
            pbg, pkg = self.bank()
            for grp in range(2):
                self.MM(pbg[0:U, grp * U:(grp + 1) * U], xbv[:, 4 + grp, cols], xbv[:, 6 + grp, cols], True, True, XBC.keys, [pkg])
            self.TT('dve', mgtv[0:U, :, 0:U], pbg[0:U, 0:2 * U].rearrange("p (g l) -> p g l", g=2), MC.unsqueeze(1).broadcast_to([U, 2, U]),
                    ALU.mult, [pkg, 'hc'], MGT.keys)
            self.TT('dve', mtv[0:U, :, 0:U].rearrange("p (g q) l -> p g q l", g=2), decv[0:U, :, 0:U].rearrange("p (g q) l -> p g q l", g=2),
                    mgtv[0:U, :, 0:U].unsqueeze(2).broadcast_to([U, 2, 4, U]), ALU.mult, DEC.keys + MGT.keys, MT.keys)
            x3 = XT.ap[0:U, 0:512].rearrange("p (h q) -> p h q", h=8)
            self.TT('dve', XDT.ap[0:U, 0:512].rearrange("p (h q) -> p h q", h=8), x3, dtv[0:U, u, :].unsqueeze(2).broadcast_to([U, 8, 64]),
                    ALU.mult, XT.keys + DT.keys, XDT.keys)
            self.TT('pool', XD.ap[0:U, 0:512].rearrange("p (h q) -> p h q", h=8), x3, self.BC[0:U, 16:24].unsqueeze(2).broadcast_to([U, 8, 64]),
                    ALU.mult, XT.keys + ['bc'], XD.keys)
            self.TT('dve', XW.ap[0:U, 0:512].rearrange("p (h q) -> p h q", h=8), XDT.ap[0:U, 0:512].rearrange("p (h q) -> p h q", h=8),
                    EX.ap[0:U, 0:8].unsqueeze(2).broadcast_to([U, 8, 64]), ALU.mult, XDT.keys + EX.keys, XW.keys)
            pby, pky = self.bank()
            self.MM(pby[0:U, :], self.IDb[0:U, 0:U], XD.ap[0:U, 0:512], True, False, ['idb'] + XD.keys, [pky])
            for h in range(8):
                self.MM(pby[0:U, h * 64:(h + 1) * 64], mtv[0:U, h, 0:U], XDT.ap[0:U, h * 64:(h + 1) * 64], False, h == 7, MT.keys + XDT.keys, [pky])
            for cu in range(NCU):
                c = u * NCU + cu
                p0 = cu * 64
                pbu, pku = self.bank()
                for grp in range(2):
                    self.MM(pbu[:, grp * 256:(grp + 1) * 256], BT.ap[p0:p0 + 64, grp * 128:(grp + 1) * 128], XW.ap[p0:p0 + 64, grp * 256:(grp + 1) * 256],
                            True, True, BT.keys + XW.keys, [pku])
                self.TT('dve', self.ST[:].rearrange("p (h q) -> p h q", h=8), self.ST[:].rearrange("p (h q) -> p h q", h=8),
                        decb[:, cu, :].unsqueeze(2).broadcast_to([128, 8, 64]), ALU.mult, ['st'] + DECB.keys, ['st'])
                self.TT('dve', self.ST[:], self.ST[:], pbu[:, :], ALU.add, ['st', pku], ['st'])
                self.CP('act', self.STBH[:, c + 1, :], self.ST[:], ['st'], [('stbh', c + 1)])
            pbi, pki = self.bank()
            for cu in range(NCU):
                c = u * NCU + cu
                for grp in range(2):
                    self.MM(pbi[cu * 64:(cu + 1) * 64, grp * 256:(grp + 1) * 256], xbv[:, 6 + grp, u * U + cu * 64:u * U + (cu + 1) * 64],
                            self.STBH[:, c, grp * 256:(grp + 1) * 256], True, True, XBC.keys + [('stbh', c)], [pki])
            self.TT('dve', T1.ap[0:U, 0:512].rearrange("p (h q) -> p h q", h=8), pbi[0:U, :].rearrange("p (h q) -> p h q", h=8),
                    EX.ap[0:U, 8:16].unsqueeze(2).broadcast_to([U, 8, 64]), ALU.mult, [pki] + EX.keys, T1.keys)
            self.TT('dve', YY.ap[0:U, 0:512], pby[0:U, :], T1.ap[0:U, 0:512], ALU.add, [pky] + T1.keys, YY.keys)
            self.TT('pool', YG.ap[0:U, 0:512], YY.ap[0:U, 0:512], zgv[0:U, u, :], ALU.mult, YY.keys + ZGS.keys, YG.keys)
            self.fw.op('act', (lambda u=u: lambda e: e.activation(out=YJ.ap[0:U, 0:512], in_=YG.ap[0:U, 0:512], func=AF.Square,
                                                                  accum_out=SS.ap[0:U, 0:1]))(), YG.keys, YJ.keys + SS.keys)
            self.ACT(SS.ap[0:U, 0:1], SS.ap[0:U, 0:1], AF.Sqrt, SS.keys + ['pv_eps'], SS.keys, bias=self.PV[0:U, self.pv['eps']:self.pv['eps'] + 1], scale=1.0 / 512)
            self.fw.op('dve', lambda e: e.reciprocal(out=SS.ap[0:U, 0:1], in_=SS.ap[0:U, 0:1]), SS.keys, SS.keys)
            self.STT(YN.ap[0:U, 0:512], YG.ap[0:U, 0:512], SS.ap[0:U, 0:1], self.BC[0:U, 32:544], ALU.mult, ALU.mult, YG.keys + SS.keys + ['bc'], YN.keys)
            pbt, pkt = self.bank()
            pbtb = pbt[:].bitcast(BF16)
            for q in range(4):
                self.TR(pbtb[:, q * U:(q + 1) * U], YN.ap[0:U, q * 128:(q + 1) * 128], self.IDb[0:U, 0:U], YN.keys + ['idb'], [pkt])
            self.CP('act', mixed[:, 4:8, cols], pbtb[:, 0:4 * U].rearrange("p (q l) -> p q l", q=4), [pkt], [('act8', 4 + q) for q in range(4)])
        if self.sub < 12:
            return
        gc0 = ti * NC
        for c in range(NC):
            gc = gc0 + c if job < 2 else 2
            odd = c % 2
            BI = self.BIO if odd else self.BIE
            pts = []
            for kvh in range(2):
                pbs, pks = self.bank()
                rows = slice(kvh * 64, (kvh + 1) * 64)
                rhs = qnv[rows, :, c * 64:(c + 1) * 64]
                if not odd:
                    self.MM(pbs[0:128, 0:256], self.KWIN[rows, c * 64:c * 64 + 128], rhs, True, True, ['kwin'] + QN.keys, [pks])
                    self.MM(pbs[0:64, 256:512], self.KWIN[rows, (c + 2) * 64:(c + 3) * 64], rhs, True, True, ['kwin'] + QN.keys, [pks])
                else:
                    self.MM(pbs[0:128, 0:256], self.KWIN[rows, (c + 1) * 64:(c + 1) * 64 + 128], rhs, True, True, ['kwin'] + QN.keys, [pks])
                    self.MM(pbs[64:128, 256:512], self.KWIN[rows, c * 64:(c + 1) * 64], rhs, True, True, ['kwin'] + QN.keys, [pks])
                tmp = TMPS[kvh]
                pt = PT[(c % 2) * 2 + kvh]
                self.TT('dve', tmp.ap[:, 0:512], pbs[:, 0:512], BI[:, kvh, :], ALU.add, [pks, 'bie', 'bio'], tmp.keys)
                self.ACT(pt.ap[:, 0:512], tmp.ap[:, 0:512], AF.Exp, tmp.keys, pt.keys)
                if gc == 0:
                    self.MEMSET('pool', pt.ap[:, 0:256], 0.0, pt.keys)
                elif gc == 1:
                    self.MEMSET('pool', pt.ap[64:128, 256:512], 0.0, pt.keys)
                pts.append(pt)
            pbo, pko = self.bank()
            pbd, pkd = self.bank()
            for kvh in range(2):
                pt = pts[kvh]
                vc = slice(kvh * 64, (kvh + 1) * 64)
                if not odd:
                    pair_slot, single_slot, sp0 = c // 2, c // 2 + 1, 0
                else:
                    pair_slot, single_slot, sp0 = (c + 1) // 2, c // 2, 64
                for (pbx, pkx, lhs_pair, lhs_single) in [
                        (pbo, pko, self.VWP[:, pair_slot, vc], self.VWP[sp0:sp0 + 64, single_slot, vc]),
                        (pbd, pkd, self.ONESb[:, 0:64], self.ONESb[sp0:sp0 + 64, 0:64])]:
                    self.MM(pbx[kvh * 64:(kvh + 1) * 64, 0:256], lhs_pair, pt.ap[:, 0:256], True, False, ['vwp', 'onesb'] + pt.keys, [pkx])
                    self.MM(pbx[kvh * 64:(kvh + 1) * 64, 0:256], lhs_single, pt.ap[sp0:sp0 + 64, 256:512], False, True, ['vwp', 'onesb'] + pt.keys, [pkx])
            den = DEN[c % 2]
            self.TT('dve', den.ap[:, 0:256].rearrange("p (g l) -> p g l", g=4), pbd[:, 0:256].rearrange("p (g l) -> p g l", g=4),
                    self.pvc('sinkexp', 0, 4).unsqueeze(2).broadcast_to([128, 4, 64]), ALU.add, [pkd, 'pv_sink'], den.keys)
            self.fw.op('dve', (lambda den=den: lambda e: e.reciprocal(out=den.ap[:, 0:256], in_=den.ap[:, 0:256]))(), den.keys, den.keys)
            self.TT('dve', mixed[:, 0:4, c * 64:(c + 1) * 64], pbo[:, 0:256].rearrange("p (g l) -> p g l", g=4),
                    den.ap[:, 0:256].rearrange("p (g l) -> p g l", g=4), ALU.mult, [pko] + den.keys, [('act8', g) for g in range(4)])
        if last:
            R = U
            if job < 2:
                self.tm_store(O['o_pk'][job], KNF.ap[:, T - 128:T], KNF.keys, 128, 128)
                self.DMA('pool', O['o_pv'][job], VLF.ap[0:128, 0:128], VLF.keys, ['out'])
            else:
                self.tm_store(O['o_sk'], KNF.ap[:, 0:64], KNF.keys, 64, 128)
                self.DMA('pool', O['o_sv'], VLF.ap[0:64, 0:128], VLF.keys, ['out'])
        else:
            self.CP('dve', self.KWIN[:, 0:128], self.KWIN[:, T:T + 128], ['kwin'], ['kwin'])
            self.CP('act', self.VWP[:, 0, :], self.VWP[:, NU, :], ['vwp'], ['vwp'])
        if self.sub < 13:
            return
        self.out_proj(job, 1, T, W['odout'], True)


_CACHE = {}


def _get_nc(SEQ, do_l1=True):
    key = (SEQ, do_l1)
    if key not in _CACHE:
        _CACHE[key] = Builder(SEQ, do_l1).build()
    return _CACHE[key]


def make_in_maps(inp, SEQ):
    f = lambda a: np.ascontiguousarray(np.asarray(a, dtype=np.float32))
    hc = host_consts()
    shared = {
        't5_bias': f(inp['t5_bias']), 'norm1_g': f(inp['norm1_g']).reshape(16, 128), 'norm2_g': f(inp['norm2_g']).reshape(16, 128),
        'w_mod': f(inp['w_mod']), 'b_mod': f(inp['b_mod']), 'ffn_w_up': f(inp['ffn_w_up']),
        'ffn_conv_w': f(inp['ffn_conv_w']).reshape(2, 132, 128), 'ffn_conv_b': f(inp['ffn_conv_b']).reshape(2, 44, 128),
        'ffn_w_down': f(inp['ffn_w_down']), 'ev_w_in': f(inp['ev_w_in'])[0], 'ev_w_out': f(inp['ev_w_out'])[0],
        's5_a_re': f(inp['s5_a_re'])[0], 's5_a_im': f(inp['s5_a_im'])[0], 's5_log_dt': f(inp['s5_log_dt'])[0],
        's5_b_re': f(inp['s5_b_re'])[0], 's5_b_im': f(inp['s5_b_im'])[0],
        's5_c_re': f(inp['s5_c_re'])[0].reshape(512, 64), 's5_c_im': f(inp['s5_c_im'])[0].reshape(512, 64),
        's5_d': f(inp['s5_d'])[0], 's5_w_glu': f(inp['s5_w_glu'])[0], 's5_b_glu': f(inp['s5_b_glu'])[0].reshape(4, 128),
        'gla_w_gate2': f(inp['gla_w_gate2'])[0], 'gla_b_gate': f(inp['gla_b_gate'])[0].reshape(2, 128),
        'gla_norm_g': f(inp['gla_norm_g'])[0], 'od_w_in': f(inp['od_w_in'])[0], 'od_w_out': f(inp['od_w_out'])[0],
        'swa_q_norm': f(inp['swa_q_norm'])[0], 'swa_k_norm': f(inp['swa_k_norm'])[0], 'swa_sink': f(inp['swa_sink'])[0],
        'ssd_conv_w': f(inp['ssd_conv_w'])[0].reshape(32, 128), 'ssd_conv_b': f(inp['ssd_conv_b'])[0].reshape(8, 128),
        'ssd_dt_bias': f(inp['ssd_dt_bias'])[0], 'ssd_a_log': f(inp['ssd_a_log'])[0], 'ssd_d': f(inp['ssd_d'])[0],
        'ssd_norm_g': f(inp['ssd_norm_g'])[0], 'hc': hc,
    }
    xp, xs = f(inp['x_prompt']), f(inp['x_sample'])
    cp, cs = f(inp['c_prompt']), f(inp['c_sample'])
    maps = []
    for c in range(8):
        m = dict(shared)
        m['xp'] = xp[2 * c:2 * c + 2]
        m['xs'] = xs[c]
        m['cvec'] = np.concatenate([cp[2 * c], cp[2 * c + 1], cs[c]]).reshape(24, 128)
        m['st_s5re'] = f(inp['state_s5_re'])[0, c]
        m['st_s5im'] = f(inp['state_s5_im'])[0, c]
        m['st_gla'] = f(inp['state_gla'])[0, c].reshape(256, 128)
        m['ck'] = f(inp['cache_swa_k'])[0, c].reshape(128, 128)
        m['cv'] = f(inp['cache_swa_v'])[0, c].reshape(128, 128)
        m['st_ssd'] = f(inp['state_ssd'])[0, c].reshape(512, 128)
        m['st_ssdconv'] = f(inp['state_ssd_conv'])[0, c].reshape(24, 128)
        m['st_ffnconv'] = f(inp['state_ffn_conv'])[:, c].reshape(2, 88, 128)
        maps.append({k: np.ascontiguousarray(v) for k, v in m.items()})
    return maps


def assemble(res, SEQ):
    B, DB = 16, 8
    y_prompt = np.zeros((B, SEQ, D), np.float32)
    y_sample = np.zeros((DB, 64, D), np.float32)
    p_s5_re = np.zeros((1, B, 32, 64), np.float32)
    p_s5_im = np.zeros((1, B, 32, 64), np.float32)
    p_gla = np.zeros((1, B, 4, 64, 128), np.float32)
    p_swa_k = np.zeros((1, B, 128, 2, 64), np.float32)
    p_swa_v = np.zeros((1, B, 128, 2, 64), np.float32)
    p_ssd = np.zeros((1, B, 8, 64, 128), np.float32)
    p_ssd_conv = np.zeros((1, B, 3, 1024), np.float32)
    p_ffn_conv = np.zeros((2, B, 2, 2 * DFF), np.float32)
    s_s5_re = np.zeros((1, DB, 32, 64), np.float32)
    s_s5_im = np.zeros((1, DB, 32, 64), np.float32)
    s_gla = np.zeros((1, DB, 4, 64, 128), np.float32)
    s_swa_k = np.zeros((1, DB, 64, 2, 64), np.float32)
    s_swa_v = np.zeros((1, DB, 64, 2, 64), np.float32)
    s_ssd = np.zeros((1, DB, 8, 64, 128), np.float32)
    s_ssd_conv = np.zeros((1, DB, 3, 1024), np.float32)
    s_ffn_conv = np.zeros((2, DB, 2, 2 * DFF), np.float32)
    for c in range(8):
        r = res[c]
        y_prompt[2 * c:2 * c + 2] = r['yp']
        y_sample[c] = r['ys']
        for jb in range(2):
            b = 2 * c + jb
            p_s5_re[0, b] = r['o_s5re'][jb]
            p_s5_im[0, b] = r['o_s5im'][jb]
            p_gla[0, b] = r['o_gla'][jb].reshape(4, 64, 128)
            p_swa_k[0, b] = r['o_pk'][jb].reshape(128, 2, 64)
            p_swa_v[0, b] = r['o_pv'][jb].reshape(128, 2, 64)
            p_ssd[0, b] = r['o_ssd'][jb].reshape(8, 64, 128)
            p_ssd_conv[0, b] = r['o_ssdconv'][jb]
            p_ffn_conv[:, b] = r['o_ffnconv'][jb]
        s_s5_re[0, c] = r['o_s5re'][2]
        s_s5_im[0, c] = r['o_s5im'][2]
        s_gla[0, c] = r['o_gla'][2].reshape(4, 64, 128)
        s_swa_k[0, c] = r['o_sk'].reshape(64, 2, 64)
        s_swa_v[0, c] = r['o_sv'].reshape(64, 2, 64)
        s_ssd[0, c] = r['o_ssd'][2].reshape(8, 64, 128)
        s_ssd_conv[0, c] = r['o_ssdconv'][2]
        s_ffn_conv[:, c] = r['o_ffnconv'][2]
    return (y_prompt, y_sample, p_s5_re, p_s5_im, p_gla, p_swa_k, p_swa_v, p_ssd, p_ssd_conv, p_ffn_conv,
            s_s5_re, s_s5_im, s_gla, s_swa_k, s_swa_v, s_ssd, s_ssd_conv, s_ffn_conv)


def kernel(**inputs):
    SEQ = int(np.asarray(inputs['x_prompt']).shape[1])
    nc = _get_nc(SEQ)
    in_maps = make_in_maps(inputs, SEQ)
    res = run_bass_kernel_spmd(nc, in_maps, core_ids=list(range(8)))
    return assemble(res.results, SEQ)
```

```python
import math
import os
from contextlib import ExitStack
import numpy as np
import concourse.bass as bass
import concourse.mybir as mybir
from concourse.bass_utils import run_bass_kernel_spmd

F32 = mybir.dt.float32
BF16 = mybir.dt.bfloat16
I32 = mybir.dt.int32
AF = mybir.ActivationFunctionType
ALU = mybir.AluOpType

NDMA = 48
D = 1024
DFF = 2816
EVEN_IN = 2064
ODD_IN = 2312
EPS = 1e-6


class FW:
    ENGS = ('pe', 'dve', 'act', 'pool', 'sp')

    def __init__(self, nc, stack):
        self.nc = nc
        self.stack = stack
        self.sem = {}
        self.prog = {e: [] for e in self.ENGS}
        self.cnt = {}
        self.seen = {e: {} for e in self.ENGS}
        self.bufs = {}
        for e in self.ENGS:
            self._mksem('E_' + e)
        self.dma_pool = {'sp': [f'DS{i}' for i in range(32)], 'pool': [f'DP{i}' for i in range(24)],
                         'act': [f'DA{i}' for i in range(8)]}
        for q, names in self.dma_pool.items():
            for n in names:
                self._mksem(n)
        self.dma_rr = {'sp': 0, 'pool': 0, 'act': 0}
        self.dma_last_clock = {}
        self.nops = 0
        self.noself = ('pe', 'sp') + tuple(os.environ.get('KNOSELF', '').split(','))

    def _mksem(self, name):
        self.sem[name] = self.stack.enter_context(self.nc.semaphore(name))
        self.cnt[name] = 0

    def _deps(self, reads, writes):
        deps = []
        for b in reads:
            st = self.bufs.get(b)
            if st and st['w'] is not None:
                deps.append(('raw', st['w']))
        for b in writes:
            st = self.bufs.get(b)
            if st:
                if st['w'] is not None:
                    deps.append(('waw', st['w']))
                for r in st['r']:
                    deps.append(('war', r))
        return deps

    def _waits(self, eng, deps):
        seen = self.seen[eng]
        own = 'E_' + eng
        need = {}
        used = []
        for kind, (s, v, clock) in deps:
            if s == own and (eng in self.noself or kind != 'raw'):
                continue
            used.append((s, v, clock))
            if seen.get(s, 0) >= v:
                continue
            if need.get(s, 0) < v:
                need[s] = v
        for s, v, clock in used:
            for cs, cv in clock.items():
                if cs == own:
                    continue
                if seen.get(cs, 0) < cv:
                    seen[cs] = cv
            if seen.get(s, 0) < v:
                seen[s] = v
        return sorted(need.items())

    def _record(self, ev, reads, writes):
        for b in reads:
            st = self.bufs.setdefault(b, {'w': None, 'r': []})
            st['r'].append(ev)
            if len(st['r']) > 96:
                st['r'] = st['r'][-96:]
        for b in writes:
            self.bufs[b] = {'w': ev, 'r': []}

    def op(self, eng, fn, reads=(), writes=()):
        deps = self._deps(reads, writes)
        waits = self._waits(eng, deps)
        own = 'E_' + eng
        self.cnt[own] += 1
        v = self.cnt[own]
        clock = dict(self.seen[eng])
        clock[own] = v
        self.prog[eng].append((waits, fn, (own, 1)))
        self._record((own, v, clock), reads, writes)
        self.nops += 1

    def dma(self, q, fn, reads=(), writes=()):
        deps = self._deps(reads, writes)
        names = self.dma_pool[q]
        s = names[self.dma_rr[q]]
        self.dma_rr[q] = (self.dma_rr[q] + 1) % len(names)
        if self.cnt[s] > 0:
            deps.append(('raw', (s, self.cnt[s], self.dma_last_clock.get(s, {}))))
        waits = self._waits(q, deps)
        self.cnt[s] += 16
        v = self.cnt[s]
        clock = dict(self.seen[q])
        clock.pop('E_' + q, None)
        self.dma_last_clock[s] = clock
        self.prog[q].append((waits, fn, (s, 16)))
        self._record((s, v, clock), reads, writes)
        self.nops += 1

    def finish_waits(self, eng='sp'):
        waits = [(s, v) for s, v in self.cnt.items() if v > 0 and s != 'E_' + eng]
        self.prog[eng].append((waits, None, None))

    def replay(self):
        nc = self.nc
        with nc.Block() as block:
            def mk(engname):
                def body(e):
                    for waits, fn, inc in self.prog[engname]:
                        for s, v in waits:
                            e.wait_ge(self.sem[s], v)
                        if fn is not None:
                            ins = fn(e)
                            ins.then_inc(self.sem[inc[0]], inc[1])
                return body
            block.tensor(mk('pe'))
            block.vector(mk('dve'))
            block.scalar(mk('act'))
            block.gpsimd(mk('pool'))
            block.sync(mk('sp'))


HC_IDENT, HC_MC, HC_MG, HC_MB, HC_S5M, HC_S5E = 0, 128, 256, 384, 512, 640
HC_CS = 768
HC_RST = 1024
HC_OHW = 1536
NHC = 1920


def _t5_bucket_np(rel):
    import jax
    import jax.numpy as jnp
    with jax.default_device(jax.devices('cpu')[0]):
        rel = jnp.asarray(rel, jnp.int32)
        nb = 16
        max_exact = 8
        ret = (rel > 0).astype(jnp.int32) * nb
        n = jnp.abs(rel)
        nf = jnp.maximum(n, 1).astype(jnp.float32)
        large = max_exact + (jnp.log(nf / max_exact) / math.log(128 / max_exact) * (nb - max_exact)).astype(jnp.int32)
        large = jnp.minimum(large, nb - 1)
        out = ret + jnp.where(n < max_exact, n, large)
        return np.asarray(out)


def host_consts():
    hc = np.zeros((128, NHC), np.float32)
    j = np.arange(128)[:, None]
    l = np.arange(128)[None, :]
    same = (j // 64) == (l // 64)
    hc[:, HC_IDENT:HC_IDENT + 128] = (j == l)
    hc[:, HC_MC:HC_MC + 128] = same & (j <= l)
    hc[:, HC_MG:HC_MG + 128] = same & (j > l)
    hc[:, HC_MB:HC_MB + 128] = same
    s_sub = j // 16
    hi = j % 16
    ho = l // 8
    l_sub = l % 8
    hc[:, HC_S5M:HC_S5M + 128] = (l_sub >= s_sub)
    hc[:, HC_S5E:HC_S5E + 128] = (l_sub == s_sub) & (ho == hi)
    for c in range(2):
        hc[:, HC_CS + c * 128:HC_CS + (c + 1) * 128] = ((j // 64) == c)
    rst = np.ones(512, np.float32)
    rst[::64] = 0.0
    hc[:, HC_RST:HC_RST + 512] = rst[None, :]
    m = np.arange(128)
    d = np.where(m < 64, m, m - 128)
    for kb in range(3):
        rel = kb * 64 - 128 - d
        bk = _t5_bucket_np(rel)
        for mm in range(128):
            hc[bk[mm], HC_OHW + kb * 128 + mm] = 1.0
    return hc


class Reg:
    def __init__(self, ap, keys):
        self.ap = ap
        self.keys = list(keys)


class Builder:
    def __init__(self, SEQ, do_l1=True):
        self.SEQ = SEQ
        self.do_l1 = do_l1
        self.nc = bass.Bass("TRN2", target_bir_lowering=False)
        self.bank_rr = 0
        self.stg_rr = 0
        self.stage = int(os.environ.get('KSTAGE', '99'))

    def din(self, name, shape, dt=F32):
        return self.nc.dram_tensor(name, list(shape), dt, kind="ExternalInput").ap()

    def dout(self, name, shape, dt=F32):
        return self.nc.dram_tensor(name, list(shape), dt, kind="ExternalOutput").ap()

    def dscr(self, name, shape, dt):
        return self.nc.dram_tensor(name, list(shape), dt, kind="Internal").ap()

    def sb(self, name, shape, dt):
        return self.st.enter_context(self.nc.sbuf_tensor(name, list(shape), dt))

    def MM(self, out, lhsT, rhs, st, sp, r, w):
        self.fw.op('pe', lambda e: e.matmul(out, lhsT=lhsT, rhs=rhs, start=st, stop=sp), r, w)

    def TR(self, out, in_, idn, r, w):
        self.fw.op('pe', lambda e: e.transpose(out=out, in_=in_, identity=idn), r, w)

    def ACT(self, out, in_, func, r, w, bias=None, scale=None, eng='act'):
        kw = {}
        if bias is not None:
            kw['bias'] = bias
        if scale is not None:
            kw['scale'] = scale
        self.fw.op(eng, lambda e: e.activation(out=out, in_=in_, func=func, **kw), r, w)

    def TT(self, eng, out, a, b, op, r, w):
        self.fw.op(eng, lambda e: e.tensor_tensor(out=out, in0=a, in1=b, op=op), r, w)

    def TS(self, eng, out, a, s1, s2, op0, op1, r, w):
        if op1 is None:
            self.fw.op(eng, lambda e: e.tensor_scalar(out=out, in0=a, scalar1=s1, scalar2=None, op0=op0), r, w)
        else:
            self.fw.op(eng, lambda e: e.tensor_scalar(out=out, in0=a, scalar1=s1, scalar2=s2, op0=op0, op1=op1), r, w)

    def STT(self, out, a, s, b, op0, op1, r, w):
        self.fw.op('dve', lambda e: e.scalar_tensor_tensor(out=out, in0=a, scalar=s, in1=b, op0=op0, op1=op1), r, w)

    def CP(self, eng, out, in_, r, w):
        if eng == 'act':
            self.fw.op('act', lambda e: e.copy(out=out, in_=in_), r, w)
        else:
            self.fw.op(eng, lambda e: e.tensor_copy(out=out, in_=in_), r, w)

    def MEMSET(self, eng, ap, val, w):
        self.fw.op(eng, lambda e: e.memset(ap, val), (), w)

    def DMA(self, q, out, in_, r, w, nonc=False):
        if nonc:
            self.fw.dma(q, lambda e: e.dma_start(out=out, in_=in_, allow_slow_non_contiguous=True), r, w)
        else:
            self.fw.dma(q, lambda e: e.dma_start(out=out, in_=in_), r, w)

    def bank(self):
        i = self.bank_rr
        self.bank_rr = (self.bank_rr + 1) % 8
        return self.PB[i], f'pb{i}'

    def scr(self, off_bytes, nbytes, dt, shape_tail=None, parts=128):
        assert off_bytes % 4 == 0 and off_bytes + nbytes <= self.SCRN * 4, (off_bytes, nbytes)
        a = self.SCR[0:parts, off_bytes // 4:(off_bytes + nbytes + 3) // 4]
        if dt == BF16:
            a = a.bitcast(BF16)
        keys = [('scr', pg) for pg in range(off_bytes // 2048, (off_bytes + nbytes - 1) // 2048 + 1)]
        return Reg(a, keys)

    def fm_load(self, dst_ap, dst_keys, src_ap, R, W, src_keys=()):
        s = self.stg_rr
        self.stg_rr = (self.stg_rr + 1) % 2
        stg = self.STG[s]
        self.DMA('sp', stg[0:R, 0:W], src_ap, list(src_keys), [f'stg{s}'])
        pb, pk = self.bank()
        self.TR(pb[0:W, 0:R], stg[0:R, 0:W], self.IDf[0:R, 0:R], [f'stg{s}', 'hc'], [pk])
        self.CP('dve', dst_ap, pb[0:W, 0:R], [pk], dst_keys)

    def tm_store(self, dst_ap, src_ap, src_keys, R, W, q='pool'):
        s = self.stg_rr
        self.stg_rr = (self.stg_rr + 1) % 2
        stg = self.STG[s]
        pb, pk = self.bank()
        self.TR(pb[0:R, 0:W], src_ap, self.IDf[0:W, 0:W], list(src_keys) + ['hc'], [pk])
        self.CP('dve', stg[0:R, 0:W], pb[0:R, 0:W], [pk], [f'stg{s}'])
        self.DMA(q, dst_ap, stg[0:R, 0:W], [f'stg{s}'], ['out'])

    def build(self):
        nc = self.nc
        SEQ = self.SEQ
        I = {}
        I['xp'] = self.din('xp', [2, SEQ, D])
        I['xs'] = self.din('xs', [64, D])
        I['cvec'] = self.din('cvec', [24, 128])
        I['st_s5re'] = self.din('st_s5re', [32, 64])
        I['st_s5im'] = self.din('st_s5im', [32, 64])
        I['st_gla'] = self.din('st_gla', [256, 128])
        I['ck'] = self.din('ck', [128, 128])
        I['cv'] = self.din('cv', [128, 128])
        I['st_ssd'] = self.din('st_ssd', [512, 128])
        I['st_ssdconv'] = self.din('st_ssdconv', [24, 128])
        I['st_ffnconv'] = self.din('st_ffnconv', [2, 88, 128])
        for nm, shp in [('t5_bias', [32, 8]), ('norm1_g', [16, 128]), ('norm2_g', [16, 128]), ('w_mod', [2, D, 6144]),
                        ('b_mod', [2, 6144]), ('ffn_w_up', [2, D, 2 * DFF]), ('ffn_conv_w', [2, 132, 128]),
                        ('ffn_conv_b', [2, 44, 128]), ('ffn_w_down', [2, DFF, D]), ('ev_w_in', [D, EVEN_IN]),
                        ('ev_w_out', [D, D]), ('s5_a_re', [32, 64]), ('s5_a_im', [32, 64]), ('s5_log_dt', [32]),
                        ('s5_b_re', [32, 64, 16]), ('s5_b_im', [32, 64, 16]), ('s5_c_re', [512, 64]),
                        ('s5_c_im', [512, 64]), ('s5_d', [512]), ('s5_w_glu', [512, 512]), ('s5_b_glu', [4, 128]),
                        ('gla_w_gate2', [16, 256]), ('gla_b_gate', [2, 128]), ('gla_norm_g', [128]),
                        ('od_w_in', [D, ODD_IN]), ('od_w_out', [D, D]), ('swa_q_norm', [64]), ('swa_k_norm', [64]),
                        ('swa_sink', [8]), ('ssd_conv_w', [32, 128]), ('ssd_conv_b', [8, 128]), ('ssd_dt_bias', [8]),
                        ('ssd_a_log', [8]), ('ssd_d', [8]), ('ssd_norm_g', [512]), ('hc', [128, NHC])]:
            I[nm] = self.din(nm, shp)
        self.I = I
        O = {}
        O['yp'] = self.dout('yp', [2, SEQ, D])
        O['ys'] = self.dout('ys', [64, D])
        O['o_s5re'] = self.dout('o_s5re', [3, 32, 64])
        O['o_s5im'] = self.dout('o_s5im', [3, 32, 64])
        O['o_gla'] = self.dout('o_gla', [3, 256, 128])
        O['o_pk'] = self.dout('o_pk', [2, 128, 128])
        O['o_pv'] = self.dout('o_pv', [2, 128, 128])
        O['o_sk'] = self.dout('o_sk', [64, 128])
        O['o_sv'] = self.dout('o_sv', [64, 128])
        O['o_ssd'] = self.dout('o_ssd', [3, 512, 128])
        O['o_ssdconv'] = self.dout('o_ssdconv', [3, 3, 1024])
        O['o_ffnconv'] = self.dout('o_ffnconv', [3, 2, 2, 2 * DFF])
        self.O = O
        W = {}
        W['up'] = self.dscr('w_up_s', [2, D, 2 * DFF], BF16)
        W['dn'] = self.dscr('w_dn_s', [2, DFF, D], BF16)
        W['evin'] = self.dscr('w_evin_s', [D, EVEN_IN], BF16)
        W['evout'] = self.dscr('w_evout_s', [D, D], BF16)
        W['odin'] = self.dscr('w_odin_s', [D, ODD_IN], BF16)
        W['odout'] = self.dscr('w_odout_s', [D, D], BF16)
        W['s5t0'] = self.dscr('w_s5t0', [128, 32 * 128], BF16)
        W['s5win'] = self.dscr('w_s5win', [128, 32 * 128], BF16)
        W['s5wout'] = self.dscr('w_s5wout', [64, 2 * 32 * 128], BF16)
        W['biasd'] = self.dscr('w_biasd', [8, 3, 64 * 128], F32)
        W['evin_lo'] = self.dscr('w_evin_lo', [D, 512], BF16)
        self.W = W

        with ExitStack() as st:
            self.st = st
            self.fw = FW(nc, st)
            self.alloc()
            self.prologue()
            for job in [int(x) for x in os.environ.get('KJOBS', '0,1,2').split(',')]:
                if self.stage >= 6:
                    self.run_job(job)
            self.fw.finish_waits('sp')
            self.fw.replay()
        return nc

    def alloc(self):
        nc = self.nc
        self.PB = [self.st.enter_context(nc.psum_tensor(f'pb{i}', [128, 512], F32)) for i in range(8)]
        self.HC = self.sb('HC', [128, NHC], F32)
        self.IDf = self.HC[:, HC_IDENT:HC_IDENT + 128]
        self.IDb = self.sb('IDb', [128, 128], BF16)
        self.ONESb = self.sb('ONESb', [128, 128], BF16)
        self.MBb = self.sb('MBb', [128, 128], BF16)
        self.STG = [self.sb(f'STG{i}', [128, 128], F32) for i in range(2)]
        self.PVN = 1100
        self.PV = self.sb('PV', [128, self.PVN], F32)
        self.BC = self.sb('BC', [128, 640], F32)
        self.XRES = self.sb('XRES', [128, 8, 512], F32)
        self.ACT8 = self.sb('ACT8', [128, 8, 512], BF16)
        self.WS = [self.sb(f'WS{i}', [128, 4096], BF16) for i in range(4)]
        self.ws_rr = 0
        self.WG2 = self.sb('WG2', [16, 256], BF16)
        self.WGL = self.sb('WGL', [128, 8, 16], BF16)
        self.WGLU = self.sb('WGLU', [128, 4, 512], BF16)
        self.A1 = self.sb('A1', [64, 2, 32], F32)
        self.A2 = self.sb('A2', [64, 2, 32], F32)
        self.S3 = self.sb('S3', [64, 3, 32], F32)
        self.GS = self.sb('GS', [128, 2, 128], F32)
        self.GSH = self.sb('GSH', [128, 9, 2, 128], BF16)
        self.FT = self.sb('FT', [128, 2, 44, 2], F32)
        self.FCORR = self.sb('FCORR', [128, 44, 2], F32)
        self.FTMP = self.sb('FTMP', [128, 44, 2], F32)
        self.HNLO = self.sb('HNLO', [128, 8, 512], BF16)
        self.ST = self.sb('ST', [128, 512], F32)
        self.STBH = self.sb('STBH', [128, 9, 512], BF16)
        self.CT = self.sb('CT', [128, 8, 4], F32)
        self.CCORR = self.sb('CCORR', [128, 8, 3], F32)
        self.CTMP = self.sb('CTMP', [128, 8, 3], F32)
        self.KWIN = self.sb('KWIN', [128, 640], BF16)
        self.VWP = self.sb('VWP', [128, 5, 128], BF16)
        self.BIE = self.sb('BIE', [128, 2, 512], F32)
        self.BIO = self.sb('BIO', [128, 2, 512], F32)
        self.WDT = self.sb('WDT', [128, 8, 8], BF16)
        self.T5T = self.sb('T5T', [32, 8], F32)
        self.WROW = self.sb('WROW', [8, 384], F32)
        self.SCRN = 18432
        self.SCR = self.sb('SCR', [128, self.SCRN], F32)
        c = 0
        self.pv = {}
        for nm, n in [('n1g', 16), ('n2g', 16), ('cfm', 24), ('csilu', 24), ('mod', 288), ('gs1', 48), ('gs2', 48),
                      ('fcw', 264), ('fcb', 88), ('bglu', 4), ('negbg', 2), ('glang', 1), ('eps', 1), ('one', 1),
                      ('scw', 32), ('scb', 8), ('qng', 1), ('kng', 1), ('sinkexp', 4), ('dvec', 32), ('tmp', 64)]:
            self.pv[nm] = c
            c += n
        assert c <= self.PVN, c

    def pvc(self, nm, i=0, n=1):
        c = self.pv[nm] + i
        return self.PV[:, c:c + n]

    def prologue(self):
        I, W = self.I, self.W
        fw = self.fw
        self.DMA('sp', self.HC[:], I['hc'], ['in_hc'], ['hc'])
        self.CP('dve', self.IDb[:], self.IDf, ['hc'], ['idb'])
        self.MEMSET('pool', self.ONESb[:], 1.0, ['onesb'])
        self.CP('dve', self.MBb[:], self.HC[:, HC_MB:HC_MB + 128], ['hc'], ['mbb'])
        self.MEMSET('pool', self.pvc('eps'), EPS, ['pv_eps'])
        self.MEMSET('pool', self.pvc('one'), 1.0, ['pv_one'])
        if self.stage < 1:
            return
        def cast_mixer(pairs):
            for nm, srcn in pairs:
                for r0 in range(0, D, 256):
                    self.DMA('pool', W[nm][r0:r0 + 256, :], I[srcn][r0:r0 + 256, :], ['in_w'], ['w_' + nm])

        def cast_ffn(l):
            for r0 in range(0, D, 128):
                self.DMA('pool', W['up'][l, r0:r0 + 128, :], I['ffn_w_up'][l, r0:r0 + 128, :], ['in_w'], ['w_up'])
            for r0 in range(0, DFF, 256):
                self.DMA('pool', W['dn'][l, r0:r0 + 256, :], I['ffn_w_down'][l, r0:r0 + 256, :], ['in_w'], ['w_dn'])
        cast_mixer([('evin', 'ev_w_in'), ('evout', 'ev_w_out')])
        cast_ffn(0)
        cast_mixer([('odin', 'od_w_in'), ('odout', 'od_w_out')])
        cast_ffn(1)
        if self.stage < 2:
            return
        for r in range(8):
            wf = self.scr(0, 2048, F32)
            hi = self.scr(2048, 1024, BF16)
            d32 = self.scr(4096, 2048, F32)
            lo = self.scr(6144, 1024, BF16)
            self.DMA('sp', wf.ap[:, 0:512], I['ev_w_in'][r * 128:(r + 1) * 128, 512:1024], ['in_w'], wf.keys)
            self.CP('dve', hi.ap[:, 0:512], wf.ap[:, 0:512], wf.keys, hi.keys)
            self.TT('dve', d32.ap[:, 0:512], wf.ap[:, 0:512], hi.ap[:, 0:512], ALU.subtract, wf.keys + hi.keys, d32.keys)
            self.CP('act', lo.ap[:, 0:512], d32.ap[:, 0:512], d32.keys, lo.keys)
            self.DMA('sp', W['evin_lo'][r * 128:(r + 1) * 128, :], lo.ap[:, 0:512], lo.keys, ['w_evin_lo'])
        self.DMA('pool', self.WG2[:], I['gla_w_gate2'], ['in_w'], ['wg2'])
        self.DMA('pool', self.WGL[:], I['ev_w_in'][:, 1536:1552].rearrange("(k p) n -> p k n", p=128), ['in_w'], ['wgl'])
        self.DMA('pool', self.WGLU[:], I['s5_w_glu'].rearrange("(k p) n -> p k n", p=128), ['in_w'], ['wglu'])
        if self.stage < 3:
            return
        self.fm_load(self.pvc('n1g', 0, 16), ['pv_n1g'], I['norm1_g'], 16, 128)
        self.fm_load(self.pvc('n2g', 0, 16), ['pv_n2g'], I['norm2_g'], 16, 128)
        self.fm_load(self.pvc('cfm', 0, 24), ['pv_cfm'], I['cvec'], 24, 128)
        for l in range(2):
            self.fm_load(self.pvc('fcw', l * 132, 128), ['pv_fcw'], I['ffn_conv_w'][l, 0:128, :], 128, 128)
            self.fm_load(self.pvc('fcw', l * 132 + 128, 4), ['pv_fcw'], I['ffn_conv_w'][l, 128:132, :], 4, 128)
            self.fm_load(self.pvc('fcb', l * 44, 44), ['pv_fcb'], I['ffn_conv_b'][l], 44, 128)
        self.fm_load(self.pvc('bglu', 0, 4), ['pv_bglu'], I['s5_b_glu'], 4, 128)
        self.fm_load(self.pvc('negbg', 0, 2), ['pv_negbg'], I['gla_b_gate'], 2, 128)
        self.TS('dve', self.pvc('negbg', 0, 2), self.pvc('negbg', 0, 2), -1.0, None, ALU.mult, None, ['pv_negbg'], ['pv_negbg'])
        self.DMA('sp', self.pvc('glang'), I['gla_norm_g'].rearrange("(p o) -> p o", o=1), ['in_w'], ['pv_glang'], nonc=True)
        if self.stage < 4:
            return
        self.ACT(self.pvc('csilu', 0, 24), self.pvc('cfm', 0, 24), AF.Silu, ['pv_cfm'], ['pv_csilu'])
        csl = self.PV[:, self.pv['csilu']:self.pv['csilu'] + 24].rearrange("p (j k) -> p k j", k=8)
        wm = self.scr(0, 8 * 512 * 4, F32)
        wmv = wm.ap.rearrange("p (k n) -> p k n", k=8)
        brow = self.scr(16384, 512 * 4, F32, parts=1)
        onesrow = self.scr(16384 + 2048, 16, F32, parts=1)
        self.MEMSET('dve', onesrow.ap[0:1, 0:3], 1.0, onesrow.keys)
        for l in range(2):
            for cb in range(12):
                self.DMA('sp', wmv, I['w_mod'][l, :, cb * 512:(cb + 1) * 512].rearrange("(k p) n -> p k n", p=128),
                         ['in_w'], wm.keys)
                self.DMA('sp', brow.ap[0:1, 0:512], I['b_mod'][l:l + 1, cb * 512:(cb + 1) * 512], ['in_w'], brow.keys)
                pb, pk = self.bank()
                for mc in range(4):
                    for k in range(8):
                        self.MM(pb[:, mc * 4:mc * 4 + 3], wmv[:, k, mc * 128:(mc + 1) * 128], csl[:, k, :],
                                k == 0, False, wm.keys + ['pv_csilu'], [pk])
                    self.MM(pb[:, mc * 4:mc * 4 + 3], brow.ap[0:1, mc * 128:(mc + 1) * 128], onesrow.ap[0:1, 0:3],
                            False, True, brow.keys + onesrow.keys, [pk])
                for job in range(3):
                    dst = self.pvc('mod', job * 96 + l * 48 + cb * 4, 4)
                    self.CP('dve', dst, pb[:, job:job + 13:4], [pk], ['pv_mod'])
        for job in range(3):
            for l in range(2):
                for which, gname, scoff in [(0, 'n1g', 8), (1, 'n2g', 32)]:
                    dst = self.pvc('gs1' if which == 0 else 'gs2', (job * 2 + l) * 8, 8)
                    sc = self.pvc('mod', job * 96 + l * 48 + scoff, 8)
                    g = self.pvc(gname, l * 8, 8)
                    self.STT(dst, sc, 1.0, g, ALU.add, ALU.mult, ['pv_mod', 'pv_' + gname], ['pv_gs'])
        if self.stage < 5:
            return
        self.s5_prologue()
        if self.do_l1:
            self.l1_prologue()

    def modv(self, job, l, which, k0=0, n=8):
        return self.pvc('mod', job * 96 + l * 48 + which * 8 + k0, n)

    def s5_prologue(self):
        I, W = self.I, self.W
        off = [0]

        def A(n, parts=64):
            r = self.scr(off[0], n * 4, F32, parts=parts)
            off[0] += n * 4
            return r
        are, aim, ldt = A(32), A(32), A(32)
        zr, zi, wr, wi, t1, t2, t3 = A(32), A(32), A(32), A(32), A(32), A(32), A(32)
        LR, LI = A(9 * 32), A(9 * 32)
        gr, gi = A(32), A(32)
        Bre, Bim = A(512), A(512)
        BBr, BBi = A(512), A(512)
        Cre, Cim = A(512), A(512)
        GC = 8
        WTr, WTi = A(GC * 128), A(GC * 128)
        Qr, Qi = A(GC * 128), A(GC * 128)
        Qsr, Qsi = A(GC * 128), A(GC * 128)
        tA, tB = A(GC * 128), A(GC * 128)
        stgb = self.scr(off[0], 2 * 128 * 2, BF16)
        off[0] += 512
        tmpf = self.scr(off[0], 512, F32)
        off[0] += 512
        assert off[0] <= self.SCRN * 4, off[0]
        dve = 'dve'

        def tt(o, a, b, op):
            self.TT(dve, o.ap if isinstance(o, Reg) else o[0], a.ap if isinstance(a, Reg) else a[0],
                    b.ap if isinstance(b, Reg) else b[0], op,
                    (a.keys if isinstance(a, Reg) else a[1]) + (b.keys if isinstance(b, Reg) else b[1]),
                    o.keys if isinstance(o, Reg) else o[1])

        def V(reg, ap):
            return (ap, reg.keys)
        self.fm_load(are.ap, are.keys, I['s5_a_re'], 32, 64)
        self.fm_load(aim.ap, aim.keys, I['s5_a_im'], 32, 64)
        self.DMA('sp', ldt.ap, I['s5_log_dt'].partition_broadcast(64), ['in_w'], ldt.keys)
        self.DMA('sp', Bre.ap.rearrange("p (g h) -> p g h", h=16), I['s5_b_re'].rearrange("g p h -> p g h"), ['in_w'], Bre.keys)
        self.DMA('sp', Bim.ap.rearrange("p (g h) -> p g h", h=16), I['s5_b_im'].rearrange("g p h -> p g h"), ['in_w'], Bim.keys)
        for q4 in range(4):
            self.fm_load(Cre.ap[:, q4 * 128:(q4 + 1) * 128], Cre.keys, I['s5_c_re'][q4 * 128:(q4 + 1) * 128, :], 128, 64)
            self.fm_load(Cim.ap[:, q4 * 128:(q4 + 1) * 128], Cim.keys, I['s5_c_im'][q4 * 128:(q4 + 1) * 128, :], 128, 64)
        self.ACT(ldt.ap, ldt.ap, AF.Exp, ldt.keys, ldt.keys)
        self.STT(zr.ap, are.ap, 1.0 / 256.0, ldt.ap, ALU.mult, ALU.mult, are.keys + ldt.keys, zr.keys)
        self.STT(zi.ap, aim.ap, 1.0 / 256.0, ldt.ap, ALU.mult, ALU.mult, aim.keys + ldt.keys, zi.keys)

        def cmul(or_, oi_, ar, ai, br, bi):
            tt(t1, ar, br, ALU.mult)
            tt(t2, ai, bi, ALU.mult)
            tt(or_, t1, t2, ALU.subtract)
            tt(t1, ar, bi, ALU.mult)
            tt(t2, ai, br, ALU.mult)
            tt(oi_, t1, t2, ALU.add)
        hr, hi_ = wr, wi
        self.TS(dve, hr.ap, zr.ap, 0.2, 1.0, ALU.mult, ALU.add, zr.keys, hr.keys)
        self.TS(dve, hi_.ap, zi.ap, 0.2, None, ALU.mult, None, zi.keys, hi_.keys)
        nr = t3
        for dv in (4.0, 3.0, 2.0):
            cmul(nr, gi, zr, zi, hr, hi_)
            self.TS(dve, hr.ap, nr.ap, 1.0 / dv, 1.0, ALU.mult, ALU.add, nr.keys, hr.keys)
            self.TS(dve, hi_.ap, gi.ap, 1.0 / dv, None, ALU.mult, None, gi.keys, hi_.keys)
        cmul(nr, gi, zr, zi, hr, hi_)
        self.CP(dve, wr.ap, nr.ap, nr.keys, wr.keys)
        self.CP(dve, wi.ap, gi.ap, gi.keys, wi.keys)
        for _ in range(8):
            self.TS(dve, zr.ap, wr.ap, 2.0, None, ALU.add, None, wr.keys, zr.keys)
            cmul(nr, gi, wr, wi, zr, wi)
            self.CP(dve, wr.ap, nr.ap, nr.keys, wr.keys)
            self.CP(dve, wi.ap, gi.ap, gi.keys, wi.keys)
        LRv = LR.ap.rearrange("p (n g) -> p n g", g=32)
        LIv = LI.ap.rearrange("p (n g) -> p n g", g=32)
        self.MEMSET(dve, LRv[:, 0, :], 1.0, LR.keys)
        self.MEMSET(dve, LIv[:, 0, :], 0.0, LI.keys)
        self.TS(dve, LRv[:, 1, :], wr.ap, 1.0, None, ALU.add, None, wr.keys, LR.keys)
        self.CP(dve, LIv[:, 1, :], wi.ap, wi.keys, LI.keys)
        for n in range(1, 8):
            cmul(V(LR, LRv[:, n + 1, :]), V(LI, LIv[:, n + 1, :]), V(LR, LRv[:, n, :]), V(LI, LIv[:, n, :]),
                 V(LR, LRv[:, 1, :]), V(LI, LIv[:, 1, :]))
        tt(t3, are, are, ALU.mult)
        tt(zr, aim, aim, ALU.mult)
        tt(t3, t3, zr, ALU.add)
        self.fw.op(dve, lambda e: e.reciprocal(out=t3.ap, in_=t3.ap), t3.keys, t3.keys)
        self.TS(dve, zi.ap, aim.ap, -1.0, None, ALU.mult, None, aim.keys, zi.keys)
        cmul(gr, gi, wr, wi, are, zi)
        tt(gr, gr, t3, ALU.mult)
        tt(gi, gi, t3, ALU.mult)
        grb = (gr.ap.unsqueeze(2).broadcast_to([64, 32, 16]), gr.keys)
        gib = (gi.ap.unsqueeze(2).broadcast_to([64, 32, 16]), gi.keys)
        v3 = lambda r: (r.ap.rearrange("p (g h) -> p g h", h=16), r.keys)
        tA3 = (tA.ap[:, 0:512].rearrange("p (g h) -> p g h", h=16), tA.keys)
        tB3 = (tB.ap[:, 0:512].rearrange("p (g h) -> p g h", h=16), tB.keys)
        tt(tA3, grb, v3(Bre), ALU.mult)
        tt(tB3, gib, v3(Bim), ALU.mult)
        tt(v3(BBr), tA3, tB3, ALU.subtract)
        tt(tA3, grb, v3(Bim), ALU.mult)
        tt(tB3, gib, v3(Bre), ALU.mult)
        tt(v3(BBi), tA3, tB3, ALU.add)
        l8r = (LRv[:, 8, :], LR.keys)
        l8i = (LIv[:, 8, :], LI.keys)
        tt(t1, l8r, l8r, ALU.mult)
        tt(t2, l8i, l8i, ALU.mult)
        tt(t1, t1, t2, ALU.add)
        self.fw.op(dve, lambda e: e.reciprocal(out=t1.ap, in_=t1.ap), t1.keys, t1.keys)
        tt(zr, l8r, t1, ALU.mult)
        tt(zi, l8i, t1, ALU.mult)
        self.CP(dve, self.A1[:, 0, :], LRv[:, 8, :], LR.keys, ['a12'])
        self.CP(dve, self.A1[:, 1, :], LRv[:, 8, :], LR.keys, ['a12'])
        self.TS(dve, self.A2[:, 0, :], LIv[:, 8, :], -1.0, None, ALU.mult, None, LI.keys, ['a12'])
        self.CP(dve, self.A2[:, 1, :], LIv[:, 8, :], LI.keys, ['a12'])
        dsrc = I['s5_d'].rearrange("(g h) -> h g", h=16)
        for s in range(8):
            self.DMA('sp', self.PV[s * 16:(s + 1) * 16, self.pv['dvec']:self.pv['dvec'] + 32], dsrc, ['in_w'], ['pv_dvec'], nonc=True)
        wout_v = W['s5wout'].rearrange("p (c g m) -> p c g m", c=2, g=32)
        win_v = W['s5win'].rearrange("p (g m) -> p g m", g=32)
        t0_v = W['s5t0'].rearrange("p (g m) -> p g m", g=32)
        S5M = self.HC[:, HC_S5M:HC_S5M + 128]
        S5E = self.HC[:, HC_S5E:HC_S5E + 128]
        sb64 = stgb.ap[0:64, :]
        for gc in range(32 // GC):
            g0 = gc * GC
            gs_ = slice(g0, g0 + GC)
            c3 = lambda r: (r.ap.rearrange("p (g h) -> p g h", h=16)[:, gs_, :], r.keys)
            tAc = (tA.ap[:, 0:GC * 16].rearrange("p (g h) -> p g h", h=16), tA.keys)
            tBc = (tB.ap[:, 0:GC * 16].rearrange("p (g h) -> p g h", h=16), tB.keys)
            WTr4 = WTr.ap.rearrange("p (g s h) -> p g s h", s=8, h=16)
            WTi4 = WTi.ap.rearrange("p (g s h) -> p g s h", s=8, h=16)
            for s in range(8):
                lr = (LRv[:, 7 - s, gs_].unsqueeze(2).broadcast_to([64, GC, 16]), LR.keys)
                li = (LIv[:, 7 - s, gs_].unsqueeze(2).broadcast_to([64, GC, 16]), LI.keys)
                tt(tAc, lr, c3(BBr), ALU.mult)
                tt(tBc, li, c3(BBi), ALU.mult)
                tt((WTr4[:, :, s, :], WTr.keys), tAc, tBc, ALU.subtract)
                tt(tAc, lr, c3(BBi), ALU.mult)
                tt(tBc, li, c3(BBr), ALU.mult)
                tt((WTi4[:, :, s, :], WTi.keys), tAc, tBc, ALU.add)
            C4 = lambda r: (r.ap.rearrange("p (g h) -> p g h", h=16)[:, gs_, :].unsqueeze(3).broadcast_to([64, GC, 16, 8]), r.keys)
            L4 = lambda r, rv: (rv[:, 1:9, gs_].rearrange("p n g -> p g n").unsqueeze(2).broadcast_to([64, GC, 16, 8]), r.keys)
            q4 = lambda r: (r.ap.rearrange("p (g h l) -> p g h l", h=16, l=8), r.keys)
            tt(q4(tA), C4(Cre), L4(LR, LRv), ALU.mult)
            tt(q4(tB), C4(Cim), L4(LI, LIv), ALU.mult)
            tt(q4(Qr), q4(tA), q4(tB), ALU.subtract)
            tt(q4(tA), C4(Cre), L4(LI, LIv), ALU.mult)
            tt(q4(tB), C4(Cim), L4(LR, LRv), ALU.mult)
            tt(q4(Qi), q4(tA), q4(tB), ALU.add)
            zrb = (zr.ap[:, gs_].unsqueeze(2).broadcast_to([64, GC, 128]), zr.keys)
            zib = (zi.ap[:, gs_].unsqueeze(2).broadcast_to([64, GC, 128]), zi.keys)
            g3 = lambda r: (r.ap.rearrange("p (g m) -> p g m", m=128), r.keys)
            tt(g3(tA), g3(Qr), zrb, ALU.mult)
            tt(g3(tB), g3(Qi), zib, ALU.mult)
            tt(g3(Qsr), g3(tA), g3(tB), ALU.add)
            tt(g3(tA), g3(Qr), zib, ALU.mult)
            tt(g3(tB), g3(Qi), zrb, ALU.mult)
            tt(g3(Qsi), g3(tA), g3(tB), ALU.subtract)
            for gg in range(GC):
                g = g0 + gg
                sl = slice(gg * 128, (gg + 1) * 128)
                self.CP('act', sb64[:, 0:128], Qr.ap[:, sl], Qr.keys, stgb.keys)
                self.ACT(sb64[:, 128:256], Qi.ap[:, sl], AF.Copy, Qi.keys, stgb.keys, scale=-1.0)
                self.DMA('sp', wout_v[:, 0, g, :], sb64[:, 0:128], stgb.keys, ['w_s5wout'])
                self.DMA('sp', wout_v[:, 1, g, :], sb64[:, 128:256], stgb.keys, ['w_s5wout'])
                pb, pk = self.bank()
                self.TR(pb[:, 0:64], WTr.ap[:, sl], self.IDf[0:64, 0:64], WTr.keys + ['hc'], [pk])
                self.TR(pb[:, 64:128], WTi.ap[:, sl], self.IDf[0:64, 0:64], WTi.keys + ['hc'], [pk])
                self.CP('act', stgb.ap[:, 0:128], pb[:, 0:128], [pk], stgb.keys)
                self.DMA('sp', win_v[:, g, :], stgb.ap[:, 0:128], stgb.keys, ['w_s5win'])
                pb2, pk2 = self.bank()
                self.MM(pb2[:, 0:128], WTr.ap[:, sl], Qsr.ap[:, sl], True, False, WTr.keys + Qsr.keys, [pk2])
                self.MM(pb2[:, 0:128], WTi.ap[:, sl], Qsi.ap[:, sl], False, True, WTi.keys + Qsi.keys, [pk2])
                self.TT('dve', tmpf.ap[:, 0:128], pb2[:, 0:128], S5M, ALU.mult, [pk2, 'hc'], tmpf.keys)
                self.STT(stgb.ap[:, 128:256], S5E, self.pvc('dvec', g, 1), tmpf.ap[:, 0:128], ALU.mult, ALU.add,
                         ['hc', 'pv_dvec'] + tmpf.keys, stgb.keys)
                self.DMA('sp', t0_v[:, g, :], stgb.ap[:, 128:256], stgb.keys, ['w_s5t0'])

    def l1_prologue(self):
        I, W = self.I, self.W
        BC = self.BC
        self.DMA('sp', BC[:, 0:8], I['ssd_dt_bias'].partition_broadcast(128), ['in_w'], ['bc'])
        self.DMA('sp', BC[:, 8:16], I['ssd_a_log'].partition_broadcast(128), ['in_w'], ['bc'])
        self.DMA('sp', BC[:, 16:24], I['ssd_d'].partition_broadcast(128), ['in_w'], ['bc'])
        self.DMA('sp', BC[:, 32:544], I['ssd_norm_g'].partition_broadcast(128), ['in_w'], ['bc'])
        self.ACT(BC[:, 8:16], BC[:, 8:16], AF.Exp, ['bc'], ['bc'])
        self.TS('dve', BC[:, 8:16], BC[:, 8:16], -1.0, None, ALU.mult, None, ['bc'], ['bc'])
        self.fm_load(self.pvc('scw', 0, 32), ['pv_scw'], I['ssd_conv_w'], 32, 128)
        self.fm_load(self.pvc('scb', 0, 8), ['pv_scb'], I['ssd_conv_b'], 8, 128)
        for half in range(2):
            self.DMA('sp', self.PV[half * 64:(half + 1) * 64, self.pv['qng']:self.pv['qng'] + 1],
                     I['swa_q_norm'].rearrange("(p o) -> p o", o=1), ['in_w'], ['pv_qng'], nonc=True)
            self.DMA('sp', self.PV[half * 64:(half + 1) * 64, self.pv['kng']:self.pv['kng'] + 1],
                     I['swa_k_norm'].rearrange("(p o) -> p o", o=1), ['in_w'], ['pv_kng'], nonc=True)
            self.DMA('sp', self.PV[half * 64:(half + 1) * 64, self.pv['sinkexp']:self.pv['sinkexp'] + 4],
                     I['swa_sink'][half * 4:(half + 1) * 4].partition_broadcast(64), ['in_w'], ['pv_sink'])
        self.TS('dve', self.pvc('qng'), self.pvc('qng'), 0.125, None, ALU.mult, None, ['pv_qng'], ['pv_qng'])
        self.ACT(self.pvc('sinkexp', 0, 4), self.pvc('sinkexp', 0, 4), AF.Exp, ['pv_sink'], ['pv_sink'])
        self.DMA('pool', self.WDT[:], I['od_w_in'][:, 2304:2312].rearrange("(k p) n -> p k n", p=128), ['in_w'], ['wdt'])
        self.DMA('sp', self.T5T[:], I['t5_bias'], ['in_w'], ['t5t'])
        pb, pk = self.bank()
        self.MM(pb[0:8, 0:384], self.T5T[:], self.HC[0:32, HC_OHW:HC_OHW + 384], True, True, ['t5t', 'hc'], [pk])
        self.CP('dve', self.WROW[:], pb[0:8, 0:384], [pk], ['wrow'])
        ps_ = self.WROW[:].ap[0][0]
        for h in range(8):
            for kb in range(3):
                srcap = bass.AP(self.WROW[:].tensor, h * ps_ + kb * 128, [[ps_, 1], [0, 64], [1, 128]])
                self.DMA('sp', W['biasd'][h:h + 1, kb, :].rearrange("o (r m) -> o r m", m=128), srcap, ['wrow'], ['w_biasd'])
        self.MEMSET('pool', self.BIE[:], 0.0, ['bie'])
        self.MEMSET('pool', self.BIO[:], 0.0, ['bio'])
        bt = W['biasd'].tensor

        def skew(h, kb):
            return bass.AP(bt, (h * 3 + kb) * 64 * 128, [[127, 64], [1, 64]])
        for kvh in range(2):
            for g in range(4):
                h = kvh * 4 + g
                for (T_, nm, p0, c0, kb) in [(self.BIE, 'bie', 0, 0, 0), (self.BIE, 'bie', 64, 0, 1), (self.BIE, 'bie', 0, 256, 2),
                                             (self.BIO, 'bio', 0, 0, 1), (self.BIO, 'bio', 64, 0, 2), (self.BIO, 'bio', 64, 256, 0)]:
                    self.DMA('sp', T_[p0:p0 + 64, kvh, c0 + g * 64:c0 + (g + 1) * 64], skew(h, kb), ['w_biasd'], [nm])

    def ws_load(self, views_and_srcs, rkeys):
        i = self.ws_rr
        self.ws_rr = (self.ws_rr + 1) % 4
        t = self.WS[i]
        for mk, src in views_and_srcs:
            self.DMA('sp', mk(t), src, rkeys, [f'ws{i}'])
        return t, f'ws{i}'

    def run_job(self, job):
        SEQ = self.SEQ
        if job < 2:
            T = 512
            ntile = SEQ // T
        else:
            T = 64
            ntile = 1
        self.job = job
        self.sub = int(os.environ.get('KSUB', '99'))
        self.fine = int(os.environ.get('KFINE', '99'))
        self.job_init(job)
        for ti in range(ntile):
            last = (ti == ntile - 1)
            self.tile_load(job, ti, T)
            if self.stage >= 7:
                self.layer0(job, ti, T, last)
            if self.stage >= 8:
                self.ffn(job, 0, T, last)
            if self.do_l1 and self.stage >= 9:
                self.layer1(job, ti, T, last)
            if self.do_l1 and self.stage >= 10:
                self.ffn(job, 1, T, last)
            self.tile_store(job, ti, T)
        self.job_finish(job)

    def job_init(self, job):
        I = self.I
        if job < 2:
            self.MEMSET('pool', self.S3[:], 0.0, ['s3'])
            self.MEMSET('pool', self.GS[:], 0.0, ['gs'])
            self.MEMSET('pool', self.FT[:], 0.0, ['ft'])
        else:
            self.fm_load(self.S3[:, 0, :], ['s3'], I['st_s5re'], 32, 64)
            self.fm_load(self.S3[:, 1, :], ['s3'], I['st_s5im'], 32, 64)
            self.CP('dve', self.S3[:, 2, :], self.S3[:, 0, :], ['s3'], ['s3'])
            self.DMA('sp', self.GS[:], I['st_gla'].rearrange("(hp q) v -> q hp v", q=128), ['in_w'], ['gs'])
            for l in range(2):
                tmp = self.scr(0, 88 * 4, F32)
                self.fm_load(tmp.ap[:, 0:88], tmp.keys, I['st_ffnconv'][l], 88, 128)
                self.CP('dve', self.FT[:, l, :, :], tmp.ap[:, 0:88].rearrange("p (r c) -> p c r", r=2), tmp.keys, ['ft'])
        if self.do_l1:
            self.l1_job_init(job)

    def l1_job_init(self, job):
        I = self.I
        self.MEMSET('pool', self.KWIN[:], 0.0, ['kwin'])
        self.MEMSET('pool', self.VWP[:], 0.0, ['vwp'])
        if job < 2:
            self.MEMSET('pool', self.ST[:], 0.0, ['st'])
            self.MEMSET('pool', self.CT[:], 0.0, ['ct'])
        else:
            for q4 in range(4):
                self.fm_load(self.ST[:, q4 * 128:(q4 + 1) * 128], ['st'], I['st_ssd'][q4 * 128:(q4 + 1) * 128, :], 128, 128)
            tmp = self.scr(0, 24 * 4, F32)
            self.fm_load(tmp.ap[:, 0:24], tmp.keys, I['st_ssdconv'], 24, 128)
            self.CP('dve', self.CT[:, :, 1:4], tmp.ap[:, 0:24].rearrange("p (r c) -> p c r", r=3), tmp.keys, ['ct'])
            tk = self.scr(2048, 128 * 4, F32)
            self.fm_load(tk.ap[:, 0:128], tk.keys, I['ck'], 128, 128)
            self.CP('dve', self.KWIN[:, 0:128], tk.ap[:, 0:128], tk.keys, ['kwin'])
            self.DMA('pool', self.VWP[:, 0, :], I['cv'], ['in_w', 'vwp'], ['vwp'])

    def job_finish(self, job):
        O = self.O
        self.tm_store(O['o_s5re'][job], self.S3[:, 0, :], ['s3'], 32, 64)
        self.tm_store(O['o_s5im'][job], self.S3[:, 1, :], ['s3'], 32, 64)
        self.DMA('pool', O['o_gla'][job].rearrange("(hp q) v -> q hp v", q=128), self.GS[:], ['gs'], ['out'])
        if self.do_l1:
            self.l1_job_finish(job)

    def l1_job_finish(self, job):
        O = self.O
        for q4 in range(4):
            self.tm_store(O['o_ssd'][job, q4 * 128:(q4 + 1) * 128, :], self.ST[:, q4 * 128:(q4 + 1) * 128], ['st'], 128, 128)

    def tile_load(self, job, ti, T):
        I = self.I
        xt = self.scr(0, 4 * 1024 * 4, F32)
        xtv = xt.ap.rearrange("p (b f) -> p b f", b=4)
        nb = max(1, T // 128)
        rows = min(T, 128)
        for b in range(nb):
            if job < 2:
                src = I['xp'][job, ti * T + b * 128: ti * T + b * 128 + rows, :]
            else:
                src = I['xs'][0:64, :]
            self.DMA('sp', xtv[0:rows, b, :], src, ['in_x'], xt.keys)
        for k in range(8):
            pb, pk = self.bank()
            for b in range(nb):
                self.TR(pb[:, b * 128:b * 128 + rows], xtv[0:rows, b, k * 128:(k + 1) * 128], self.IDf[0:rows, 0:rows],
                        xt.keys + ['hc'], [pk])
            self.CP('dve' if k % 2 == 0 else 'act', self.XRES[:, k, 0:T], pb[:, 0:T], [pk], [('xres', k)])

    def tile_store(self, job, ti, T):
        O = self.O
        xt = self.scr(0, 4 * 1024 * 4, F32)
        xtv = xt.ap.rearrange("p (b f) -> p b f", b=4)
        nb = max(1, T // 128)
        rows = min(T, 128)
        for b in range(nb):
            for k2 in range(2):
                pb, pk = self.bank()
                for kk in range(4):
                    k = k2 * 4 + kk
                    self.TR(pb[0:rows, kk * 128:(kk + 1) * 128], self.XRES[:, k, b * 128:b * 128 + rows], self.IDf,
                            [('xres', k), 'hc'], [pk])
                self.CP('dve' if k2 == 0 else 'act', xtv[0:rows, b, k2 * 512:(k2 + 1) * 512], pb[0:rows, :], [pk], xt.keys)
            if job < 2:
                dst = O['yp'][job, ti * T + b * 128: ti * T + b * 128 + rows, :]
            else:
                dst = O['ys'][0:64, :]
            self.DMA('pool', dst, xtv[0:rows, b, :], xt.keys, ['out'])

    def norm(self, job, l, which, T, want_lo=False):
        xres = self.XRES
        sq = self.scr(0, 8 * 512 * 2, BF16)
        sqv = sq.ap.rearrange("p (k t) -> p k t", k=8)
        rs = self.scr(8192, 2048, F32)
        tmpn = [self.scr(10240 + i * 2048, 2048, F32) for i in range(2)]
        for k in range(8):
            self.ACT(sqv[:, k, 0:T], xres[:, k, 0:T], AF.Square, [('xres', k)], sq.keys)
        pb, pk = self.bank()
        for k in range(8):
            self.MM(pb[:, 0:T], self.ONESb[:], sqv[:, k, 0:T], k == 0, k == 7, sq.keys + ['onesb'], [pk])
        self.ACT(rs.ap[:, 0:T], pb[:, 0:T], AF.Sqrt, [pk, 'pv_eps'], rs.keys, bias=self.pvc('eps'), scale=1.0 / D)
        self.fw.op('dve', lambda e: e.reciprocal(out=rs.ap[:, 0:T], in_=rs.ap[:, 0:T]), rs.keys, rs.keys)
        gsn = 'gs1' if which == 0 else 'gs2'
        for k in range(8):
            tm = tmpn[k % 2]
            gs = self.pvc(gsn, (job * 2 + l) * 8 + k, 1)
            sh = self.modv(job, l, 0 if which == 0 else 3, k, 1)
            self.STT(tm.ap[:, 0:T], xres[:, k, 0:T], gs, rs.ap[:, 0:T], ALU.mult, ALU.mult,
                     [('xres', k), 'pv_gs'] + rs.keys, tm.keys)
            self.ACT(self.ACT8[:, k, 0:T], tm.ap[:, 0:T], AF.Identity, tm.keys + ['pv_mod'], [('act8', k)], bias=sh, scale=1.0)
            if want_lo:
                h32 = self.scr(14336 + (k % 2) * 2048, 2048, F32)
                self.TS('pool', h32.ap[:, 0:T], tm.ap[:, 0:T], sh, None, ALU.add, None, tm.keys + ['pv_mod'], h32.keys)
                self.TT('pool', self.HNLO[:, k, 0:T], h32.ap[:, 0:T], self.ACT8[:, k, 0:T], ALU.subtract,
                        h32.keys + [('act8', k)], [('hnlo', k)])

    def resid(self, job, l, which, m, pb, pk, T):
        g = self.modv(job, l, 2 if which == 0 else 5, m, 1)
        self.STT(self.XRES[:, m, 0:T], pb[:, 0:T], g, self.XRES[:, m, 0:T], ALU.mult, ALU.add,
                 [pk, 'pv_mod', ('xres', m)], [('xres', m)])

    def layer0(self, job, ti, T, last):
        W = self.W
        NC = T // 64
        U = min(T, 128)
        NU = T // U
        J = T // 8
        hn = self.ACT8
        hnk = [('act8', k) for k in range(8)]
        self.norm(job, 0, 0, T, want_lo=True)
        o = [0]

        def A(nbytes, dt, parts=128):
            r = self.scr(o[0], nbytes, dt, parts=parts)
            o[0] += (nbytes + 63) // 64 * 64
            return r
        RSIL = A(4 * 512 * 2, BF16)
        VTOK = A(4 * 512 * 2, BF16)
        ZT2 = A(32 * 128 * 2, BF16)
        U2 = A(32 * 64 * 2, BF16)
        gla_start = o[0]
        GL = A(512 * 2, BF16)
        LL = A(2 * 512 * 4, F32)
        CUM = A(2 * 512 * 4, F32)
        E1 = A(2 * 512 * 4, F32)
        E2 = A(2 * 512 * 4, F32)
        KEF = A(2 * 512 * 4, F32)
        QE = A(2 * 512 * 2, BF16)
        QE32 = A(2 * 512 * 4, F32)
        KE = A(2 * 512 * 2, BF16)
        KD = A(2 * 512 * 2, BF16)
        KDT = A(4 * 2 * 128 * 2, BF16)
        ATT = [A(4 * 128 * 2, BF16) for _ in range(2)]
        OSB = [A(4 * 128 * 4, F32) for _ in range(2)]
        OSQ = [A(4 * 128 * 2, BF16) for _ in range(2)]
        ORS = [A(4 * 128 * 4, F32) for _ in range(2)]
        OT = [A(4 * 128 * 4, F32) for _ in range(2)]
        assert o[0] <= self.SCRN * 4, o[0]
        gla_end = o[0]
        rsv = RSIL.ap.rearrange("p (c t) -> p c t", c=4)
        vtv = VTOK.ap.rearrange("p (u f) -> p u f", u=4)
        zt4 = ZT2.ap.rearrange("p (g s h) -> p g s h", g=32, s=8)
        v2 = lambda r: r.ap.rearrange("p (c t) -> p c t", c=2)
        if self.sub < 1:
            return
        evin = W['evin']
        wt, wk = self.ws_load([(lambda t: t[:].rearrange("p (k n) -> p k n", k=8),
                                evin[:, 512:1024].rearrange("(k p) n -> p k n", p=128))], ['w_evin'])
        wv = wt[:].rearrange("p (k n) -> p k n", k=8)
        wtl, wkl = self.ws_load([(lambda t: t[:].rearrange("p (k n) -> p k n", k=8),
                                  W['evin_lo'].rearrange("(k p) n -> p k n", p=128))], ['w_evin_lo'])
        wvl = wtl[:].rearrange("p (k n) -> p k n", k=8)
        qkb = []
        for mc in range(4):
            pb, pk = self.bank()
            for k in range(8):
                self.MM(pb[:, 0:T], wv[:, k, mc * 128:(mc + 1) * 128], hn[:, k, 0:T], k == 0, False, [wk, hnk[k]], [pk])
            for k in range(8):
                self.MM(pb[:, 0:T], wvl[:, k, mc * 128:(mc + 1) * 128], hn[:, k, 0:T], False, False, [wkl, hnk[k]], [pk])
            for k in range(8):
                self.MM(pb[:, 0:T], wv[:, k, mc * 128:(mc + 1) * 128], self.HNLO[:, k, 0:T], False, k == 7, [wk, ('hnlo', k)], [pk])
            qkb.append((pb, pk))
        if self.fine < 1:
            return
        pbg, pkg = self.bank()
        for k in range(8):
            self.MM(pbg[0:16, 0:T], self.WGL[:, k, :], hn[:, k, 0:T], k == 0, k == 7, ['wgl', hnk[k]], [pkg])
        self.CP('act', GL.ap[0:16, 0:T], pbg[0:16, 0:T], [pkg], GL.keys)
        if self.fine < 2:
            return
        pbG, pkG = self.bank(), None
        pbG, pkG = pbG
        gate_banks = []
        for c2 in range(2):
            if T == 512:
                pbx, pkx = (pbG, pkG) if c2 == 0 else self.bank()
            else:
                pbx, pkx = (pbG, pkG)
            col0 = 0 if T == 512 else c2 * 64
            self.MM(pbx[:, col0:col0 + T], self.WG2[:, c2 * 128:(c2 + 1) * 128], GL.ap[0:16, 0:T], True, True,
                    ['wg2'] + GL.keys, [pkx])
            gate_banks.append((pbx, pkx, col0))
        if self.fine < 3:
            return
        llv, cumv, e1v, e2v, kefv = v2(LL), v2(CUM), v2(E1), v2(E2), v2(KEF)
        qev, kev, kdv = v2(QE), v2(KE), v2(KD)
        qe32v = v2(QE32)
        for c2 in range(2):
            pbx, pkx, col0 = gate_banks[c2]
            self.ACT(llv[:, c2, 0:T], pbx[:, col0:col0 + T], AF.Exp, [pkx, 'pv_negbg'], LL.keys,
                     bias=self.pvc('negbg', c2, 1), scale=-1.0)
        for c2 in range(2):
            self.ACT(llv[:, c2, 0:T], llv[:, c2, 0:T], AF.Ln, LL.keys + ['pv_one'], LL.keys, bias=self.pvc('one'), scale=1.0)
        if self.fine < 4:
            return
        for c2 in range(2):
            self.fw.op('dve', (lambda c2: lambda e: e.tensor_tensor_scan(
                out=cumv[:, c2, 0:T], data0=self.HC[:, HC_RST:HC_RST + T], data1=llv[:, c2, 0:T], initial=0.0,
                op0=ALU.mult, op1=ALU.add))(c2), LL.keys + ['hc'], CUM.keys)
        if self.fine < 5:
            return
        self.ACT(e1v[:, :, 0:T], cumv[:, :, 0:T], AF.Exp, CUM.keys, E1.keys, scale=-1.0 / 16.0)
        self.ACT(e2v[:, :, 0:T], cumv[:, :, 0:T], AF.Exp, CUM.keys, E2.keys, scale=1.0 / 16.0)
        if self.fine < 6:
            return
        for c2 in range(2):
            pbq, pkq = qkb[c2]
            self.STT(qe32v[:, c2, 0:T], pbq[:, 0:T], 0.125, e1v[:, c2, 0:T], ALU.mult, ALU.mult, [pkq] + E1.keys, QE32.keys)
            self.CP('pool', qev[:, c2, 0:T], qe32v[:, c2, 0:T], QE32.keys, QE.keys)
            pbk, pkk = qkb[2 + c2]
            self.TT('dve', kefv[:, c2, 0:T], pbk[:, 0:T], e2v[:, c2, 0:T], ALU.mult, [pkk] + E2.keys, KEF.keys)
        self.CP('act', kev[:, :, 0:T], kefv[:, :, 0:T], KEF.keys, KE.keys)
        if self.fine < 7:
            return
        e1last = E1.ap.rearrange("p (c n l) -> p c n l", c=2, l=64)[:, :, 0:NC, 63:64].broadcast_to([128, 2, NC, 64])
        self.TT('pool', KD.ap.rearrange("p (c n l) -> p c n l", c=2, l=64)[:, :, 0:NC, :],
                KEF.ap.rearrange("p (c n l) -> p c n l", c=2, l=64)[:, :, 0:NC, :], e1last, ALU.mult,
                KEF.keys + E1.keys, KD.keys)
        if self.fine < 8:
            return
        wt, wk = self.ws_load([(lambda t: t[:].rearrange("p (k n) -> p k n", k=8),
                                evin[:, 1552:2064].rearrange("(k p) n -> p k n", p=128))], ['w_evin'])
        wv = wt[:].rearrange("p (k n) -> p k n", k=8)
        for mc in range(4):
            pb, pk = self.bank()
            for k in range(8):
                self.MM(pb[:, 0:T], wv[:, k, mc * 128:(mc + 1) * 128], hn[:, k, 0:T], k == 0, k == 7, [wk, hnk[k]], [pk])
            self.ACT(rsv[:, mc, 0:T], pb[:, 0:T], AF.Silu, [pk], RSIL.keys)
        if self.fine < 9:
            return
        if os.environ.get('KSLOT'):
            self.ws_rr = int(os.environ['KSLOT'])
        kvar = int(os.environ.get('KVAR', '0'))
        if kvar != 5:
            wt, wk = self.ws_load([(lambda t: t[:].rearrange("p (k n) -> p k n", k=8),
                                    evin[:, 1024:1536].rearrange("(k p) n -> p k n", p=128))], ['w_evin'])
            wv = wt[:].rearrange("p (k n) -> p k n", k=8)
        for u in range(NU):
            if kvar == 6:
                continue
            pb, pk = self.bank()
            for k in range(8):
                if kvar not in (2, 3, 5):
                    self.MM(pb[0:U, :], hn[:, k, u * U:(u + 1) * U], wv[:, k, :], k == 0, k == 7, [wk, hnk[k]], [pk])
            if kvar not in (1, 3, 5):
                self.CP('act' if u % 2 else 'dve', vtv[0:U, u, :], pb[0:U, :], [pk], VTOK.keys)
        if self.fine < 10:
            return
        wt, wk = self.ws_load([(lambda t: t[:].rearrange("p (k n) -> p k n", k=8),
                                evin[:, 0:512].rearrange("(k p) n -> p k n", p=128))], ['w_evin'])
        wv = wt[:].rearrange("p (k n) -> p k n", k=8)
        for s in range(8):
            pb, pk = self.bank()
            for k in range(8):
                self.MM(pb[0:J, :], hn[:, k, s:T:8], wv[:, k, :], k == 0, k == 7, [wk, hnk[k]], [pk])
            self.CP('act' if s % 2 else 'dve', zt4[0:J, :, s, :], pb[0:J, :].rearrange("p (g h) -> p g h", h=16), [pk], ZT2.keys)
        if self.sub < 2:
            return
        kdt4 = KDT.ap.rearrange("p (u c f) -> p u c f", u=4, c=2)
        if U < 128:
            self.MEMSET('pool', vtv[64:128, 0, :], 0.0, VTOK.keys)
            for s2_ in range(2):
                self.MEMSET('pool', ATT[s2_].ap[64:128, :], 0.0, ATT[s2_].keys)
        for u in range(NU):
            pb, pk = self.bank()
            pbb = pb[:, 0:128].bitcast(BF16)
            for c2 in range(2):
                self.TR(pbb[0:U, c2 * 128:(c2 + 1) * 128], kdv[:, c2, u * U:(u + 1) * U], self.IDb[:], KD.keys + ['idb'], [pk])
            self.CP('act' if u % 2 else 'dve', kdt4[0:U, u, :, :], pbb[0:U, 0:256].rearrange("p (c f) -> p c f", c=2), [pk], KDT.keys)
        if self.fine < 21:
            return
        self.CP('act', self.GSH[:, 0, :, :], self.GS[:], ['gs'], [('gsh', 0)])
        e1l = E1.ap.rearrange("p (c n l) -> p c n l", c=2, l=64)
        for c in range(NC):
            u, cu = divmod(c, U // 64)
            pb, pk = self.bank()
            p0 = cu * 64
            for h in range(4):
                hp, hb = divmod(h, 2)
                self.MM(pb[hb * 64:(hb + 1) * 64, hp * 128:(hp + 1) * 128],
                        kdt4[p0:p0 + 64, u, hp, hb * 64:(hb + 1) * 64], vtv[p0:p0 + 64, u, h * 128:(h + 1) * 128],
                        True, True, KDT.keys + VTOK.keys, [pk])
            for hp in range(2):
                self.STT(self.GS[:, hp, :], self.GS[:, hp, :], e1l[:, hp, c, 63:64], pb[:, hp * 128:(hp + 1) * 128],
                         ALU.mult, ALU.add, ['gs', pk] + E1.keys, ['gs'])
            if c + 1 < NC:
                self.CP('act', self.GSH[:, c + 1, :, :], self.GS[:], ['gs'], [('gsh', c + 1)])
        if self.fine < 22:
            return
        MCm = self.HC[0:U, HC_MC:HC_MC + U]
        mixed = self.ACT8
        for u in range(NU):
            s2 = u % 2
            cols = slice(u * U, (u + 1) * U)
            attv = ATT[s2].ap.rearrange("p (h l) -> p h l", h=4)
            sbk = [self.bank(), self.bank()]
            for h in range(4):
                hp, hb = divmod(h, 2)
                pbs, pks = sbk[hb]
                self.MM(pbs[0:U, hp * U:(hp + 1) * U], kefv[hb * 64:(hb + 1) * 64, hp, cols], qe32v[hb * 64:(hb + 1) * 64, hp, cols],
                        True, True, KEF.keys + QE32.keys, [pks])
            for hb in range(2):
                pbs, pks = sbk[hb]
                self.TT('dve', attv[0:U, hb:4:2, 0:U], pbs[0:U, 0:2 * U].rearrange("p (h l) -> p h l", h=2),
                        MCm.unsqueeze(1).broadcast_to([U, 2, U]), ALU.mult, [pks, 'hc'], ATT[s2].keys)
            obk = [self.bank(), self.bank()]
            for h in range(4):
                hp, hb = divmod(h, 2)
                pbo, pko = obk[hb]
                self.MM(pbo[:, hp * U:(hp + 1) * U], vtv[0:128, u, h * 128:(h + 1) * 128], attv[0:128, h, 0:U], True, False,
                        VTOK.keys + ATT[s2].keys, [pko])
                for cu in range(U // 64):
                    c = u * (U // 64) + cu
                    self.MM(pbo[:, hp * U + cu * 64:hp * U + (cu + 1) * 64], self.GSH[hb * 64:(hb + 1) * 64, c, hp, :],
                            qev[hb * 64:(hb + 1) * 64, hp, c * 64:(c + 1) * 64], False, cu == U // 64 - 1,
                            [('gsh', c)] + QE.keys, [pko])
            if self.fine < 23:
                continue
            osb, osq, ors, ot = OSB[s2], OSQ[s2], ORS[s2], OT[s2]
            n4 = 4 * U
            osb3 = osb.ap[:, 0:n4].rearrange("p (h l) -> p h l", h=4)
            osq3 = osq.ap[:, 0:n4].rearrange("p (h l) -> p h l", h=4)
            for hb in range(2):
                pbo, pko = obk[hb]
                src3 = pbo[:, 0:2 * U].rearrange("p (h l) -> p h l", h=2)
                self.CP('act', osb3[:, hb:4:2, :], src3, [pko], osb.keys)
                self.ACT(osq3[:, hb:4:2, :], src3, AF.Square, [pko], osq.keys)
            pbn, pkn = self.bank()
            self.MM(pbn[:, 0:n4], self.ONESb[:], osq.ap[:, 0:n4], True, True, osq.keys + ['onesb'], [pkn])
            self.ACT(ors.ap[:, 0:n4], pbn[:, 0:n4], AF.Sqrt, [pkn, 'pv_eps'], ors.keys, bias=self.pvc('eps'), scale=1.0 / 128)
            self.fw.op('dve', (lambda ors=ors, n4=n4: lambda e: e.reciprocal(out=ors.ap[:, 0:n4], in_=ors.ap[:, 0:n4]))(),
                       ors.keys, ors.keys)
            self.STT(ot.ap[:, 0:n4], osb.ap[:, 0:n4], self.pvc('glang'), ors.ap[:, 0:n4], ALU.mult, ALU.mult,
                     osb.keys + ors.keys + ['pv_glang'], ot.keys)
            self.TT('pool', mixed[:, 4:8, cols], ot.ap[:, 0:n4].rearrange("p (h l) -> p h l", h=4), rsv[:, :, cols], ALU.mult,
                    ot.keys + RSIL.keys, [('act8', 4 + h) for h in range(4)])
        if self.sub < 3:
            return
        o[0] = gla_start
        VSB = A(2 * 32 * 64 * 4, F32, parts=64)
        XBF = A(2 * 32 * 64 * 2, BF16, parts=64)
        Y2 = A(32 * 64 * 2, BF16)
        YTOK = A(512 * 8 * 2, BF16, parts=64)
        YFM = A(4 * 512 * 2, BF16)
        SGT = [A(512 * 4, F32) for _ in range(2)]
        TM1 = A(64 * 4, F32, parts=64)
        TM2 = A(64 * 4, F32, parts=64)
        assert o[0] <= self.SCRN * 4, o[0]
        u2v = U2.ap.rearrange("p (g j) -> p g j", g=32)
        for half in range(2):
            pb, pk = self.bank()
            pbb = pb[:].bitcast(BF16)
            for gg in range(16):
                g = half * 16 + gg
                self.TR(pbb[:, gg * J:(gg + 1) * J], ZT2.ap[0:J, g * 128:(g + 1) * 128], self.IDb[0:J, 0:J],
                        ZT2.keys + ['idb'], [pk])
            self.CP('act' if half else 'dve', u2v[:, half * 16:(half + 1) * 16, 0:J],
                    pbb[:, 0:16 * J].rearrange("p (g j) -> p g j", g=16), [pk], U2.keys)
        wt_in, wk_in = self.ws_load([(lambda t: t[:], W['s5win'])], ['w_s5win'])
        winv = wt_in[:].rearrange("p (g m) -> p g m", g=32)
        vsb4 = VSB.ap.rearrange("p (c g j) -> p c g j", c=2, g=32)
        xbf4 = XBF.ap.rearrange("p (c g j) -> p c g j", c=2, g=32)
        gpbv = min(32, 512 // J)
        for c in range(2):
            for bi_, g0 in enumerate(range(0, 32, gpbv)):
                pb, pk = self.bank()
                for gg in range(gpbv):
                    g = g0 + gg
                    self.MM(pb[0:64, gg * J:(gg + 1) * J], winv[:, g, c * 64:(c + 1) * 64], u2v[:, g, 0:J], True, True,
                            [wk_in] + U2.keys, [pk])
                self.CP('act' if bi_ % 2 else 'dve', vsb4[:, c, g0:g0 + gpbv, 0:J],
                        pb[0:64, 0:gpbv * J].rearrange("p (g j) -> p g j", g=gpbv), [pk], VSB.keys)
        S3 = self.S3
        tm1 = TM1.ap.rearrange("p (c g) -> p c g", c=2)
        tm2 = TM2.ap.rearrange("p (c g) -> p c g", c=2)
        eng = os.environ.get('KSCAN', 'dve')
        for j in range(J):
            self.CP(eng, xbf4[:, :, :, j], S3[:, 0:2, :], ['s3'], XBF.keys)
            self.TT(eng, tm1, self.A1[:], S3[:, 0:2, :], ALU.mult, ['a12', 's3'], TM1.keys)
            self.TT(eng, tm2, self.A2[:], S3[:, 1:3, :], ALU.mult, ['a12', 's3'], TM2.keys)
            self.TT(eng, tm1, tm1, tm2, ALU.add, TM1.keys + TM2.keys, TM1.keys)
            self.TT(eng, S3[:, 0:2, :], tm1, vsb4[:, :, :, j], ALU.add, TM1.keys + VSB.keys, ['s3'])
            self.CP(eng, S3[:, 2, :], S3[:, 0, :], ['s3'], ['s3'])
        wt_t0, wk_t0 = self.ws_load([(lambda t: t[:], W['s5t0'])], ['w_s5t0'])
        t0v = wt_t0[:].rearrange("p (g m) -> p g m", g=32)
        wo = []
        for c in range(2):
            wt_o, wk_o = self.ws_load([(lambda t: t[0:64, :], W['s5wout'][:, c * 4096:(c + 1) * 4096])], ['w_s5wout'])
            wo.append((wt_o[0:64, :].rearrange("p (g m) -> p g m", g=32), wk_o))
        y2v = Y2.ap.rearrange("p (g j) -> p g j", g=32)
        gpb = 512 // J if J >= 16 else 32
        for g0 in range(0, 32, gpb):
            pb, pk = self.bank()
            ng = min(gpb, 32 - g0)
            for gg in range(ng):
                g = g0 + gg
                self.MM(pb[:, gg * J:(gg + 1) * J], t0v[:, g, :], u2v[:, g, 0:J], True, False, [wk_t0] + U2.keys, [pk])
                self.MM(pb[:, gg * J:(gg + 1) * J], wo[0][0][:, g, :], xbf4[:, 0, g, 0:J], False, False, [wo[0][1]] + XBF.keys, [pk])
                self.MM(pb[:, gg * J:(gg + 1) * J], wo[1][0][:, g, :], xbf4[:, 1, g, 0:J], False, True, [wo[1][1]] + XBF.keys, [pk])
            self.ACT(y2v[:, g0:g0 + ng, 0:J], pb[:, 0:ng * J].rearrange("p (g j) -> p g j", g=ng), AF.Gelu, [pk], Y2.keys)
        ytv = YTOK.ap.rearrange("p (g m) -> p g m", g=32)
        for q4 in range(4):
            pb, pk = self.bank()
            pbb = pb[:].bitcast(BF16)
            for gg in range(8):
                g = q4 * 8 + gg
                self.TR(pbb[0:J, gg * 128:(gg + 1) * 128], y2v[:, g, 0:J], self.IDb[:], Y2.keys + ['idb'], [pk])
            self.CP('act' if q4 % 2 else 'dve', ytv[0:J, q4 * 8:(q4 + 1) * 8, :],
                    pbb[0:J, 0:1024].rearrange("p (g m) -> p g m", g=8), [pk], YTOK.keys)
        yt3 = YTOK.ap.rearrange("p (c l) -> p c l", l=8)
        yfv = YFM.ap.rearrange("p (b t) -> p b t", b=4)
        for cb in range(4):
            pb, pk = self.bank()
            pbb = pb[:].bitcast(BF16)
            for l in range(8):
                self.TR(pbb[:, l * J:(l + 1) * J], yt3[0:J, cb * 128:(cb + 1) * 128, l], self.IDb[0:J, 0:J],
                        YTOK.keys + ['idb'], [pk])
            self.CP('act' if cb % 2 else 'dve', yfv[:, cb, 0:T].rearrange("p (j l) -> p l j", l=8),
                    pbb[:, 0:8 * J].rearrange("p (l j) -> p l j", l=8), [pk], YFM.keys)
        for m in range(4):
            pb, pk = self.bank()
            for k in range(4):
                self.MM(pb[:, 0:T], self.WGLU[:, k, m * 128:(m + 1) * 128], yfv[:, k, 0:T], k == 0, k == 3,
                        ['wglu'] + YFM.keys, [pk])
            sg = SGT[m % 2]
            self.ACT(sg.ap[:, 0:T], pb[:, 0:T], AF.Sigmoid, [pk, 'pv_bglu'], sg.keys, bias=self.pvc('bglu', m, 1), scale=1.0)
            self.TT('dve', mixed[:, m, 0:T], yfv[:, m, 0:T], sg.ap[:, 0:T], ALU.mult, YFM.keys + sg.keys, [('act8', m)])
        if self.sub < 4:
            return
        self.out_proj(job, 0, T, W['evout'], None)

    def out_proj(self, job, l, T, wsrc, rowperm):
        mixed = self.ACT8
        for half in range(2):
            cs_ = slice(half * 512, (half + 1) * 512)
            if rowperm:
                pieces = []
                for g in range(4):
                    pieces.append(((lambda g: lambda t: t[0:64, g * 512:(g + 1) * 512])(g), wsrc[g * 64:(g + 1) * 64, cs_]))
                    pieces.append(((lambda g: lambda t: t[64:128, g * 512:(g + 1) * 512])(g), wsrc[256 + g * 64:256 + (g + 1) * 64, cs_]))
                pieces.append((lambda t: t[:, 2048:4096].rearrange("p (k n) -> p k n", k=4),
                               wsrc[512:1024, cs_].rearrange("(k p) n -> p k n", p=128)))
                wt, wk = self.ws_load(pieces, ['w_out'])
            else:
                wt, wk = self.ws_load([(lambda t: t[:].rearrange("p (k n) -> p k n", k=8),
                                        wsrc[:, cs_].rearrange("(k p) n -> p k n", p=128))], ['w_out'])
            wv = wt[:].rearrange("p (k n) -> p k n", k=8)
            for mc in range(4):
                m = half * 4 + mc
                pb, pk = self.bank()
                for k in range(8):
                    self.MM(pb[:, 0:T], wv[:, k, mc * 128:(mc + 1) * 128], mixed[:, k, 0:T], k == 0, k == 7,
                            [wk, ('act8', k)], [pk])
                self.resid(job, l, 0, m, pb, pk, T)

    def ffn(self, job, l, T, last):
        W, O = self.W, self.O
        hn = self.ACT8
        hnk = [('act8', k) for k in range(8)]
        self.norm(job, l, 1, T)
        o = [14336]

        def A(nbytes, dt, parts=128):
            r = self.scr(o[0], nbytes, dt, parts=parts)
            o[0] += (nbytes + 63) // 64 * 64
            return r
        HB = A(22 * 512 * 2, BF16)
        ACC = [[A(512 * 4, F32) for _ in range(2)] for _ in range(2)]
        SA = [A(512 * 4, F32) for _ in range(2)]
        UT = [A(512 * 4, F32, parts=2) for _ in range(2)]
        assert o[0] <= self.SCRN * 4
        hbv = HB.ap.rearrange("p (j t) -> p j t", j=22)
        FT = self.FT
        fcw = lambda j: self.PV[:, self.pv['fcw'] + l * 132 + j * 44: self.pv['fcw'] + l * 132 + (j + 1) * 44]
        self.TT('dve', self.FCORR[:, :, 0], FT[:, l, :, 1], fcw(1), ALU.mult, ['ft', 'pv_fcw'], ['fcorr'])
        self.TT('dve', self.FTMP[:, :, 0], FT[:, l, :, 0], fcw(0), ALU.mult, ['ft', 'pv_fcw'], ['ftmp'])
        self.TT('dve', self.FCORR[:, :, 0], self.FCORR[:, :, 0], self.FTMP[:, :, 0], ALU.add, ['fcorr', 'ftmp'], ['fcorr'])
        self.TT('dve', self.FCORR[:, :, 1], FT[:, l, :, 1], fcw(0), ALU.mult, ['ft', 'pv_fcw'], ['fcorr'])
        up = W['up'][l]
        for pa in range(11):
            wt, wk = self.ws_load([
                (lambda t: t[:].rearrange("p (k n) -> p k n", k=8)[:, :, 0:256],
                 up[:, pa * 256:(pa + 1) * 256].rearrange("(k p) n -> p k n", p=128)),
                (lambda t: t[:].rearrange("p (k n) -> p k n", k=8)[:, :, 256:512],
                 up[:, DFF + pa * 256:DFF + (pa + 1) * 256].rearrange("(k p) n -> p k n", p=128))], ['w_up'])
            wv = wt[:].rearrange("p (k n) -> p k n", k=8)
            for bi in range(2):
                j = pa * 2 + bi
                slot = j % 2
                accs = []
                for ag in range(2):
                    blk = ag * 22 + j
                    pb, pk = self.bank()
                    c0 = ag * 256 + bi * 128
                    for k in range(8):
                        self.MM(pb[:, 0:T], wv[:, k, c0:c0 + 128], hn[:, k, 0:T], k == 0, k == 7, [wk, hnk[k]], [pk])
                    acc = ACC[slot][ag]
                    w2 = self.pvc('fcw', l * 132 + 2 * 44 + blk, 1)
                    w1 = self.pvc('fcw', l * 132 + 1 * 44 + blk, 1)
                    w0 = self.pvc('fcw', l * 132 + 0 * 44 + blk, 1)
                    bb = self.pvc('fcb', l * 44 + blk, 1)
                    self.ACT(acc.ap[:, 0:T], pb[:, 0:T], AF.Identity, [pk, 'pv_fcw', 'pv_fcb'], acc.keys, bias=bb, scale=w2)
                    self.STT(acc.ap[:, 1:T], pb[:, 0:T - 1], w1, acc.ap[:, 1:T], ALU.mult, ALU.add, [pk, 'pv_fcw'] + acc.keys, acc.keys)
                    self.STT(acc.ap[:, 2:T], pb[:, 0:T - 2], w0, acc.ap[:, 2:T], ALU.mult, ALU.add, [pk, 'pv_fcw'] + acc.keys, acc.keys)
                    self.TT('dve', acc.ap[:, 0:2], acc.ap[:, 0:2], self.FCORR[:, blk, :], ALU.add, acc.keys + ['fcorr'], acc.keys)
                    self.CP('act', FT[:, l, blk, :], pb[:, T - 2:T], [pk, 'fcorr'], ['ft'])
                    accs.append(acc)
                sa = SA[slot]
                self.ACT(sa.ap[:, 0:T], accs[0].ap[:, 0:T], AF.Silu, accs[0].keys, sa.keys)
                self.TT('pool', hbv[:, j, 0:T], sa.ap[:, 0:T], accs[1].ap[:, 0:T], ALU.mult, sa.keys + accs[1].keys, [('hb', j)])
            if last:
                pb, pk = self.bank()
                for k in range(8):
                    self.MM(pb[0:2, :], hn[:, k, T - 2:T], wv[:, k, :], k == 0, k == 7, [wk, hnk[k]], [pk])
                ut = UT[pa % 2]
                self.CP('dve', ut.ap[0:2, 0:512], pb[0:2, 0:512], [pk], ut.keys)
                self.DMA('pool', O['o_ffnconv'][job, l, :, pa * 256:(pa + 1) * 256], ut.ap[0:2, 0:256], ut.keys, ['out'])
                self.DMA('pool', O['o_ffnconv'][job, l, :, DFF + pa * 256:DFF + (pa + 1) * 256], ut.ap[0:2, 256:512], ut.keys, ['out'])
        dn = W['dn'][l]
        for m in range(8):
            wt, wk = self.ws_load([(lambda t: t[:, 0:22 * 128].rearrange("p (k n) -> p k n", k=22),
                                    dn[:, m * 128:(m + 1) * 128].rearrange("(k p) n -> p k n", p=128))], ['w_dn'])
            wv = wt[:, 0:22 * 128].rearrange("p (k n) -> p k n", k=22)
            pb, pk = self.bank()
            for k in range(22):
                self.MM(pb[:, 0:T], wv[:, k, :], hbv[:, k, 0:T], k == 0, k == 21, [wk, ('hb', k)], [pk])
            self.resid(job, l, 1, m, pb, pk, T)

    def layer1(self, job, ti, T, last):
        W, O = self.W, self.O
        NC = T // 64
        U = min(T, 128)
        NU = T // U
        NCU = U // 64
        hn = self.ACT8
        hnk = [('act8', k) for k in range(8)]
        self.norm(job, 1, 0, T)
        o = [0]

        def A(nbytes, dt, parts=128):
            r = self.scr(o[0], nbytes, dt, parts=parts)
            o[0] += (nbytes + 63) // 64 * 64
            return r
        QN = A(4 * 512 * 2, BF16)
        KNF = A(512 * 4, F32)
        ZGS = A(4 * 512 * 2, BF16)
        XBC = A(8 * 512 * 2, BF16)
        DT = A(128 * 4, F32)
        DTA = A(4 * 8 * 4, F32)
        VLF = A(128 * 4, F32)
        ACCX = [A(512 * 4, F32) for _ in range(2)]
        NSQ = [A(512 * 2, BF16) for _ in range(2)]
        NRS = [A(512 * 4, F32) for _ in range(2)]
        XT = A(512 * 2, BF16)
        BT = A(256 * 2, BF16)
        LH = A(8 * 128 * 4, F32)
        DEC = A(8 * 128 * 4, F32)
        MGT = A(2 * 128 * 4, F32)
        MT = A(8 * 128 * 2, BF16)
        XDT = A(512 * 2, BF16)
        XD = A(512 * 2, BF16)
        XW = A(512 * 2, BF16)
        EX = A(16 * 4, F32)
        DECB = A(2 * 8 * 4, F32)
        T1 = A(512 * 4, F32)
        YY = A(512 * 4, F32)
        YG = A(512 * 4, F32)
        YJ = A(512 * 4, F32)
        YN = A(512 * 2, BF16)
        SS = A(4 * 4, F32)
        TMPS = [A(512 * 4, F32) for _ in range(2)]
        PT = [A(512 * 2, BF16) for _ in range(4)]
        DEN = [A(256 * 4, F32) for _ in range(2)]
        UT3 = [A(512 * 4, F32, parts=3) for _ in range(2)]
        assert o[0] <= self.SCRN * 4, o[0]
        qnv = QN.ap.rearrange("p (g t) -> p g t", g=4)
        zgv = ZGS.ap.rearrange("p (u f) -> p u f", u=4)
        xbv = XBC.ap.rearrange("p (c t) -> p c t", c=8)
        dtv = DT.ap[:, 0:32].rearrange("p (u h) -> p u h", u=4)
        dtav = self.PV[:, self.pv['tmp']:self.pv['tmp'] + 32].rearrange("p (u h) -> p u h", u=4)
        DTA = Reg(dtav, ['pv_tmp'])
        odin = W['odin']
        MBb = self.MBb

        def qknorm(pb, pk, gvec, gkey, out_ap, out_keys, slot, out2=None, out2_keys=None):
            sq, rs = NSQ[slot], NRS[slot]
            self.ACT(sq.ap[:, 0:T], pb[:, 0:T], AF.Square, [pk], sq.keys)
            pb2, pk2 = self.bank()
            self.MM(pb2[:, 0:T], MBb[:], sq.ap[:, 0:T], True, True, sq.keys + ['mbb'], [pk2])
            self.ACT(rs.ap[:, 0:T], pb2[:, 0:T], AF.Sqrt, [pk2, 'pv_eps'], rs.keys, bias=self.pvc('eps'), scale=1.0 / 64)
            self.fw.op('dve', (lambda rs=rs: lambda e: e.reciprocal(out=rs.ap[:, 0:T], in_=rs.ap[:, 0:T]))(), rs.keys, rs.keys)
            self.STT(out_ap, pb[:, 0:T], gvec, rs.ap[:, 0:T], ALU.mult, ALU.mult, [pk, gkey] + rs.keys, out_keys)
            if out2 is not None:
                self.CP('act', out2, out_ap, out_keys, out2_keys)
        wt, wk = self.ws_load([(lambda t: t[:].rearrange("p (k n) -> p k n", k=8),
                                odin[:, 0:512].rearrange("(k p) n -> p k n", p=128))], ['w_odin'])
        wv = wt[:].rearrange("p (k n) -> p k n", k=8)
        for g in range(4):
            pb, pk = self.bank()
            for kvh in range(2):
                for k in range(8):
                    c0 = kvh * 256 + g * 64
                    self.MM(pb[kvh * 64:(kvh + 1) * 64, 0:T], wv[:, k, c0:c0 + 64], hn[:, k, 0:T], k == 0, k == 7, [wk, hnk[k]], [pk])
            qknorm(pb, pk, self.pvc('qng'), 'pv_qng', qnv[:, g, 0:T], QN.keys, g % 2)
        if self.fine < 31:
            return
        wt, wk = self.ws_load([(lambda t: t[:, 0:2048].rearrange("p (k n) -> p k n", k=8),
                                odin[:, 512:768].rearrange("(k p) n -> p k n", p=128))], ['w_odin'])
        wv = wt[:, 0:2048].rearrange("p (k n) -> p k n", k=8)
        pb, pk = self.bank()
        for k in range(8):
            self.MM(pb[:, 0:T], wv[:, k, 0:128], hn[:, k, 0:T], k == 0, k == 7, [wk, hnk[k]], [pk])
        qknorm(pb, pk, self.pvc('kng'), 'pv_kng', KNF.ap[:, 0:T], KNF.keys, 0, self.KWIN[:, 128:128 + T], ['kwin'])
        for u in range(NU):
            pb, pk = self.bank()
            for k in range(8):
                self.MM(pb[0:U, 0:128], hn[:, k, u * U:(u + 1) * U], wv[:, k, 128:256], k == 0, k == 7, [wk, hnk[k]], [pk])
            self.CP('act', self.VWP[0:U, u + 1, :], pb[0:U, 0:128], [pk], ['vwp'])
            if last and u == NU - 1:
                self.CP('dve', VLF.ap[0:U, 0:128], pb[0:U, 0:128], [pk], VLF.keys)
        if self.fine < 32:
            return
        wt, wk = self.ws_load([(lambda t: t[:].rearrange("p (k n) -> p k n", k=8),
                                odin[:, 768:1280].rearrange("(k p) n -> p k n", p=128))], ['w_odin'])
        wv = wt[:].rearrange("p (k n) -> p k n", k=8)
        for u in range(NU):
            pb, pk = self.bank()
            for k in range(8):
                self.MM(pb[0:U, :], hn[:, k, u * U:(u + 1) * U], wv[:, k, :], k == 0, k == 7, [wk, hnk[k]], [pk])
            self.ACT(zgv[0:U, u, :], pb[0:U, :], AF.Silu, [pk], ZGS.keys)
        if self.fine < 33:
            return
        self.MEMSET('dve', DT.ap[:, 0:128], 0.0, DT.keys)
        for u in range(NU):
            pb, pk = self.bank()
            for k in range(8):
                if os.environ.get('KVAR') != '7':
                    self.MM(pb[0:U, 0:8], hn[:, k, u * U:(u + 1) * U], self.WDT[:, k, :], k == 0, k == 7, ['wdt', hnk[k]], [pk])
            self.TT('dve', dtv[0:U, u, :], pb[0:U, 0:8], self.BC[0:U, 0:8], ALU.add, [pk, 'bc'], DT.keys)
        kv_ = int(os.environ.get('KVAR', '0'))
        if kv_ == 8:
            return
        NW = 128 if os.environ.get('KPAD') else NU * 8
        self.ACT(DT.ap[0:U, 0:NW], DT.ap[0:U, 0:NW], AF.Exp, DT.keys, DT.keys)
        if kv_ == 9:
            return
        if kv_ in (15, 19):
            self.TS('dve', dtav[0:U, 0, :], self.BC[0:U, 8:16], -1.0, None, ALU.mult, None, DT.keys + ['bc'], DTA.keys)
            if kv_ == 15:
                return
        if os.environ.get('KLN') == '1':
            self.TS('dve', DT.ap[0:U, 0:NW], DT.ap[0:U, 0:NW], 1.0, None, ALU.add, None, DT.keys, DT.keys)
            self.ACT(DT.ap[0:U, 0:NW], DT.ap[0:U, 0:NW], AF.Ln, DT.keys, DT.keys)
        else:
            self.ACT(DT.ap[0:U, 0:NW], DT.ap[0:U, 0:NW], AF.Ln, DT.keys + ['pv_one'], DT.keys, bias=self.PV[0:U, self.pv['one']:self.pv['one'] + 1], scale=1.0)
        if kv_ in (10, 19):
            return
        if kv_ == 16:
            self.CP('act', DT.ap[0:U, 0:NU * 8], DT.ap[0:U, 0:NU * 8], DT.keys, DT.keys)
        for u in range(NU):
            if kv_ == 11:
                self.TS('dve', dtav[0:U, u, :], dtv[0:U, u, :], -1.0, None, ALU.mult, None, DT.keys + ['bc'], DTA.keys)
            elif kv_ == 13:
                self.MEMSET('dve', dtav[0:U, u, :], 0.5, DTA.keys)
            elif kv_ == 17:
                self.fw.op('dve', (lambda u=u: lambda e: e.memset(dtav[0:U, u, :], 0.5))(), DT.keys, DTA.keys)
            elif kv_ == 18:
                self.fw.op('pe', (lambda u=u: lambda e: e.matmul(self.PB[7][0:8, 0:8], lhsT=self.IDb[:, 0:8], rhs=self.IDb[:, 0:8], start=True, stop=True))(), DT.keys, ['pb7'])
            elif kv_ == 14:
                self.TS('dve', dtav[0:U, u, :], self.BC[0:U, 8:16], -1.0, None, ALU.mult, None, DT.keys + ['bc'], DTA.keys)
            elif kv_ == 12:
                self.TT('dve', dtav[0:U, u, :], dtv[0:U, u, :], self.BC[0:U, 0:8], ALU.mult, DT.keys + ['bc'], DTA.keys)
            else:
                self.TT('dve', dtav[0:U, u, :], dtv[0:U, u, :], self.BC[0:U, 8:16], ALU.mult, DT.keys + ['bc'], DTA.keys)
        if self.fine < 34:
            return
        CT = self.CT
        scw = lambda j: self.PV[:, self.pv['scw'] + j * 8: self.pv['scw'] + (j + 1) * 8]
        CC, CM = self.CCORR, self.CTMP
        self.TT('dve', CC[:, :, 0], CT[:, :, 3], scw(2), ALU.mult, ['ct', 'pv_scw'], ['ccorr'])
        self.TT('dve', CM[:, :, 0], CT[:, :, 2], scw(1), ALU.mult, ['ct', 'pv_scw'], ['ctmp'])
        self.TT('dve', CC[:, :, 0], CC[:, :, 0], CM[:, :, 0], ALU.add, ['ccorr', 'ctmp'], ['ccorr'])
        self.TT('dve', CM[:, :, 0], CT[:, :, 1], scw(0), ALU.mult, ['ct', 'pv_scw'], ['ctmp'])
        self.TT('dve', CC[:, :, 0], CC[:, :, 0], CM[:, :, 0], ALU.add, ['ccorr', 'ctmp'], ['ccorr'])
        self.TT('dve', CC[:, :, 1], CT[:, :, 3], scw(1), ALU.mult, ['ct', 'pv_scw'], ['ccorr'])
        self.TT('dve', CM[:, :, 1], CT[:, :, 2], scw(0), ALU.mult, ['ct', 'pv_scw'], ['ctmp'])
        self.TT('dve', CC[:, :, 1], CC[:, :, 1], CM[:, :, 1], ALU.add, ['ccorr', 'ctmp'], ['ccorr'])
        self.TT('dve', CC[:, :, 2], CT[:, :, 3], scw(0), ALU.mult, ['ct', 'pv_scw'], ['ccorr'])
        if kv_ == 31:
            return
        for pa in range(2):
            wt, wk = self.ws_load([(lambda t: t[:].rearrange("p (k n) -> p k n", k=8),
                                    odin[:, 1280 + pa * 512:1280 + (pa + 1) * 512].rearrange("(k p) n -> p k n", p=128))], ['w_odin'])
            wv = wt[:].rearrange("p (k n) -> p k n", k=8)
            for mc in range(4):
                c = pa * 4 + mc
                pb, pk = self.bank()
                for k in range(8):
                    self.MM(pb[:, 0:T], wv[:, k, mc * 128:(mc + 1) * 128], hn[:, k, 0:T], k == 0, k == 7, [wk, hnk[k]], [pk])
                if kv_ == 32:
                    continue
                acc = ACCX[c % 2]
                w = [self.pvc('scw', j * 8 + c, 1) for j in range(4)]
                self.ACT(acc.ap[:, 0:T], pb[:, 0:T], AF.Identity, [pk, 'pv_scw', 'pv_scb'], acc.keys, bias=self.pvc('scb', c, 1), scale=w[3])
                for sh in (1, 2, 3):
                    if kv_ == 34 and sh == 3:
                        continue
                    self.STT(acc.ap[:, sh:T], pb[:, 0:T - sh], w[3 - sh], acc.ap[:, sh:T], ALU.mult, ALU.add, [pk, 'pv_scw'] + acc.keys, acc.keys)
                if kv_ != 35:
                    self.TT('dve', acc.ap[:, 0:3], acc.ap[:, 0:3], CC[:, c, :], ALU.add, acc.keys + ['ccorr'], acc.keys)
                if kv_ != 36:
                    self.CP('dve', CT[:, c, :], pb[:, T - 4:T], [pk, 'ccorr'], ['ct'])
                self.ACT(xbv[:, c, 0:T], acc.ap[:, 0:T], AF.Silu, acc.keys, XBC.keys)
            if last and kv_ != 33:
                pb, pk = self.bank()
                for k in range(8):
                    self.MM(pb[0:3, :], hn[:, k, T - 3:T], wv[:, k, :], k == 0, k == 7, [wk, hnk[k]], [pk])
                ut = UT3[pa]
                self.CP('dve', ut.ap[0:3, 0:512], pb[0:3, 0:512], [pk], ut.keys)
                self.DMA('pool', O['o_ssdconv'][job, :, pa * 512:(pa + 1) * 512], ut.ap[0:3, 0:512], ut.keys, ['out'])
        if self.sub < 11:
            return
        mixed = self.ACT8
        MC = self.HC[0:U, HC_MC:HC_MC + U]
        MG = self.HC[0:U, HC_MG:HC_MG + U]
        self.CP('act', self.STBH[:, 0, :], self.ST[:], ['st'], [('stbh', 0)])
        lhv = LH.ap.rearrange("p (h l) -> p h l", h=8)
        decv = DEC.ap.rearrange("p (h l) -> p h l", h=8)
        mgtv = MGT.ap.rearrange("p (g l) -> p g l", g=2)
        mtv = MT.ap.rearrange("p (h l) -> p h l", h=8)
        decb = DECB.ap.rearrange("p (c h) -> p c h", c=2)
        def ssd_gen():
            for u in range(NU):
                cols = slice(u * U, (u + 1) * U)
                pb, pk = self.bank()
                pbb = pb[:].bitcast(BF16)
                for q in range(4):
                    self.TR(pbb[0:U, q * 128:(q + 1) * 128], xbv[:, q, cols], self.IDb[:], XBC.keys + ['idb'], [pk])
                for q in range(2):
                    self.TR(pbb[0:U, 512 + q * 128:512 + (q + 1) * 128], xbv[:, 4 + q, cols], self.IDb[:], XBC.keys + ['idb'], [pk])
                self.CP('dve', XT.ap[0:U, 0:512], pbb[0:U, 0:512], [pk], XT.keys)
                self.CP('act', BT.ap[0:U, 0:256], pbb[0:U, 512:768], [pk], BT.keys)
                yield
                pb, pk = self.bank()
                self.MM(pb[0:U, 0:8], MG, dtav[0:U, u, :], True, True, ['hc'] + DTA.keys, [pk])
                self.MM(pb[0:U, 8:16], MC, dtav[0:U, u, :], True, True, ['hc'] + DTA.keys, [pk])
                for cu in range(NCU):
                    self.MM(pb[:, 16 + cu * 8:24 + cu * 8], self.HC[0:U, HC_CS + cu * 128:HC_CS + (cu + 1) * 128], dtav[0:U, u, :], True, True,
                            ['hc'] + DTA.keys, [pk])
                self.ACT(EX.ap[0:U, 0:16], pb[0:U, 0:16], AF.Exp, [pk], EX.keys)
                self.ACT(DECB.ap[:, 0:NCU * 8], pb[:, 16:16 + NCU * 8], AF.Exp, [pk], DECB.keys)
                yield
                for h in range(8):
                    self.TS('pool' if h % 2 else 'dve', lhv[0:U, h, 0:U], MG, dtav[0:U, u, h:h + 1], None, ALU.mult, None, ['hc'] + DTA.keys, LH.keys)
                yield
                dbk = [self.bank(), self.bank()]
                for h in range(8):
                    pbd, pkd = dbk[h // 4]
                    self.MM(pbd[0:U, (h % 4) * U:(h % 4 + 1) * U], lhv[0:U, h, 0:U], MC, True, True, LH.keys + ['hc'], [pkd])
                for hh in range(2):
                    pbd, pkd = dbk[hh]
                    self.ACT(decv[0:U, hh * 4:(hh + 1) * 4, 0:U], pbd[0:U, 0:4 * U].rearrange("p (h l) -> p h l", h=4), AF.Exp, [pkd], DEC.keys)
                yield
                pbg, pkg = self.bank()
                for grp in range(2):
                    self.MM(pbg[0:U, grp * U:(grp + 1) * U], xbv[:, 4 + grp, cols], xbv[:, 6 + grp, cols], True, True, XBC.keys, [pkg])
                self.TT('dve', mgtv[0:U, :, 0:U], pbg[0:U, 0:2 * U].rearrange("p (g l) -> p g l", g=2), MC.unsqueeze(1).broadcast_to([U, 2, U]),
                        ALU.mult, [pkg, 'hc'], MGT.keys)
                self.TT('dve', mtv[0:U, :, 0:U].rearrange("p (g q) l -> p g q l", g=2), decv[0:U, :, 0:U].rearrange("p (g q) l -> p g q l", g=2),
                        mgtv[0:U, :, 0:U].unsqueeze(2).broadcast_to([U, 2, 4, U]), ALU.mult, DEC.keys + MGT.keys, MT.keys)
                yield
                x3 = XT.ap[0:U, 0:512].rearrange("p (h q) -> p h q", h=8)
                self.TT('dve', XDT.ap[0:U, 0:512].rearrange("p (h q) -> p h q", h=8), x3, dtv[0:U, u, :].unsqueeze(2).broadcast_to([U, 8, 64]),
                        ALU.mult, XT.keys + DT.keys, XDT.keys)
                self.TT('pool', XD.ap[0:U, 0:512].rearrange("p (h q) -> p h q", h=8), x3, self.BC[0:U, 16:24].unsqueeze(2).broadcast_to([U, 8, 64]),
                        ALU.mult, XT.keys + ['bc'], XD.keys)
                self.TT('dve', XW.ap[0:U, 0:512].rearrange("p (h q) -> p h q", h=8), XDT.ap[0:U, 0:512].rearrange("p (h q) -> p h q", h=8),
                        EX.ap[0:U, 0:8].unsqueeze(2).broadcast_to([U, 8, 64]), ALU.mult, XDT.keys + EX.keys, XW.keys)
                yield
                pby, pky = self.bank()
                self.MM(pby[0:U, :], self.IDb[0:U, 0:U], XD.ap[0:U, 0:512], True, False, ['idb'] + XD.keys, [pky])
                for h in range(8):
                    self.MM(pby[0:U, h * 64:(h + 1) * 64], mtv[0:U, h, 0:U], XDT.ap[0:U, h * 64:(h + 1) * 64], False, h == 7, MT.keys + XDT.keys, [pky])
                yield
                for cu in range(NCU):
                    c = u * NCU + cu
                    p0 = cu * 64
                    pbu, pku = self.bank()
                    for grp in range(2):
                        self.MM(pbu[:, grp * 256:(grp + 1) * 256], BT.ap[p0:p0 + 64, grp * 128:(grp + 1) * 128], XW.ap[p0:p0 + 64, grp * 256:(grp + 1) * 256],
                                True, True, BT.keys + XW.keys, [pku])
                    self.TT('dve', self.ST[:].rearrange("p (h q) -> p h q", h=8), self.ST[:].rearrange("p (h q) -> p h q", h=8),
                            decb[:, cu, :].unsqueeze(2).broadcast_to([128, 8, 64]), ALU.mult, ['st'] + DECB.keys, ['st'])
                    self.TT('dve', self.ST[:], self.ST[:], pbu[:, :], ALU.add, ['st', pku], ['st'])
                    self.CP('act', self.STBH[:, c + 1, :], self.ST[:], ['st'], [('stbh', c + 1)])
                    yield
                yield
                pbi, pki = self.bank()
                for cu in range(NCU):
                    c = u * NCU + cu
                    for grp in range(2):
                        self.MM(pbi[cu * 64:(cu + 1) * 64, grp * 256:(grp + 1) * 256], xbv[:, 6 + grp, u * U + cu * 64:u * U + (cu + 1) * 64],
                                self.STBH[:, c, grp * 256:(grp + 1) * 256], True, True, XBC.keys + [('stbh', c)], [pki])
                self.TT('dve', T1.ap[0:U, 0:512].rearrange("p (h q) -> p h q", h=8), pbi[0:U, :].rearrange("p (h q) -> p h q", h=8),
                        EX.ap[0:U, 8:16].unsqueeze(2).broadcast_to([U, 8, 64]), ALU.mult, [pki] + EX.keys, T1.keys)
                self.TT('dve', YY.ap[0:U, 0:512], pby[0:U, :], T1.ap[0:U, 0:512], ALU.add, [pky] + T1.keys, YY.keys)
                yield
                self.TT('pool', YG.ap[0:U, 0:512], YY.ap[0:U, 0:512], zgv[0:U, u, :], ALU.mult, YY.keys + ZGS.keys, YG.keys)
                self.fw.op('act', (lambda u=u: lambda e: e.activation(out=YJ.ap[0:U, 0:512], in_=YG.ap[0:U, 0:512], func=AF.Square,
                                                                      accum_out=SS.ap[0:U, 0:1]))(), YG.keys, YJ.keys + SS.keys)
                self.ACT(SS.ap[0:U, 0:1], SS.ap[0:U, 0:1], AF.Sqrt, SS.keys + ['pv_eps'], SS.keys, bias=self.PV[0:U, self.pv['eps']:self.pv['eps'] + 1], scale=1.0 / 512)
                self.fw.op('dve', lambda e: e.reciprocal(out=SS.ap[0:U, 0:1], in_=SS.ap[0:U, 0:1]), SS.keys, SS.keys)
                self.STT(YN.ap[0:U, 0:512], YG.ap[0:U, 0:512], SS.ap[0:U, 0:1], self.BC[0:U, 32:544], ALU.mult, ALU.mult, YG.keys + SS.keys + ['bc'], YN.keys)
                yield
                pbt, pkt = self.bank()
                pbtb = pbt[:].bitcast(BF16)
                for q in range(4):
                    self.TR(pbtb[:, q * U:(q + 1) * U], YN.ap[0:U, q * 128:(q + 1) * 128], self.IDb[0:U, 0:U], YN.keys + ['idb'], [pkt])
                self.CP('act', mixed[:, 4:8, cols], pbtb[:, 0:4 * U].rearrange("p (q l) -> p q l", q=4), [pkt], [('act8', 4 + q) for q in range(4)])
                yield
        gc0 = ti * NC
        def swa_gen():
            for c in range(NC):
                gc = gc0 + c if job < 2 else 2
                odd = c % 2
                BI = self.BIO if odd else self.BIE
                pts = []
                for kvh in range(2):
                    pbs, pks = self.bank()
                    rows = slice(kvh * 64, (kvh + 1) * 64)
                    rhs = qnv[rows, :, c * 64:(c + 1) * 64]
                    if not odd:
                        self.MM(pbs[0:128, 0:256], self.KWIN[rows, c * 64:c * 64 + 128], rhs, True, True, ['kwin'] + QN.keys, [pks])
                        self.MM(pbs[0:64, 256:512], self.KWIN[rows, (c + 2) * 64:(c + 3) * 64], rhs, True, True, ['kwin'] + QN.keys, [pks])
                    else:
                        self.MM(pbs[0:128, 0:256], self.KWIN[rows, (c + 1) * 64:(c + 1) * 64 + 128], rhs, True, True, ['kwin'] + QN.keys, [pks])
                        self.MM(pbs[64:128, 256:512], self.KWIN[rows, c * 64:(c + 1) * 64], rhs, True, True, ['kwin'] + QN.keys, [pks])
                    yield
                    tmp = TMPS[kvh]
                    pt = PT[(c % 2) * 2 + kvh]
                    self.TT('dve', tmp.ap[:, 0:512], pbs[:, 0:512], BI[:, kvh, :], ALU.add, [pks, 'bie', 'bio'], tmp.keys)
                    self.ACT(pt.ap[:, 0:512], tmp.ap[:, 0:512], AF.Exp, tmp.keys, pt.keys)
                    if gc == 0:
                        self.MEMSET('pool', pt.ap[:, 0:256], 0.0, pt.keys)
                    elif gc == 1:
                        self.MEMSET('pool', pt.ap[64:128, 256:512], 0.0, pt.keys)
                    pts.append(pt)
                yield
                pbo, pko = self.bank()
                pbd, pkd = self.bank()
                for kvh in range(2):
                    pt = pts[kvh]
                    vc = slice(kvh * 64, (kvh + 1) * 64)
                    if not odd:
                        pair_slot, single_slot, sp0 = c // 2, c // 2 + 1, 0
                    else:
                        pair_slot, single_slot, sp0 = (c + 1) // 2, c // 2, 64
                    for (pbx, pkx, lhs_pair, lhs_single) in [
                            (pbo, pko, self.VWP[:, pair_slot, vc], self.VWP[sp0:sp0 + 64, single_slot, vc]),
                            (pbd, pkd, self.ONESb[:, 0:64], self.ONESb[sp0:sp0 + 64, 0:64])]:
                        self.MM(pbx[kvh * 64:(kvh + 1) * 64, 0:256], lhs_pair, pt.ap[:, 0:256], True, False, ['vwp', 'onesb'] + pt.keys, [pkx])
                        self.MM(pbx[kvh * 64:(kvh + 1) * 64, 0:256], lhs_single, pt.ap[sp0:sp0 + 64, 256:512], False, True, ['vwp', 'onesb'] + pt.keys, [pkx])
                yield
                den = DEN[c % 2]
                self.TT('dve', den.ap[:, 0:256].rearrange("p (g l) -> p g l", g=4), pbd[:, 0:256].rearrange("p (g l) -> p g l", g=4),
                        self.pvc('sinkexp', 0, 4).unsqueeze(2).broadcast_to([128, 4, 64]), ALU.add, [pkd, 'pv_sink'], den.keys)
                self.fw.op('dve', (lambda den=den: lambda e: e.reciprocal(out=den.ap[:, 0:256], in_=den.ap[:, 0:256]))(), den.keys, den.keys)
                self.TT('dve', mixed[:, 0:4, c * 64:(c + 1) * 64], pbo[:, 0:256].rearrange("p (g l) -> p g l", g=4),
                        den.ap[:, 0:256].rearrange("p (g l) -> p g l", g=4), ALU.mult, [pko] + den.keys, [('act8', g) for g in range(4)])
                yield
        gens = [ssd_gen(), swa_gen()]
        if os.environ.get('KNOILV'):
            for g_ in gens:
                for _ in g_:
                    pass
        else:
            while gens:
                for g_ in list(gens):
                    try:
                        next(g_)
                    except StopIteration:
                        gens.remove(g_)
        if last:
            R = U
            if job < 2:
                self.tm_store(O['o_pk'][job], KNF.ap[:, T - 128:T], KNF.keys, 128, 128)
                self.DMA('pool', O['o_pv'][job], VLF.ap[0:128, 0:128], VLF.keys, ['out'])
            else:
                self.tm_store(O['o_sk'], KNF.ap[:, 0:64], KNF.keys, 64, 128)
                self.DMA('pool', O['o_sv'], VLF.ap[0:64, 0:128], VLF.keys, ['out'])
        else:
            self.CP('dve', self.KWIN[:, 0:128], self.KWIN[:, T:T + 128], ['kwin'], ['kwin'])
            self.CP('act', self.VWP[:, 0, :], self.VWP[:, NU, :], ['vwp'], ['vwp'])
        if self.sub < 13:
            return
        self.out_proj(job, 1, T, W['odout'], True)


_CACHE = {}


def _get_nc(SEQ, do_l1=True):
    key = (SEQ, do_l1)
    if key not in _CACHE:
        _CACHE[key] = Builder(SEQ, do_l1).build()
    return _CACHE[key]


def make_in_maps(inp, SEQ):
    f = lambda a: np.ascontiguousarray(np.asarray(a, dtype=np.float32))
    hc = host_consts()
    shared = {
        't5_bias': f(inp['t5_bias']), 'norm1_g': f(inp['norm1_g']).reshape(16, 128), 'norm2_g': f(inp['norm2_g']).reshape(16, 128),
        'w_mod': f(inp['w_mod']), 'b_mod': f(inp['b_mod']), 'ffn_w_up': f(inp['ffn_w_up']),
        'ffn_conv_w': f(inp['ffn_conv_w']).reshape(2, 132, 128), 'ffn_conv_b': f(inp['ffn_conv_b']).reshape(2, 44, 128),
        'ffn_w_down': f(inp['ffn_w_down']), 'ev_w_in': f(inp['ev_w_in'])[0], 'ev_w_out': f(inp['ev_w_out'])[0],
        's5_a_re': f(inp['s5_a_re'])[0], 's5_a_im': f(inp['s5_a_im'])[0], 's5_log_dt': f(inp['s5_log_dt'])[0],
        's5_b_re': f(inp['s5_b_re'])[0], 's5_b_im': f(inp['s5_b_im'])[0],
        's5_c_re': f(inp['s5_c_re'])[0].reshape(512, 64), 's5_c_im': f(inp['s5_c_im'])[0].reshape(512, 64),
        's5_d': f(inp['s5_d'])[0], 's5_w_glu': f(inp['s5_w_glu'])[0], 's5_b_glu': f(inp['s5_b_glu'])[0].reshape(4, 128),
        'gla_w_gate2': f(inp['gla_w_gate2'])[0], 'gla_b_gate': f(inp['gla_b_gate'])[0].reshape(2, 128),
        'gla_norm_g': f(inp['gla_norm_g'])[0], 'od_w_in': f(inp['od_w_in'])[0], 'od_w_out': f(inp['od_w_out'])[0],
        'swa_q_norm': f(inp['swa_q_norm'])[0], 'swa_k_norm': f(inp['swa_k_norm'])[0], 'swa_sink': f(inp['swa_sink'])[0],
        'ssd_conv_w': f(inp['ssd_conv_w'])[0].reshape(32, 128), 'ssd_conv_b': f(inp['ssd_conv_b'])[0].reshape(8, 128),
        'ssd_dt_bias': f(inp['ssd_dt_bias'])[0], 'ssd_a_log': f(inp['ssd_a_log'])[0], 'ssd_d': f(inp['ssd_d'])[0],
        'ssd_norm_g': f(inp['ssd_norm_g'])[0], 'hc': hc,
    }
    xp, xs = f(inp['x_prompt']), f(inp['x_sample'])
    cp, cs = f(inp['c_prompt']), f(inp['c_sample'])
    maps = []
    for c in range(8):
        m = dict(shared)
        m['xp'] = xp[2 * c:2 * c + 2]
        m['xs'] = xs[c]
        m['cvec'] = np.concatenate([cp[2 * c], cp[2 * c + 1], cs[c]]).reshape(24, 128)
        m['st_s5re'] = f(inp['state_s5_re'])[0, c]
        m['st_s5im'] = f(inp['state_s5_im'])[0, c]
        m['st_gla'] = f(inp['state_gla'])[0, c].reshape(256, 128)
        m['ck'] = f(inp['cache_swa_k'])[0, c].reshape(128, 128)
        m['cv'] = f(inp['cache_swa_v'])[0, c].reshape(128, 128)
        m['st_ssd'] = f(inp['state_ssd'])[0, c].reshape(512, 128)
        m['st_ssdconv'] = f(inp['state_ssd_conv'])[0, c].reshape(24, 128)
        m['st_ffnconv'] = f(inp['state_ffn_conv'])[:, c].reshape(2, 88, 128)
        maps.append({k: np.ascontiguousarray(v) for k, v in m.items()})
    return maps


def assemble(res, SEQ):
    B, DB = 16, 8
    y_prompt = np.zeros((B, SEQ, D), np.float32)
    y_sample = np.zeros((DB, 64, D), np.float32)
    p_s5_re = np.zeros((1, B, 32, 64), np.float32)
    p_s5_im = np.zeros((1, B, 32, 64), np.float32)
    p_gla = np.zeros((1, B, 4, 64, 128), np.float32)
    p_swa_k = np.zeros((1, B, 128, 2, 64), np.float32)
    p_swa_v = np.zeros((1, B, 128, 2, 64), np.float32)
    p_ssd = np.zeros((1, B, 8, 64, 128), np.float32)
    p_ssd_conv = np.zeros((1, B, 3, 1024), np.float32)
    p_ffn_conv = np.zeros((2, B, 2, 2 * DFF), np.float32)
    s_s5_re = np.zeros((1, DB, 32, 64), np.float32)
    s_s5_im = np.zeros((1, DB, 32, 64), np.float32)
    s_gla = np.zeros((1, DB, 4, 64, 128), np.float32)
    s_swa_k = np.zeros((1, DB, 64, 2, 64), np.float32)
    s_swa_v = np.zeros((1, DB, 64, 2, 64), np.float32)
    s_ssd = np.zeros((1, DB, 8, 64, 128), np.float32)
    s_ssd_conv = np.zeros((1, DB, 3, 1024), np.float32)
    s_ffn_conv = np.zeros((2, DB, 2, 2 * DFF), np.float32)
    for c in range(8):
        r = res[c]
        y_prompt[2 * c:2 * c + 2] = r['yp']
        y_sample[c] = r['ys']
        for jb in range(2):
            b = 2 * c + jb
            p_s5_re[0, b] = r['o_s5re'][jb]
            p_s5_im[0, b] = r['o_s5im'][jb]
            p_gla[0, b] = r['o_gla'][jb].reshape(4, 64, 128)
            p_swa_k[0, b] = r['o_pk'][jb].reshape(128, 2, 64)
            p_swa_v[0, b] = r['o_pv'][jb].reshape(128, 2, 64)
            p_ssd[0, b] = r['o_ssd'][jb].reshape(8, 64, 128)
            p_ssd_conv[0, b] = r['o_ssdconv'][jb]
            p_ffn_conv[:, b] = r['o_ffnconv'][jb]
        s_s5_re[0, c] = r['o_s5re'][2]
        s_s5_im[0, c] = r['o_s5im'][2]
        s_gla[0, c] = r['o_gla'][2].reshape(4, 64, 128)
        s_swa_k[0, c] = r['o_sk'].reshape(64, 2, 64)
        s_swa_v[0, c] = r['o_sv'].reshape(64, 2, 64)
        s_ssd[0, c] = r['o_ssd'][2].reshape(8, 64, 128)
        s_ssd_conv[0, c] = r['o_ssdconv'][2]
        s_ffn_conv[:, c] = r['o_ffnconv'][2]
    return (y_prompt, y_sample, p_s5_re, p_s5_im, p_gla, p_swa_k, p_swa_v, p_ssd, p_ssd_conv, p_ffn_conv,
            s_s5_re, s_s5_im, s_gla, s_swa_k, s_swa_v, s_ssd, s_ssd_conv, s_ffn_conv)


def kernel(**inputs):
    SEQ = int(np.asarray(inputs['x_prompt']).shape[1])
    nc = _get_nc(SEQ)
    in_maps = make_in_maps(inputs, SEQ)
    res = run_bass_kernel_spmd(nc, in_maps, core_ids=list(range(8)))
    return assemble(res.results, SEQ)
```
